# Optimizing a Trainium2 kernel written in Bass

```python
import jax
import jax.numpy as jnp
from jax import lax
import numpy as np

D_MODEL = 1024
BATCH = 8
SEQ = 4096
DEPTH = 2

GRID_W = 64
CTX_LEN = 256

NA_HEAD_DIM = 64
NA_HEADS = (D_MODEL // 2) // NA_HEAD_DIM
NA_WIDTH = NA_HEADS * NA_HEAD_DIM
NA_KH = 8
NA_KW = 16
ML_HEAD_DIM = 128
ML_HEADS = (D_MODEL // 2) // ML_HEAD_DIM
ML_WIDTH = ML_HEADS * ML_HEAD_DIM
ML_CHUNK = 128
ML_GATE_CAP = 15.0
AB_IN_WIDTH = 3 * NA_WIDTH + 4 * ML_WIDTH + 4 * ML_HEADS
RW_HEAD_DIM = 64
RW_HEADS = D_MODEL // RW_HEAD_DIM
RW_DECAY_LORA = 64
RW_AAA_LORA = 64
RW_GATE_LORA = 160
RW_GN_EPS = 64e-5
D_FF = 4 * D_MODEL
ROPE_BASE = 10000.0
NORM_EPS = 1e-6

kernel_name = 'hybrid_natten_mlstm_rwkv7_dit'


def rms_norm(x, eps=NORM_EPS):
    xf = x.astype(jnp.float32)
    return (xf * lax.rsqrt(jnp.mean(xf * xf, axis=-1, keepdims=True) + eps)).astype(x.dtype)


def modulate(x, shift, scale):
    return rms_norm(x) * (1.0 + scale) + shift


def split_heads(t, n_heads, head_dim):
    return t.reshape(*t.shape[:-1], n_heads, head_dim)


def adaln(cond, w, b):
    return jnp.split(jax.nn.silu(cond) @ w + b, 6, axis=-1)


def squared_relu_mlp(h, w1, w2):
    return jnp.square(jax.nn.relu(h @ w1)) @ w2


def axial_rope_angles(n_tokens, head_dim):
    pos = jnp.arange(n_tokens)
    rows = (pos // GRID_W).astype(jnp.float32)
    cols = (pos % GRID_W).astype(jnp.float32)
    n_freq = head_dim // 4
    inv_freq = ROPE_BASE ** (-jnp.arange(n_freq, dtype=jnp.float32) / n_freq)
    return rows[:, None] * inv_freq, cols[:, None] * inv_freq


def rope_rotate(x, ang):
    x1, x2 = jnp.split(x, 2, axis=-1)
    cos = jnp.cos(ang)[:, None, :].astype(x.dtype)
    sin = jnp.sin(ang)[:, None, :].astype(x.dtype)
    return jnp.concatenate([x1 * cos - x2 * sin, x1 * sin + x2 * cos], axis=-1)


def axial_rope(x, ang_r, ang_c):
    x_row, x_col = jnp.split(x, 2, axis=-1)
    return jnp.concatenate([rope_rotate(x_row, ang_r), rope_rotate(x_col, ang_c)], axis=-1)


def context_attention(q, k, v):
    s = jnp.einsum('bqhd,bkhd->bhqk', q, k).astype(jnp.float32) * (q.shape[-1] ** -0.5)
    p = jax.nn.softmax(s, axis=-1).astype(v.dtype)
    return jnp.einsum('bhqk,bkhd->bqhd', p, v)


def neighbourhood_attention(q, k, v, k_ctx, v_ctx, rpb):
    n_b, n_tok, n_h, d_h = q.shape
    rows = n_tok // GRID_W
    kh = min(NA_KH, rows)
    n_win = kh * NA_KW
    scale = d_h ** -0.5
    qg = q.reshape(n_b, rows, GRID_W, n_h, d_h)
    kg = k.reshape(n_b, rows, GRID_W, n_h, d_h)
    vg = v.reshape(n_b, rows, GRID_W, n_h, d_h)
    row_idx = jnp.arange(rows)
    row_start = jnp.clip(row_idx - kh // 2, 0, rows - kh)
    col_idx = jnp.arange(GRID_W)
    col_win = jnp.clip(col_idx - NA_KW // 2, 0, GRID_W - NA_KW)[:, None] + jnp.arange(NA_KW)[None, :]
    rpb_c = rpb[:, :, col_win - col_idx[:, None] + NA_KW - 1]

    def one_row(args):
        r, rs, q_row = args
        k_nb = lax.dynamic_slice_in_dim(kg, rs, kh, axis=1)[:, :, col_win]
        v_nb = lax.dynamic_slice_in_dim(vg, rs, kh, axis=1)[:, :, col_win]
        bias = rpb_c[:, rs + jnp.arange(kh) - r + NA_KH - 1]
        s_nb = jnp.einsum('bchd,bicjhd->bhcij', q_row, k_nb).astype(jnp.float32) * scale + jnp.transpose(bias, (0, 2, 1, 3))
        s_cx = jnp.einsum('bchd,bnhd->bhcn', q_row, k_ctx).astype(jnp.float32) * scale
        p = jax.nn.softmax(jnp.concatenate([s_nb.reshape(n_b, n_h, GRID_W, n_win), s_cx], axis=-1), axis=-1).astype(v.dtype)
        p_nb = p[..., :n_win].reshape(n_b, n_h, GRID_W, kh, NA_KW)
        return jnp.einsum('bhcij,bicjhd->bchd', p_nb, v_nb) + jnp.einsum('bhcn,bnhd->bchd', p[..., n_win:], v_ctx)

    out = lax.map(one_row, (row_idx, row_start, jnp.moveaxis(qg, 1, 0)))
    return jnp.moveaxis(out, 0, 1).reshape(n_b, n_tok, n_h, d_h)


def mlstm_chunked(q, k, v, log_i, log_f, state, with_out):
    n_b, n_h, n_t, _ = q.shape
    n_c = n_t // ML_CHUNK

    def chunks(t):
        return jnp.moveaxis(t.reshape(n_b, n_h, n_c, ML_CHUNK, *t.shape[3:]), 2, 0)

    tril = jnp.tril(jnp.ones((ML_CHUNK, ML_CHUNK), dtype=bool))

    def step(carry, inp):
        c_st, n_st, m_st = carry
        qc, kc, vc, ic, fc = inp
        b = jnp.cumsum(fc, axis=-1)
        log_d = jnp.where(tril, b[..., :, None] - b[..., None, :] + ic[..., None, :], -jnp.inf)
        log_inter = b + m_st[..., None]
        m_t = jnp.maximum(log_inter, jnp.max(log_d, axis=-1))
        m_new = m_t[..., -1]
        w_s = jnp.exp(b[..., -1:] - b + ic - m_new[..., None])
        decay = jnp.exp(b[..., -1] + m_st - m_new)
        c_new = decay[..., None, None] * c_st + jnp.einsum('bhs,bhsd,bhse->bhde', w_s, kc, vc)
        n_new = decay[..., None] * n_st + jnp.einsum('bhs,bhsd->bhd', w_s, kc)
        if not with_out:
            return (c_new, n_new, m_new), None
        d_mat = jnp.exp(log_d - m_t[..., None])
        w_inter = jnp.exp(log_inter - m_t)
        s = jnp.einsum('bhtd,bhsd->bhts', qc, kc) * d_mat
        num = jnp.einsum('bhts,bhse->bhte', s, vc) + w_inter[..., None] * jnp.einsum('bhtd,bhde->bhte', qc, c_st)
        den = jnp.sum(s, axis=-1) + w_inter * jnp.einsum('bhtd,bhd->bht', qc, n_st)
        h = num / jnp.maximum(jnp.abs(den), jnp.exp(-m_t))[..., None]
        return (c_new, n_new, m_new), h

    state, hs = lax.scan(step, state, tuple(chunks(t) for t in (q, k, v, log_i, log_f)))
    if not with_out:
        return None, state
    return jnp.moveaxis(hs, 0, 2).reshape(n_b, n_h, n_t, -1), state


def mlstm_bidir(q_l, k_l, v_l, g_l, q_c, k_c, v_c, g_c, need_ctx):
    def heads_first(t):
        return jnp.swapaxes(t, 1, 2).astype(jnp.float32)

    def gates(g):
        g = ML_GATE_CAP * jnp.tanh(g.astype(jnp.float32) / ML_GATE_CAP)
        i_f, f_f, i_b, f_b = jnp.split(jnp.swapaxes(g, 1, 2), 4, axis=1)
        return i_f, jax.nn.log_sigmoid(f_f), i_b, jax.nn.log_sigmoid(f_b)

    def flip(t):
        return jnp.flip(t, axis=2)

    n_b = q_l.shape[0]
    zero = (jnp.zeros((n_b, ML_HEADS, ML_HEAD_DIM, ML_HEAD_DIM), jnp.float32),
            jnp.zeros((n_b, ML_HEADS, ML_HEAD_DIM), jnp.float32),
            jnp.zeros((n_b, ML_HEADS), jnp.float32))
    qc, kc, vc = heads_first(q_c), heads_first(k_c), heads_first(v_c)
    if_c, lf_c, ib_c, lb_c = gates(g_c)
    hc_f, st_f = mlstm_chunked(qc, kc, vc, if_c, lf_c, zero, need_ctx)
    hc_b, st_b = mlstm_chunked(flip(qc), flip(kc), flip(vc), flip(ib_c), flip(lb_c), zero, need_ctx)
    ql, kl, vl = heads_first(q_l), heads_first(k_l), heads_first(v_l)
    if_l, lf_l, ib_l, lb_l = gates(g_l)
    hl_f, _ = mlstm_chunked(ql, kl, vl, if_l, lf_l, st_f, True)
    hl_b, _ = mlstm_chunked(flip(ql), flip(kl), flip(vl), flip(ib_l), flip(lb_l), st_b, True)
    h_lat = jnp.swapaxes(hl_f + flip(hl_b), 1, 2)
    h_ctx = jnp.swapaxes(hc_f + flip(hc_b), 1, 2) if need_ctx else None
    return h_lat, h_ctx


def ab_mixer(h_l, h_c, w_in, gate_b, q_norm, k_norm, rpb, head_norm, w_out, ang_r, ang_c, need_ctx):
    splits = [NA_WIDTH, 2 * NA_WIDTH, 3 * NA_WIDTH, 3 * NA_WIDTH + ML_WIDTH, 3 * NA_WIDTH + 2 * ML_WIDTH,
              3 * NA_WIDTH + 3 * ML_WIDTH, 3 * NA_WIDTH + 4 * ML_WIDTH]

    def project(h):
        qa, ka, va, qb, kb, vb, ob, g = jnp.split(h @ w_in, splits, axis=-1)
        qa = rms_norm(split_heads(qa, NA_HEADS, NA_HEAD_DIM)) * q_norm
        ka = rms_norm(split_heads(ka, NA_HEADS, NA_HEAD_DIM)) * k_norm
        va = split_heads(va, NA_HEADS, NA_HEAD_DIM)
        qb = split_heads(qb, ML_HEADS, ML_HEAD_DIM)
        kb = split_heads(kb, ML_HEADS, ML_HEAD_DIM) * (ML_HEAD_DIM ** -0.5)
        vb = split_heads(vb, ML_HEADS, ML_HEAD_DIM)
        return qa, ka, va, qb, kb, vb, ob, g + gate_b

    qa_l, ka_l, va_l, qb_l, kb_l, vb_l, ob_l, g_l = project(h_l)
    qa_c, ka_c, va_c, qb_c, kb_c, vb_c, ob_c, g_c = project(h_c)
    qb_l = axial_rope(qb_l, ang_r, ang_c)
    kb_l = axial_rope(kb_l, ang_r, ang_c)
    na_l = neighbourhood_attention(qa_l, ka_l, va_l, ka_c, va_c, rpb)
    ml_l, ml_c = mlstm_bidir(qb_l, kb_l, vb_l, g_l, qb_c, kb_c, vb_c, g_c, need_ctx)

    def merge(na, ml, o):
        ml = (rms_norm(ml) * head_norm).astype(o.dtype) * jax.nn.sigmoid(split_heads(o, ML_HEADS, ML_HEAD_DIM))
        cat = jnp.concatenate([na.reshape(*na.shape[:2], NA_WIDTH), ml.reshape(*ml.shape[:2], ML_WIDTH)], axis=-1)
        return cat @ w_out

    out_l = merge(na_l, ml_l, ob_l)
    out_c = merge(context_attention(qa_c, ka_c, va_c), ml_c, ob_c) if need_ctx else None
    return out_l, out_c


def centred_shift(x):
    xp = jnp.pad(x, ((0, 0), (1, 1), (0, 0)))
    return 0.5 * (xp[:, :-2] + xp[:, 2:])


def rwkv_prepare(h, mu, w_rkv, w0, w1, w2, a0, a1, a2, g1, g2, k_k, k_a):
    xx = centred_shift(h) - h
    xr, xw, xk, xv, xa, xg = h[None] + xx[None] * mu[:, None, None, :]
    r, k, v = jnp.einsum('sbtd,sde->sbte', jnp.stack([xr, xk, xv]), w_rkv)
    w_logit = w0[:, None, None, :] + jnp.einsum('zbtr,zrd->zbtd', jnp.tanh(jnp.einsum('btd,zdr->zbtr', xw, w1)), w2)
    decay = jnp.exp(-jnp.exp(-jax.nn.softplus(-w_logit.astype(jnp.float32)) - 0.5))
    a = jax.nn.sigmoid(a0[:, None, None, :] + jnp.einsum('zbtr,zrd->zbtd', jnp.einsum('btd,zdr->zbtr', xa, a1), a2))
    g = jax.nn.sigmoid(xg @ g1) @ g2
    kk = split_heads((k * k_k).astype(jnp.float32), RW_HEADS, RW_HEAD_DIM)
    kk = (kk / jnp.maximum(jnp.linalg.norm(kk, axis=-1, keepdims=True), 1e-12)).reshape(k.shape)
    k_dir = k[None] * (1.0 + (a - 1.0) * k_a)
    return r, k_dir, v, decay, -kk, kk[None] * a, g


def rwkv7_scan(r, w, k, v, a, b, state, reverse, with_out):
    def time_major(t):
        return jnp.moveaxis(split_heads(t.astype(jnp.float32), RW_HEADS, RW_HEAD_DIM), 1, 0)

    def step(s, inp):
        rt, wt, kt, vt, at, bt = inp
        sa = jnp.einsum('bhij,bhj->bhi', s, at)
        s = s * wt[:, :, None, :] + sa[..., None] * bt[:, :, None, :] + vt[..., None] * kt[:, :, None, :]
        y = jnp.einsum('bhij,bhj->bhi', s, rt) if with_out else None
        return s, y

    state, ys = lax.scan(step, state, tuple(time_major(t) for t in (r, w, k, v, a, b)), reverse=reverse)
    return (jnp.moveaxis(ys, 0, 1) if with_out else None), state


def rwkv_mixer(h_l, h_c, mu, w_rkv, w0, w1, w2, a0, a1, a2, g1, g2, k_k, k_a, r_k, lnx_w, lnx_b, w_o, need_ctx):
    lat = rwkv_prepare(h_l, mu, w_rkv, w0, w1, w2, a0, a1, a2, g1, g2, k_k, k_a)
    cxt = rwkv_prepare(h_c, mu, w_rkv, w0, w1, w2, a0, a1, a2, g1, g2, k_k, k_a)
    s0 = jnp.zeros((h_l.shape[0], RW_HEADS, RW_HEAD_DIM, RW_HEAD_DIM), jnp.float32)

    def bidir(p, s_f, s_b, with_out):
        r, k_dir, v, decay, a, b_dir, _ = p
        y_f, s_f = rwkv7_scan(r, decay[0], k_dir[0], v, a, b_dir[0], s_f, False, with_out)
        y_b, s_b = rwkv7_scan(r, decay[1], k_dir[1], v, a, b_dir[1], s_b, True, with_out)
        return y_f, y_b, s_f, s_b

    yc_f, yc_b, st_f, st_b = bidir(cxt, s0, s0, need_ctx)
    yl_f, yl_b, _, _ = bidir(lat, st_f, st_b, True)

    def readout(p, y_f, y_b):
        r, k_dir, v, _, _, _, g = p
        y = y_f + y_b
        mean = jnp.mean(y, axis=-1, keepdims=True)
        var = jnp.mean(jnp.square(y - mean), axis=-1, keepdims=True)
        y = ((y - mean) * lax.rsqrt(var + RW_GN_EPS)).reshape(r.shape) * lnx_w + lnx_b
        coef = jnp.sum(split_heads(r, RW_HEADS, RW_HEAD_DIM)[None] * split_heads(k_dir, RW_HEADS, RW_HEAD_DIM)
                       * split_heads(r_k, RW_HEADS, RW_HEAD_DIM), axis=(0, -1))[..., None]
        bonus = (coef * split_heads(v, RW_HEADS, RW_HEAD_DIM)).reshape(r.shape)
        return ((y + bonus) * g).astype(h_l.dtype) @ w_o

    out_l = readout(lat, yl_f, yl_b)
    out_c = readout(cxt, yc_f, yc_b) if need_ctx else None
    return out_l, out_c


def setup_inputs(seed: int = 0) -> dict:
    key = jax.random.key(seed)
    keys = iter(jax.random.split(key, 48))
    n_even = (DEPTH + 1) // 2
    n_odd = DEPTH // 2
    d = D_MODEL

    def normal(shape, scale):
        return scale * jax.random.normal(next(keys), shape, jnp.float32)

    def uniform(shape, lo, hi):
        return jax.random.uniform(next(keys), shape, jnp.float32, lo, hi)

    ab_gate_b = jnp.concatenate([normal((n_even, ML_HEADS), 0.1), uniform((n_even, ML_HEADS), 3.0, 6.0),
                                 normal((n_even, ML_HEADS), 0.1), uniform((n_even, ML_HEADS), 3.0, 6.0)], axis=-1)
    return {
        'x': normal((BATCH, SEQ, d), 1.0),
        'c': normal((BATCH, d), 1.0),
        'ctx': normal((BATCH, CTX_LEN, d), 1.0),
        'c_ctx': normal((d,), 1.0),
        'ada_w': normal((DEPTH, d, 6 * d), 0.5 * d ** -0.5),
        'ada_b': normal((DEPTH, 6 * d), 0.02),
        'ab_w_in': normal((n_even, d, AB_IN_WIDTH), d ** -0.5),
        'ab_gate_b': ab_gate_b,
        'na_q_norm': 1.0 + normal((n_even, NA_HEAD_DIM), 0.05),
        'na_k_norm': 1.0 + normal((n_even, NA_HEAD_DIM), 0.05),
        'na_rpb': normal((n_even, NA_HEADS, 2 * NA_KH - 1, 2 * NA_KW - 1), 0.1),
        'ml_head_norm': 1.0 + normal((n_even, ML_HEADS, ML_HEAD_DIM), 0.05),
        'ab_w_out': normal((n_even, d, d), d ** -0.5),
        'rw_mu': uniform((n_odd, 6, d), 0.0, 1.0),
        'rw_w_rkv': normal((n_odd, 3, d, d), d ** -0.5),
        'rw_w0': uniform((n_odd, 2, d), -6.0, -1.0),
        'rw_w1': normal((n_odd, 2, d, RW_DECAY_LORA), 0.1 * d ** -0.5),
        'rw_w2': normal((n_odd, 2, RW_DECAY_LORA, d), 0.1 * RW_DECAY_LORA ** -0.5),
        'rw_a0': normal((n_odd, 2, d), 0.1),
        'rw_a1': normal((n_odd, 2, d, RW_AAA_LORA), d ** -0.5),
        'rw_a2': normal((n_odd, 2, RW_AAA_LORA, d), 0.5 * RW_AAA_LORA ** -0.5),
        'rw_g1': normal((n_odd, d, RW_GATE_LORA), d ** -0.5),
        'rw_g2': normal((n_odd, RW_GATE_LORA, d), RW_GATE_LORA ** -0.5),
        'rw_k_k': 0.85 + normal((n_odd, d), 0.05),
        'rw_k_a': 1.0 + normal((n_odd, d), 0.05),
        'rw_r_k': normal((n_odd, d), 0.1),
        'rw_lnx_w': 1.0 + normal((n_odd, d), 0.05),
        'rw_lnx_b': normal((n_odd, d), 0.02),
        'rw_w_o': normal((n_odd, d, d), d ** -0.5),
        'mlp_w1': normal((DEPTH, d, D_FF), d ** -0.5),
        'mlp_w2': normal((DEPTH, D_FF, d), D_FF ** -0.5),
    }


def reference(x, c, ctx, c_ctx, ada_w, ada_b, ab_w_in, ab_gate_b, na_q_norm, na_k_norm, na_rpb, ml_head_norm,
              ab_w_out, rw_mu, rw_w_rkv, rw_w0, rw_w1, rw_w2, rw_a0, rw_a1, rw_a2, rw_g1, rw_g2, rw_k_k, rw_k_a,
              rw_r_k, rw_lnx_w, rw_lnx_b, rw_w_o, mlp_w1, mlp_w2):
    ang_r, ang_c = axial_rope_angles(x.shape[1], ML_HEAD_DIM)
    for layer in range(DEPTH):
        need_ctx = layer < DEPTH - 1
        j = layer // 2
        sh_a, sc_a, gt_a, sh_m, sc_m, gt_m = adaln(c[:, None, :], ada_w[layer], ada_b[layer])
        csh_a, csc_a, cgt_a, csh_m, csc_m, cgt_m = adaln(c_ctx, ada_w[layer], ada_b[layer])
        h_l = modulate(x, sh_a, sc_a)
        h_c = modulate(ctx, csh_a, csc_a)
        if layer % 2 == 0:
            o_l, o_c = ab_mixer(h_l, h_c, ab_w_in[j], ab_gate_b[j], na_q_norm[j], na_k_norm[j], na_rpb[j],
                                ml_head_norm[j], ab_w_out[j], ang_r, ang_c, need_ctx)
        else:
            o_l, o_c = rwkv_mixer(h_l, h_c, rw_mu[j], rw_w_rkv[j], rw_w0[j], rw_w1[j], rw_w2[j], rw_a0[j],
                                  rw_a1[j], rw_a2[j], rw_g1[j], rw_g2[j], rw_k_k[j], rw_k_a[j], rw_r_k[j],
                                  rw_lnx_w[j], rw_lnx_b[j], rw_w_o[j], need_ctx)
        x = x + gt_a * o_l
        x = x + gt_m * squared_relu_mlp(modulate(x, sh_m, sc_m), mlp_w1[layer], mlp_w2[layer])
        if need_ctx:
            ctx = ctx + cgt_a * o_c
            ctx = ctx + cgt_m * squared_relu_mlp(modulate(ctx, csh_m, csc_m), mlp_w1[layer], mlp_w2[layer])
    return x
```

```python
import numpy as np
import concourse.bass as bass
import concourse.mybir as mybir
from concourse.bass_utils import run_bass_kernel_spmd

F32 = mybir.dt.float32
BF16 = mybir.dt.bfloat16
U8 = mybir.dt.uint8
ALU = mybir.AluOpType
AF = mybir.ActivationFunctionType
AX = mybir.AxisListType

ENGS = ['pe', 'act', 'dve', 'pool', 'sp']
EPOCH = 12000
NDS = 12

D = 1024
T_LAT = 4096
T_CTX = 256
NT = 34
NEG = -30000.0
GLEVEL = 4
GTEST = 0
GSTEPS = NT


class Prog:
    def __init__(self, nc):
        self.nc = nc
        self.ops = {e: [] for e in ENGS}
        self.cnt = {}
        self.last_w = {}
        self.readers = {}
        self.waited = {e: {} for e in ENGS}
        self.sems = {}
        self.nops = 0

    def _ticket(self, eng, kind):
        k = (eng, kind)
        n = self.cnt.get(k, 0)
        self.cnt[k] = n + 1
        if kind == 'c':
            ep, v = divmod(n, EPOCH)
            return ((eng, kind, ep), v + 1)
        return ((eng, kind, n % NDS), 16 * (n // NDS + 1))

    def op(self, eng, fn, reads=(), writes=(), dma=False):
        kind = 'd' if dma else 'c'
        deps = {}
        pr = [r for r in reads if isinstance(r, tuple) and r[0] == 'ps']
        if pr:
            reads = [r for r in reads if not (isinstance(r, tuple) and r[0] == 'ps')]
            writes = list(writes) + [r for r in pr if r not in writes]

        def add(t, war=False):
            if t is None:
                return
            key, val = t
            if not dma and key[1] == 'c' and key[0] == eng:
                if eng == 'pe':
                    return
            if deps.get(key, 0) < val:
                deps[key] = val

        for r in reads:
            add(self.last_w.get(r))
        for w in writes:
            add(self.last_w.get(w))
            for t in self.readers.get(w, ()):
                add(t, war=True)
        if dma:
            n = self.cnt.get((eng, 'd'), 0)
            if n >= NDS:
                add(((eng, 'd', n % NDS), 16 * (n // NDS)))
        waits = []
        wd = self.waited[eng]
        for key, val in deps.items():
            if wd.get(key, 0) >= val:
                continue
            wd[key] = val
            waits.append((key, val))
        t = self._ticket(eng, kind)
        self.ops[eng].append((fn, waits, t))
        for w in writes:
            self.last_w[w] = t
            self.readers[w] = []
        for r in reads:
            self.readers.setdefault(r, []).append(t)
        self.nops += 1
        return t

    def barrier(self):
        finals = {}
        for e in ENGS:
            for (fn, waits, t) in self.ops[e]:
                if t is None:
                    continue
                key, val = t
                if finals.get(key, 0) < val:
                    finals[key] = val
        for e in ENGS:
            wd = self.waited[e]
            waits = []
            for key, val in finals.items():
                if key[0] == e and key[1] == 'c' and e == 'pe':
                    continue
                if wd.get(key, 0) >= val:
                    continue
                wd[key] = val
                waits.append((key, val))
            if waits:
                self.ops[e].append((None, waits, None))
        self.last_w = {}
        self.readers = {}

    def finish(self):
        self.barrier()
        nc = self.nc
        keys = set()
        for e in ENGS:
            for (fn, waits, t) in self.ops[e]:
                if t is not None:
                    keys.add(t[0])
                for (k, v) in waits:
                    keys.add(k)
        for k in sorted(keys):
            self.sems[k] = nc.alloc_semaphore("s_%s_%s_%d" % k)
        sems = self.sems
        ops = self.ops

        def emit(ename, e):
            for (fn, waits, t) in ops[ename]:
                for (k, v) in waits:
                    e.wait_ge(sems[k], v)
                if fn is None:
                    continue
                ins = fn(e)
                ins.then_inc(sems[t[0]], 16 if t[0][1] == 'd' else 1)

        with nc.Block() as block:
            @block.tensor
            def _(e):
                emit('pe', e)

            @block.scalar
            def _(e):
                emit('act', e)

            @block.vector
            def _(e):
                emit('dve', e)

            @block.gpsimd
            def _(e):
                emit('pool', e)

            @block.sync
            def _(e):
                emit('sp', e)


class Arena:
    def __init__(self, nc, nbytes):
        self.t = nc.alloc_sbuf_tensor("arena", [128, nbytes], U8)
        self.size = nbytes
        self.off = 0
        self.mark_ = 0

    def alloc(self, shape, dt):
        esz = 2 if dt == BF16 else 4
        n = 1
        for s in shape:
            n *= s
        nb = (n * esz + 63) // 64 * 64
        assert self.off + nb <= self.size, ("SBUF arena overflow", self.off, nb)
        self.hw = max(getattr(self, 'hw', 0), self.off + nb)
        ap = self.t[:, self.off:self.off + n * esz].bitcast(dt)
        self.off += nb
        if len(shape) == 2:
            ap = ap.rearrange("p (a b) -> p a b", b=shape[1])
        elif len(shape) == 3:
            ap = ap.rearrange("p (a b c) -> p a b c", b=shape[1], c=shape[2])
        return ap

    def mark(self):
        self.mark_ = self.off

    def reset(self):
        self.off = self.mark_


def build_program(debug=(), stop_after=None):
    nc = bass.Bass("TRN2", target_bir_lowering=False)
    P = Prog(nc)

    def din(name, shape, dt=F32):
        return nc.dram_tensor(name, list(shape), dt, kind="ExternalInput").ap()

    def scratch(name, shape, dt):
        kind = "ExternalOutput" if name in debug else "Internal"
        return nc.dram_tensor(name, list(shape), dt, kind=kind).ap()

    x_in = din("x", [T_LAT, D])
    ctx_in = din("ctx", [T_CTX, D])
    c2T = din("c2T", [128, 8, 2])
    ada_w = din("ada_w", [2, D, 6 * D])
    ada_b = din("ada_b", [2, 6 * D])
    ab_w_in = din("ab_w_in", [D, 3600])
    gate_b = din("ab_gate_b", [1, 16])
    qk_norm = din("qk_norm", [2, 512])
    ident_in = din("ident", [128, 128])
    ropeC = din("ropeC", [T_LAT, 512])
    ropeS = din("ropeS", [T_LAT, 512])
    na_bias = din("na_bias", [8, 128, 21 * 128])
    trimask = din("trimask", [4, 128, 128])
    rwmask = din("rwmask", [4, 128, 128])
    lvlmask = din("lvlmask", [7, 128, 128])
    pmask_in = din("pmask", [128, 2])
    head_norm = din("head_norm", [1, 512])
    ab_w_out = din("ab_w_out", [D, D])
    mlp_w1 = din("mlp_w1", [2, D, 4 * D])
    mlp_w2 = din("mlp_w2", [2, 4 * D, D])
    rw_muT = din("rw_muT", [128, 6, 8])
    rw_w_rkv = din("rw_w_rkv", [3, D, D])
    rw_w1c = din("rw_w1c", [D, 128])
    rw_w2c = din("rw_w2c", [128, D])
    rw_a1c = din("rw_a1c", [D, 128])
    rw_a2c = din("rw_a2c", [128, D])
    rw_g1 = din("rw_g1", [D, 160])
    rw_g2 = din("rw_g2", [160, D])
    rw_vecs = din("rw_vecs", [9, D])
    rw_w_o = din("rw_w_o", [D, D])
    out_d = nc.dram_tensor("out", [T_LAT, D], F32, kind="ExternalOutput").ap()

    modv = scratch("modv", [2, 2, 6 * D], F32)
    qa_s = scratch("qa_s", [NT * 128, 512], BF16)
    ka_s = scratch("ka_s", [NT * 128, 512], BF16)
    va_s = scratch("va_s", [NT * 128, 520], BF16)
    qb_s = scratch("qb_s", [NT * 128, 512], BF16)
    kb_s = scratch("kb_s", [NT * 128, 512], BF16)
    vb_s = scratch("vb_s", [NT * 128, 516], BF16)
    ob_s = scratch("ob_s", [NT * 128, 512], F32)
    gt_s = scratch("gt_s", [NT * 128, 16], F32)
    na_s = scratch("na_s", [NT * 128, 512], BF16)
    hf_s = scratch("hf_s", [NT * 128, 512], F32)
    hb_s = scratch("hb_s", [NT * 128, 512], F32)
    ml_s = scratch("ml_s", [NT * 128, 512], BF16)
    x1_s = scratch("x1_s", [NT * 128, D], F32)
    xs1 = scratch("xs1", [NT * 128, D], F32)
    hbuf = scratch("hbuf", [NT * 128, D], F32)
    r_s = scratch("r_s", [NT * 128, D], BF16)
    v_s = scratch("v_s", [NT * 128, D], BF16)
    a_s = scratch("a_s", [NT * 128, D], BF16)
    g_s = scratch("g_s", [NT * 128, D], BF16)
    bon_s = scratch("bon_s", [NT * 128, D], F32)
    kd_s = scratch("kd_s", [2, NT * 128, D], BF16)
    bb_s = scratch("bb_s", [2, NT * 128, D], BF16)
    lw_s = scratch("lw_s", [2, NT * 128, D], F32)
    y_s = scratch("y_s", [2, NT * 128, D], F32)
    yc_s = scratch("yc_s", [NT * 128, D], BF16)

    A = Arena(nc, 190 * 1024)
    psum = nc.alloc_psum_tensor("psum", [128, 4096], F32)

    def bank(i, n=512, off=0):
        return psum[:, i * 512 + off:i * 512 + off + n]

    def bank_bf(i):
        return psum[:, i * 512:(i + 1) * 512].bitcast(BF16)

    def O(eng, method, R, W, **kw):
        return P.op(eng, lambda e: getattr(e, method)(**kw), reads=R, writes=W)

    def DMA(eng, out, in_, R, W, slow=False):
        if slow:
            return P.op(eng, lambda e: e.dma_start(out=out, in_=in_, allow_slow_non_contiguous=True), reads=R, writes=W, dma=True)
        return P.op(eng, lambda e: e.dma_start(out=out, in_=in_), reads=R, writes=W, dma=True)

    def xrows(t):
        if t < 2:
            return ctx_in[t * 128:(t + 1) * 128, :]
        return x_in[(t - 2) * 128:(t - 1) * 128, :]

    identf = A.alloc([128], F32)
    identb = A.alloc([128], BF16)
    epsb = A.alloc([1], F32)
    DMA('sp', identf, ident_in, [], ['identf'])
    O('dve', 'tensor_copy', ['identf'], ['identb'], out=identb, in_=identf)
    O('dve', 'memset', [], ['epsb'], ap=epsb, constant=1e-6)
    oneb = A.alloc([1], F32)
    O('dve', 'memset', [], ['oneb'], ap=oneb, constant=1.0)
    A.mark()

    sT = A.alloc([8, 2], F32)
    mod = A.alloc([6 * D], F32)
    badd = A.alloc([6 * D], F32)
    wst = [A.alloc([8, 512], F32) for _ in range(2)]
    DMA('sp', sT, c2T, [], ['sT'])
    O('act', 'activation', ['sT'], ['sT'], out=sT, in_=sT, func=AF.Silu)
    it = 0
    for l in range(2):
        DMA('pool', badd[0:2, :], ada_b[l:l + 1, :].partition_broadcast(2), [], ['badd'])
        for n in range(12):
            wb_ = wst[it % 2]
            wk = ('wst', it % 2)
            DMA('sp', wb_, ada_w[l].rearrange("(k p) n -> p k n", p=128)[:, :, n * 512:(n + 1) * 512], [], [wk])
            pb = ('ps', it % 2)
            for k in range(8):
                O('pe', 'matmul', ['sT', wk], [pb], out=bank(it % 2)[0:2, :], lhsT=sT[:, k, :], rhs=wb_[:, k, :],
                  start=(k == 0), stop=(k == 7))
            O('dve', 'tensor_tensor', [pb, 'badd'], ['mod'], out=mod[0:2, n * 512:(n + 1) * 512],
              in0=bank(it % 2)[0:2, :], in1=badd[0:2, n * 512:(n + 1) * 512], op=ALU.add)
            it += 1
        for i in (1, 4):
            O('dve', 'tensor_scalar', ['mod'], ['mod'], out=mod[0:2, i * D:(i + 1) * D], in0=mod[0:2, i * D:(i + 1) * D],
              scalar1=1.0, scalar2=None, op0=ALU.add)
        DMA('pool', modv[l], mod[0:2, :], ['mod'], ['modv'])
    P.barrier()
    A.reset()

    def load_mod(dst, layer, seg, idx, key):
        DMA('pool', dst, modv[layer, seg:seg + 1, idx * D:(idx + 1) * D].partition_broadcast(128), ['modv'], [key])

    def rms_modulate(xt, xk, sc, sck, sh, shk, outb, outk, tmp, tmpk, ss, ssk, junk, junkk):
        O('act', 'activation', [xk], [junkk, ssk], out=junk, in_=xt, func=AF.Square, accum_out=ss)
        O('act', 'activation', [ssk, 'epsb'], [ssk], out=ss, in_=ss, func=AF.Ln, scale=1.0 / D, bias=epsb)
        O('act', 'activation', [ssk], [ssk], out=ss, in_=ss, func=AF.Exp, scale=-0.5)
        O('dve', 'scalar_tensor_tensor', [xk, ssk, sck], [tmpk], out=tmp, in0=xt, scalar=ss[:, 0:1], in1=sc,
          op0=ALU.mult, op1=ALU.mult)
        O('dve', 'tensor_tensor', [tmpk, shk], [outk], out=outb, in0=tmp, in1=sh, op=ALU.add)

    def transpose8(src, srck, dst, dstk, pbank, pkey, evac='act'):
        pv = bank_bf(pbank).rearrange("p (c t) -> p c t", t=128)
        for c in range(8):
            O('pe', 'transpose', [srck, 'identb'], [pkey], out=pv[:, c, :], in_=src[:, c * 128:(c + 1) * 128], identity=identb)
        if evac == 'act':
            O('act', 'copy', [pkey], [dstk], out=dst, in_=pv)
        else:
            O('dve', 'tensor_copy', [pkey], [dstk], out=dst, in_=pv)

    w_in = A.alloc([8, 3600], BF16)
    wst = [A.alloc([8, 400], F32) for _ in range(2)]
    for i in range(9):
        DMA('sp', wst[i % 2], ab_w_in.rearrange("(k p) n -> p k n", p=128)[:, :, i * 400:(i + 1) * 400], [], [('wst', i % 2)])
        O('pool', 'tensor_copy', [('wst', i % 2)], ['w_in'], out=w_in[:, :, i * 400:(i + 1) * 400], in_=wst[i % 2])
    scs = [A.alloc([D], F32) for _ in range(2)]
    shs = [A.alloc([D], F32) for _ in range(2)]
    for seg in range(2):
        load_mod(scs[seg], 0, seg, 1, ('sc', seg))
        load_mod(shs[seg], 0, seg, 0, ('sh', seg))
    qkn = A.alloc([2, 512], F32)
    DMA('pool', qkn[:, 0, :], qk_norm[0:1, :].partition_broadcast(128), [], ['qkn'])
    DMA('pool', qkn[:, 1, :], qk_norm[1:2, :].partition_broadcast(128), [], ['qkn'])
    gb = A.alloc([16], F32)
    DMA('pool', gb, gate_b.partition_broadcast(128), [], ['gb'])
    xts = [A.alloc([D], F32) for _ in range(2)]
    rC = [A.alloc([512], F32) for _ in range(2)]
    rS = [A.alloc([512], F32) for _ in range(2)]
    junk = A.alloc([D], F32)
    tmp = A.alloc([D], F32)
    hb = A.alloc([D], BF16)
    hT = A.alloc([8, 128], BF16)
    ss = A.alloc([1], F32)
    proj = A.alloc([3600], F32)
    sq = A.alloc([512], F32)
    hs = A.alloc([8], F32)
    qo = [A.alloc([512], BF16) for _ in range(2)]
    ko = [A.alloc([512], BF16) for _ in range(2)]
    qbo = [A.alloc([512], BF16) for _ in range(2)]
    kbo = [A.alloc([512], BF16) for _ in range(2)]
    vao = [A.alloc([8, 65], BF16) for _ in range(2)]
    vbo = [A.alloc([4, 129], BF16) for _ in range(2)]
    oo = [A.alloc([512], F32) for _ in range(2)]
    go = [A.alloc([16], F32) for _ in range(2)]
    rt1 = A.alloc([512], F32)
    rt2 = A.alloc([512], F32)
    for i in range(2):
        O('dve', 'memset', [], [('vao', i)], ap=vao[i], constant=1.0)
        O('dve', 'memset', [], [('vbo', i)], ap=vbo[i], constant=1.0)
    nchunks = [(n * 512, 512) for n in range(7)] + [(3584, 16)]
    for t in range(NT):
        p = t % 2
        seg = 1 if t < 2 else 0
        xt = xts[p]
        xk = ('xt', p)
        DMA('sp', xt, xrows(t), [], [xk])
        if seg == 0:
            DMA('sp', rC[p], ropeC[(t - 2) * 128:(t - 1) * 128, :], [], [('rC', p)])
            DMA('sp', rS[p], ropeS[(t - 2) * 128:(t - 1) * 128, :], [], [('rS', p)])
        rms_modulate(xt, xk, scs[seg], ('sc', seg), shs[seg], ('sh', seg), hb, 'hb', tmp, 'tmp', ss, 'ss', junk, 'junk')
        transpose8(hb, 'hb', hT, 'hT', 7, ('ps', 7))
        for ci, (c0, cn) in enumerate(nchunks):
            b = ci % 4
            pk = ('ps', b)
            for k in range(8):
                O('pe', 'matmul', ['hT', 'w_in'], [pk], out=bank(b, cn), lhsT=hT[:, k, :], rhs=w_in[:, k, c0:c0 + cn],
                  start=(k == 0), stop=(k == 7))
            if ci % 2 == 0:
                O('act', 'copy', [pk], [('proj', ci)], out=proj[:, c0:c0 + cn], in_=bank(b, cn))
            else:
                O('dve', 'tensor_copy', [pk], [('proj', ci)], out=proj[:, c0:c0 + cn], in_=bank(b, cn))
        rows = slice(t * 128, (t + 1) * 128)
        for which, (dst, dstk, dram) in enumerate(((qo[p], ('qo', p), qa_s), (ko[p], ('ko', p), ka_s))):
            src = proj[:, which * 512:(which + 1) * 512]
            sk = ('proj', which)
            O('dve', 'tensor_tensor', [sk], ['sq'], out=sq, in0=src, in1=src, op=ALU.mult)
            O('dve', 'tensor_reduce', ['sq'], ['hs'], out=hs, in_=sq.rearrange("p (h d) -> p h d", d=64), axis=AX.X, op=ALU.add)
            O('act', 'activation', ['hs', 'epsb'], ['hs'], out=hs, in_=hs, func=AF.Ln, scale=1.0 / 64, bias=epsb)
            O('act', 'activation', ['hs'], ['hs'], out=hs, in_=hs, func=AF.Exp, scale=-0.5)
            O('dve', 'tensor_tensor', [sk, 'hs'], ['sq'], out=sq.rearrange("p (h d) -> p h d", d=64),
              in0=src.rearrange("p (h d) -> p h d", d=64), in1=hs.unsqueeze(2).broadcast_to([128, 8, 64]), op=ALU.mult)
            O('dve', 'scalar_tensor_tensor', ['sq', 'qkn'], [dstk], out=dst, in0=sq, scalar=(0.125 if which == 0 else 1.0),
              in1=qkn[:, which, :], op0=ALU.mult, op1=ALU.mult)
            DMA('pool', dram[rows, :], dst, [dstk], [])
        O('act', 'copy', [('proj', 2)], [('vao', p)], out=vao[p][:, :, 0:64], in_=proj[:, 1024:1536].rearrange("p (h d) -> p h d", d=64))
        DMA('pool', va_s[rows, :], vao[p].rearrange("p h d -> p (h d)"), [('vao', p)], [])
        for which, (dst, dstk, dram, scl) in enumerate(((qbo[p], ('qbo', p), qb_s, 1.0), (kbo[p], ('kbo', p), kb_s, 128 ** -0.5))):
            c0 = 1536 + which * 512
            src = proj[:, c0:c0 + 512]
            sk = ('proj', 3 + which)
            if seg == 1:
                O('act', 'mul', [sk], [dstk], out=dst, in_=src, mul=scl)
            else:
                v5 = lambda ap: ap.rearrange("p (h b f d) -> p h b f d", h=4, b=2, f=2)
                O('dve', 'tensor_tensor', [sk, ('rC', p)], ['rt1'], out=rt1, in0=src, in1=rC[p], op=ALU.mult)
                for f in range(2):
                    O('dve', 'tensor_tensor', [sk, ('rS', p)], [('rt2', f)], out=v5(rt2)[:, :, :, f, :], in0=v5(src)[:, :, :, 1 - f, :],
                      in1=v5(rS[p])[:, :, :, f, :], op=ALU.mult)
                O('dve', 'tensor_tensor', ['rt1', ('rt2', 0), ('rt2', 1)], ['rt1'], out=rt1, in0=rt1, in1=rt2, op=ALU.add)
                O('act', 'mul', ['rt1'], [dstk], out=dst, in_=rt1, mul=scl)
            DMA('pool', dram[rows, :], dst, [dstk], [])
        O('act', 'copy', [('proj', 5)], [('vbo', p)], out=vbo[p][:, :, 0:128], in_=proj[:, 2560:3072].rearrange("p (h d) -> p h d", d=128))
        DMA('pool', vb_s[rows, :], vbo[p].rearrange("p h d -> p (h d)"), [('vbo', p)], [])
        O('act', 'activation', [('proj', 6)], [('oo', p)], out=oo[p], in_=proj[:, 3072:3584], func=AF.Sigmoid)
        DMA('pool', ob_s[rows, :], oo[p], [('oo', p)], [])
        g = go[p]
        gk = ('go', p)
        O('dve', 'tensor_tensor', [('proj', 7), 'gb'], [gk], out=g, in0=proj[:, 3584:3600], in1=gb, op=ALU.add)
        O('act', 'activation', [gk], [gk], out=g, in_=g, func=AF.Tanh, scale=1.0 / 15.0)
        O('dve', 'tensor_scalar', [gk], [gk], out=g, in0=g, scalar1=15.0, scalar2=None, op0=ALU.mult)
        gv = g.rearrange("p (a b h) -> p a b h", a=2, b=2)
        fv = gv[:, :, 1, :]
        O('act', 'activation', [gk], [gk], out=fv, in_=fv, func=AF.Exp, scale=-1.0)
        O('act', 'activation', [gk, 'oneb'], [gk], out=fv, in_=fv, func=AF.Ln, bias=oneb)
        O('dve', 'tensor_scalar', [gk], [gk], out=fv, in0=fv, scalar1=-1.0, scalar2=None, op0=ALU.mult)
        DMA('pool', gt_s[rows, :], g, [gk], [])
    P.barrier()
    A.reset()
    if stop_after == 'B':
        P.finish()
        return nc

    def load_T(src_dram, dstT, dstk, nchunk=4):
        tin = [A.alloc([nchunk * 128], BF16) for _ in range(2)]
        for t in range(NT):
            p = t % 2
            DMA('sp', tin[p], src_dram[t * 128:(t + 1) * 128, :], [], [('tin', p)])
            pv = bank_bf(6 + p).rearrange("p (c t) -> p c t", t=128)
            for c in range(nchunk):
                O('pe', 'transpose', [('tin', p), 'identb'], [('ps', 6 + p)], out=pv[:, c, :], in_=tin[p][:, c * 128:(c + 1) * 128],
                  identity=identb)
            if p == 0:
                O('act', 'copy', [('ps', 6 + p)], [dstk], out=dstT[:, :, t * 128:(t + 1) * 128], in_=pv[:, 0:nchunk, :])
            else:
                O('dve', 'tensor_copy', [('ps', 6 + p)], [dstk], out=dstT[:, :, t * 128:(t + 1) * 128], in_=pv[:, 0:nchunk, :])

    QT = A.alloc([4, NT * 128], BF16)
    KT = A.alloc([4, NT * 128], BF16)
    Vn = A.alloc([NT, 520], BF16)
    BT = A.alloc([8, 21, 128], BF16)
    bst = [A.alloc([21 * 128], F32) for _ in range(2)]
    for h in range(8):
        DMA('sp', bst[h % 2], na_bias[h], [], [('bst', h % 2)])
        O('pool', 'tensor_copy', [('bst', h % 2)], ['BT'], out=BT[:, h].rearrange("p a b -> p (a b)"), in_=bst[h % 2])
    DMA('sp', Vn, va_s.rearrange("(t p) f -> p t f", p=128), [], ['Vn'])
    load_T(qa_s, QT, 'QT')
    load_T(ka_s, KT, 'KT')
    PT = [A.alloc([7 * 128], BF16) for _ in range(2)]
    nao = [A.alloc([512], BF16) for _ in range(2)]
    rden = A.alloc([8], F32)
    order = list(range(2, NT)) + [0, 1]
    for qi, qt in enumerate(order):
        p = qi % 2
        if qt >= 2:
            lq = qt - 2
            if 2 <= lq <= 29:
                chunks = [(2 + lq + r, r + 2) for r in range(-2, 3)]
            else:
                sidx = {0: 0, 1: 1, 30: 2, 31: 3}[lq]
                base = 0 if lq < 2 else 28
                chunks = [(2 + base + j, 5 + 4 * sidx + j) for j in range(4)]
            chunks += [(0, None), (1, None)]
        else:
            chunks = [(0, None), (1, None)]
        ncx = len(chunks)
        for h in range(8):
            c, po = h // 2, (h % 2) * 64
            sb = h % 2
            sk = ('ps', 2 * sb)
            sk2 = ('ps', 2 * sb + 1)
            for ci, (kt, bidx) in enumerate(chunks):
                bnk = sb * 2 + ci // 4
                outp = bank(bnk, 128, (ci % 4) * 128)
                O('pe', 'matmul', ['QT', 'KT'], [sk, sk2], out=outp, lhsT=KT[po:po + 64, c, kt * 128:(kt + 1) * 128],
                  rhs=QT[po:po + 64, c, qt * 128:(qt + 1) * 128], start=True, stop=(bidx is None))
                if bidx is not None:
                    O('pe', 'matmul', ['BT', 'identb'], [sk, sk2], out=outp, lhsT=identb, rhs=BT[:, h, bidx, :], start=False, stop=True)
            n = ncx * 128
            O('act', 'activation', [sk, sk2], [('PT', sb)], out=PT[sb][:, 0:n], in_=psum[:, sb * 1024:sb * 1024 + n], func=AF.Exp)
            ob = 4 + p * 2 + h // 4
            for ci, (kt, _) in enumerate(chunks):
                O('pe', 'matmul', [('PT', sb), 'Vn'], [('ps', ob)], out=bank(ob, 65, (h % 4) * 65), lhsT=PT[sb][:, ci * 128:(ci + 1) * 128],
                  rhs=Vn[:, kt, h * 65:(h + 1) * 65], start=(ci == 0), stop=(ci == ncx - 1))
        for half in range(2):
            ob = 4 + p * 2 + half
            ov = bank(ob, 260).rearrange("p (h d) -> p h d", d=65)
            O('dve', 'reciprocal', [('ps', ob)], ['rden'], out=rden[:, half * 4:(half + 1) * 4], in_=ov[:, :, 64])
            O('dve', 'tensor_tensor', [('ps', ob), 'rden'], [('nao', p)],
              out=nao[p][:, half * 256:(half + 1) * 256].rearrange("p (h d) -> p h d", d=64), in0=ov[:, :, 0:64],
              in1=rden[:, half * 4:(half + 1) * 4].unsqueeze(2).broadcast_to([128, 4, 64]), op=ALU.mult)
        DMA('pool', na_s[qt * 128:(qt + 1) * 128, :], nao[p], [('nao', p)], [])
    P.barrier()
    A.reset()
    if stop_after == 'C':
        P.finish()
        return nc

    QbT = A.alloc([4, NT * 128], BF16)
    KbT = A.alloc([4, NT * 128], BF16)
    Kb = A.alloc([NT, 512], BF16)
    Vb = A.alloc([NT, 516], BF16)
    G = A.alloc([NT, 16], F32)
    tm = A.alloc([4, 128], F32)
    hn = A.alloc([512], F32)
    DMA('sp', Kb, kb_s.rearrange("(t p) f -> p t f", p=128), [], ['Kb'])
    DMA('sp', Vb, vb_s.rearrange("(t p) f -> p t f", p=128), [], ['Vb'])
    DMA('sp', G, gt_s.rearrange("(t p) f -> p t f", p=128), [], ['G'])
    DMA('sp', tm, trimask.rearrange("a p f -> p a f"), [], ['tm'])
    DMA('pool', hn, head_norm.partition_broadcast(128), [], ['hn'])
    load_T(qb_s, QbT, 'QbT')
    load_T(kb_s, KbT, 'KbT')
    Cst = [A.alloc([4, 129], F32) for _ in range(2)]
    Cbf = [A.alloc([4, 129], BF16) for _ in range(2)]
    for d_ in range(2):
        O('dve', 'memset', [], [('Cst', d_)], ap=Cst[d_], constant=0.0)
        O('dve', 'memset', [], [('Cbf', d_)], ap=Cbf[d_], constant=0.0)
    nb = A.alloc([4], F32)
    LFbc = A.alloc([4, 128], F32)
    Ebc = A.alloc([4, 128], F32)
    Dm = A.alloc([4, 128], F32)
    DT = A.alloc([4, 128], F32)
    PTm = A.alloc([4, 128], BF16)
    Qs = A.alloc([4, 128], BF16)
    Kt = A.alloc([4, 128], BF16)
    den = A.alloc([4], F32)
    hout = [A.alloc([512], F32) for _ in range(2)]
    hfl = [A.alloc([512], F32) for _ in range(2)]
    obl = [A.alloc([512], F32) for _ in range(2)]
    msq = A.alloc([512], F32)
    mss = A.alloc([4], F32)
    mlo = [A.alloc([512], BF16) for _ in range(2)]
    orders = [list(range(NT)), [1, 0] + list(range(NT - 1, 1, -1))]
    for step in range(NT):
        for d_ in range(2):
            t = orders[d_][step]
            p = step % 2
            tri = tm[:, d_, :]
            mask = tm[:, 2 + d_, :]
            last = 127 if d_ == 0 else 0
            lf = G[:, t, d_ * 8 + 4:d_ * 8 + 8]
            ii = G[:, t, d_ * 8:d_ * 8 + 4]
            tok = slice(t * 128, (t + 1) * 128)
            O('pe', 'matmul', ['G', 'tm'], [('ps', 0)], out=bank(0, 4), lhsT=tri, rhs=lf, start=True, stop=True)
            O('dve', 'tensor_tensor', ['G', ('ps', 0)], ['nb'], out=nb, in0=ii, in1=bank(0, 4), op=ALU.subtract)
            O('dve', 'tensor_copy', ['G'], ['LFbc'], out=LFbc, in_=lf.unsqueeze(2).broadcast_to([128, 4, 128]))
            for h in range(4):
                O('pe', 'matmul', ['LFbc', 'tm'], [('ps', 1)], out=bank(1, 128, h * 128), lhsT=LFbc[:, h, :], rhs=tri, start=True, stop=True)
            pY = bank(1).rearrange("p (h t) -> p h t", t=128)
            O('act', 'activation', [('ps', 1)], ['Ebc'], out=Ebc, in_=pY, func=AF.Exp)
            O('dve', 'tensor_tensor', [('ps', 1), 'tm'], ['Dm'], out=Dm, in0=pY, in1=mask.unsqueeze(1).broadcast_to([128, 4, 128]), op=ALU.add)
            for h in range(4):
                O('act', 'activation', ['Dm', 'nb'], [('DT', h)], out=DT[:, h, :], in_=Dm[:, h, :], func=AF.Exp, bias=nb[:, h:h + 1])
            for h in range(4):
                O('pe', 'matmul', ['QbT', 'KbT'], [('ps', 2)], out=bank(2, 128, h * 128), lhsT=KbT[:, h, tok], rhs=QbT[:, h, tok], start=True, stop=True)
            DTk = [('DT', h) for h in range(4)]
            O('dve', 'tensor_tensor', [('ps', 2)] + DTk, ['PTm'], out=PTm, in0=bank(2).rearrange("p (h t) -> p h t", t=128), in1=DT, op=ALU.mult)
            O('dve', 'tensor_tensor', ['QbT', 'Ebc'], ['Qs'], out=Qs, in0=QbT[:, :, tok], in1=Ebc, op=ALU.mult)
            O('dve', 'tensor_tensor', ['Kb'] + DTk, ['Kt'], out=Kt, in0=Kb[:, t, :].rearrange("p (h d) -> p h d", d=128),
              in1=DT[:, :, last:last + 1].broadcast_to([128, 4, 128]), op=ALU.mult)
            for h in range(4):
                wb_ = 3 + h // 2
                outp = bank(wb_, 129, (h % 2) * 129)
                O('pe', 'matmul', ['PTm', 'Vb'], [('ps', wb_)], out=outp, lhsT=PTm[:, h, :], rhs=Vb[:, t, h * 129:(h + 1) * 129], start=True, stop=False)
                O('pe', 'matmul', ['Qs', ('Cbf', d_)], [('ps', wb_)], out=outp, lhsT=Qs[:, h, :], rhs=Cbf[d_][:, h, :], start=False, stop=True)
            ho = hout[p]
            hk = ('hout', p)
            for half in range(2):
                wv = bank(3 + half, 258).rearrange("p (h d) -> p h d", d=129)
                O('act', 'activation', [('ps', 3 + half)], ['den'], out=den[:, half * 2:half * 2 + 2], in_=wv[:, :, 128], func=AF.Abs)
                O('dve', 'tensor_scalar', ['den'], ['den'], out=den[:, half * 2:half * 2 + 2], in0=den[:, half * 2:half * 2 + 2], scalar1=1.0, scalar2=None,
                  op0=ALU.max)
                O('dve', 'reciprocal', ['den'], ['den'], out=den[:, half * 2:half * 2 + 2], in_=den[:, half * 2:half * 2 + 2])
                O('dve', 'tensor_tensor', [('ps', 3 + half), 'den'], [hk], out=ho[:, half * 256:(half + 1) * 256].rearrange("p (h d) -> p h d", d=128),
                  in0=wv[:, :, 0:128], in1=den[:, half * 2:half * 2 + 2].unsqueeze(2).broadcast_to([128, 2, 128]), op=ALU.mult)
            for h in range(4):
                vb_ = 5 + h // 2
                O('pe', 'matmul', ['Kt', 'Vb'], [('ps', vb_)], out=bank(vb_, 129, (h % 2) * 129), lhsT=Kt[:, h, :], rhs=Vb[:, t, h * 129:(h + 1) * 129],
                  start=True, stop=True)
            for h in range(4):
                vb_ = 5 + h // 2
                O('dve', 'scalar_tensor_tensor', [('Cst', d_), 'Ebc', ('ps', vb_)], [('Cst', d_)], out=Cst[d_][:, h, :], in0=Cst[d_][:, h, :],
                  scalar=Ebc[:, h, last:last + 1], in1=bank(vb_, 129, (h % 2) * 129), op0=ALU.mult, op1=ALU.add)
            O('act', 'copy', [('Cst', d_)], [('Cbf', d_)], out=Cbf[d_], in_=Cst[d_])
            rows = slice(t * 128, (t + 1) * 128)
            DMA('pool', (hf_s if d_ == 0 else hb_s)[rows, :], ho, [hk], [('hfb_s', d_, t)])
    for t in range(NT):
        p = t % 2
        rows = slice(t * 128, (t + 1) * 128)
        ho = hout[p]
        hk = ('hout', p)
        DMA('sp', ho, hb_s[rows, :], [('hfb_s', 1, t)], [hk])
        DMA('sp', hfl[p], hf_s[rows, :], [('hfb_s', 0, t)], [('hfl', p)])
        DMA('sp', obl[p], ob_s[rows, :], [], [('obl', p)])
        O('dve', 'tensor_tensor', [hk, ('hfl', p)], [hk], out=ho, in0=ho, in1=hfl[p], op=ALU.add)
        for h in range(4):
            O('act', 'activation', [hk], ['msq', 'mss'], out=msq[:, h * 128:(h + 1) * 128], in_=ho[:, h * 128:(h + 1) * 128], func=AF.Square,
              accum_out=mss[:, h:h + 1])
        O('act', 'activation', ['mss', 'epsb'], ['mss'], out=mss, in_=mss, func=AF.Ln, scale=1.0 / 128, bias=epsb)
        O('act', 'activation', ['mss'], ['mss'], out=mss, in_=mss, func=AF.Exp, scale=-0.5)
        O('dve', 'tensor_tensor', [hk, 'mss'], ['msq'], out=msq.rearrange("p (h d) -> p h d", d=128), in0=ho.rearrange("p (h d) -> p h d", d=128),
          in1=mss.unsqueeze(2).broadcast_to([128, 4, 128]), op=ALU.mult)
        O('dve', 'tensor_tensor', ['msq', 'hn'], ['msq'], out=msq, in0=msq, in1=hn, op=ALU.mult)
        O('dve', 'tensor_tensor', ['msq', ('obl', p)], [('mlo', p)], out=mlo[p], in0=msq, in1=obl[p], op=ALU.mult)
        DMA('pool', ml_s[rows, :], mlo[p], [('mlo', p)], [])
    P.barrier()
    A.reset()
    if stop_after == 'D':
        P.finish()
        return nc

    def outproj_mlp(layer, wo_dram, cat_srcs, tiles, xsrc, dst):
        wo = A.alloc([8, D], BF16)
        wst = [A.alloc([8, 512], F32) for _ in range(2)]
        for n in range(2):
            DMA('sp', wst[n], wo_dram.rearrange("(k p) n -> p k n", p=128)[:, :, n * 512:(n + 1) * 512], [], [('wst', n)])
            O('pool', 'tensor_copy', [('wst', n)], ['wo'], out=wo[:, :, n * 512:(n + 1) * 512], in_=wst[n])
        gta = A.alloc([D], F32)
        cat = [A.alloc([D], BF16) for _ in range(2)]
        catT = A.alloc([8, 128], BF16)
        xts = [A.alloc([D], F32) for _ in range(2)]
        x1o = [A.alloc([D], F32) for _ in range(2)]
        tmp = A.alloc([D], F32)
        cur_seg = None
        for ti, t in enumerate(tiles):
            p = ti % 2
            seg = 1 if t < 2 else 0
            if seg != cur_seg:
                load_mod(gta, layer, seg, 2, 'gta')
                cur_seg = seg
            rows = slice(t * 128, (t + 1) * 128)
            c0 = 0
            for (src, wd) in cat_srcs:
                DMA('sp', cat[p][:, c0:c0 + wd], src[rows, :], [], [('cat', p)])
                c0 += wd
            DMA('sp', xts[p], xsrc(t), [], [('xt', p)])
            transpose8(cat[p], ('cat', p), catT, 'catT', 7, ('ps', 7))
            for n in range(2):
                for k in range(8):
                    O('pe', 'matmul', ['catT', 'wo'], [('ps', 2 * p + n)], out=bank(2 * p + n), lhsT=catT[:, k, :], rhs=wo[:, k, n * 512:(n + 1) * 512],
                      start=(k == 0), stop=(k == 7))
            O('dve', 'tensor_tensor', [('ps', 2 * p), ('ps', 2 * p + 1), 'gta'], ['tmp'], out=tmp, in0=psum[:, p * 1024:(p + 1) * 1024], in1=gta, op=ALU.mult)
            O('dve', 'tensor_tensor', ['tmp', ('xt', p)], [('x1o', p)], out=x1o[p], in0=tmp, in1=xts[p], op=ALU.add)
            DMA('pool', x1_s[rows, :], x1o[p], [('x1o', p)], [])
        P.barrier()
        A.reset()
        w1 = A.alloc([8, 4 * D], BF16)
        w2 = A.alloc([32, D], BF16)
        off0 = A.off
        wst = [A.alloc([8, 512], F32) for _ in range(2)]
        i = 0
        for n in range(8):
            DMA('sp', wst[i % 2], mlp_w1[layer].rearrange("(k p) n -> p k n", p=128)[:, :, n * 512:(n + 1) * 512], [], [('wst', i % 2)])
            O('pool', 'tensor_copy', [('wst', i % 2)], ['w1'], out=w1[:, :, n * 512:(n + 1) * 512], in_=wst[i % 2])
            i += 1
        for n in range(8):
            wv_ = wst[i % 2].rearrange("p a b -> p (a b)").rearrange("p (x c) -> p x c", c=1024)
            DMA('sp', wv_, mlp_w2[layer].rearrange("(k p) n -> p k n", p=128)[:, n * 4:(n + 1) * 4, :], [], [('wst', i % 2)])
            O('pool', 'tensor_copy', [('wst', i % 2)], ['w2'], out=w2[:, n * 4:(n + 1) * 4, :], in_=wv_)
            i += 1
        P.barrier()
        A.off = off0
        mods = [A.alloc([D], F32) for _ in range(3)]
        xts = [A.alloc([D], F32) for _ in range(2)]
        tmp = A.alloc([D], F32)
        hb = A.alloc([D], BF16)
        xmT = A.alloc([8, 128], BF16)
        r1 = [A.alloc([128], F32) for _ in range(2)]
        h1T = A.alloc([32, 128], BF16)
        ss = A.alloc([1], F32)
        cur_seg = None
        for ti, t in enumerate(tiles):
            p = ti % 2
            seg = 1 if t < 2 else 0
            if seg != cur_seg:
                for mi, idx in enumerate((3, 4, 5)):
                    load_mod(mods[mi], layer, seg, idx, ('mod', mi))
                cur_seg = seg
            rows = slice(t * 128, (t + 1) * 128)
            x1t = xts[p]
            x1k = ('xt', p)
            DMA('sp', x1t, x1_s[rows, :], [], [x1k])
            rms_modulate(x1t, x1k, mods[1], ('mod', 1), mods[0], ('mod', 0), hb, 'hb', tmp, 'tmp', ss, 'ss', tmp, 'tmp')
            transpose8(hb, 'hb', xmT, 'xmT', 7, ('ps', 7), evac='dve')
            for f in range(32):
                b = 2 + f % 4
                for k in range(8):
                    O('pe', 'matmul', ['xmT', 'w1'], [('ps', b)], out=bank(b, 128), lhsT=w1[:, k, f * 128:(f + 1) * 128], rhs=xmT[:, k, :],
                      start=(k == 0), stop=(k == 7))
                O('act', 'activation', [('ps', b)], [('r1', f % 2)], out=r1[f % 2], in_=bank(b, 128), func=AF.Relu)
                O('dve', 'tensor_tensor', [('r1', f % 2)], [('h1T', f)], out=h1T[:, f, :], in0=r1[f % 2], in1=r1[f % 2], op=ALU.mult)
            h1k = [('h1T', f) for f in range(32)]
            for n in range(2):
                for k in range(32):
                    O('pe', 'matmul', h1k + ['w2'], [('ps', n)], out=bank(n), lhsT=h1T[:, k, :], rhs=w2[:, k, n * 512:(n + 1) * 512],
                      start=(k == 0), stop=(k == 31))
            O('dve', 'tensor_tensor', [('ps', 0), ('ps', 1), ('mod', 2)], ['tmp'], out=tmp, in0=psum[:, 0:1024], in1=mods[2], op=ALU.mult)
            O('dve', 'tensor_tensor', ['tmp', x1k], [x1k], out=x1t, in0=tmp, in1=x1t, op=ALU.add)
            DMA('pool', dst(t), x1t, [x1k], [])
        P.barrier()
        A.reset()

    outproj_mlp(0, ab_w_out, [(na_s, 512), (ml_s, 512)], list(range(NT)), xrows, lambda t: xs1[t * 128:(t + 1) * 128, :])
    if stop_after == 'E':
        P.finish()
        return nc

    scs = [A.alloc([D], F32) for _ in range(2)]
    shs = [A.alloc([D], F32) for _ in range(2)]
    for seg in range(2):
        load_mod(scs[seg], 1, seg, 1, ('sc', seg))
        load_mod(shs[seg], 1, seg, 0, ('sh', seg))
    xts = [A.alloc([D], F32) for _ in range(2)]
    hos = [A.alloc([D], F32) for _ in range(2)]
    tmp = A.alloc([D], F32)
    ss = A.alloc([1], F32)
    for t in range(NT):
        p = t % 2
        seg = 1 if t < 2 else 0
        rows = slice(t * 128, (t + 1) * 128)
        DMA('sp', xts[p], xs1[rows, :], [], [('xt', p)])
        rms_modulate(xts[p], ('xt', p), scs[seg], ('sc', seg), shs[seg], ('sh', seg), hos[p], ('ho', p), tmp, 'tmp', ss, 'ss', tmp, 'tmp')
        DMA('pool', hbuf[rows, :], hos[p], [('ho', p)], [])
    P.barrier()
    A.reset()

    wrkv = A.alloc([3, 8, D], BF16)
    w1c = A.alloc([8, 128], BF16)
    a1c = A.alloc([8, 128], BF16)
    g1 = A.alloc([8, 160], BF16)
    w2c = A.alloc([D], BF16)
    a2c = A.alloc([D], BF16)
    g2 = A.alloc([2, D], BF16)
    vecs = [A.alloc([D], F32) for _ in range(7)]
    muT = A.alloc([6, 8], F32)
    omka = A.alloc([D], F32)
    off0 = A.off
    wst = [A.alloc([8, 512], F32) for _ in range(2)]
    i = 0
    for j in range(3):
        for n in range(2):
            DMA('sp', wst[i % 2], rw_w_rkv[j].rearrange("(k p) n -> p k n", p=128)[:, :, n * 512:(n + 1) * 512], [], [('wst', i % 2)])
            O('pool', 'tensor_copy', [('wst', i % 2)], ['wrkv'], out=wrkv[:, j, :, n * 512:(n + 1) * 512], in_=wst[i % 2])
            i += 1
    for (src, dst_, wd, key) in ((rw_w1c, w1c, 128, 'w1c'), (rw_a1c, a1c, 128, 'a1c'), (rw_g1, g1, 160, 'g1')):
        DMA('sp', wst[i % 2][:, :, 0:wd], src.rearrange("(k p) n -> p k n", p=128), [], [('wst', i % 2)])
        O('pool', 'tensor_copy', [('wst', i % 2)], [key], out=dst_, in_=wst[i % 2][:, :, 0:wd])
        i += 1
    for (src, dst_, key) in ((rw_w2c, w2c, 'w2c'), (rw_a2c, a2c, 'a2c')):
        wv_ = wst[i % 2].rearrange("p a b -> p (a b)")[:, 0:D]
        DMA('sp', wv_, src, [], [('wst', i % 2)])
        O('pool', 'tensor_copy', [('wst', i % 2)], [key], out=dst_, in_=wv_)
        i += 1
    wv_ = wst[i % 2].rearrange("p a b -> p (a b)")[:, 0:2 * D].rearrange("p (a b) -> p a b", b=D)
    DMA('sp', wv_[:, 0, :], rw_g2[0:128, :], [], [('wst', i % 2)])
    DMA('sp', wv_[0:32, 1, :], rw_g2[128:160, :], [], [('wst', i % 2)])
    O('pool', 'tensor_copy', [('wst', i % 2)], ['g2'], out=g2[:, 0, :], in_=wv_[:, 0, :])
    O('pool', 'tensor_copy', [('wst', i % 2)], ['g2'], out=g2[0:32, 1, :], in_=wv_[0:32, 1, :])
    for j in range(7):
        DMA('pool', vecs[j], rw_vecs[j:j + 1, :].partition_broadcast(128), [], [('vec', j)])
    DMA('sp', muT, rw_muT, [], ['muT'])
    O('dve', 'tensor_scalar', [('vec', 5)], ['omka'], out=omka, in0=vecs[5], scalar1=-1.0, scalar2=1.0, op0=ALU.mult, op1=ALU.add)
    P.barrier()
    A.off = off0
    w0b, a0b, kkv, kav, rkv = vecs[0:2], vecs[2:4], vecs[4], vecs[5], vecs[6]
    hc = A.alloc([D], F32)
    hp_ = A.alloc([D], F32)
    hn_ = A.alloc([D], F32)
    hcT = A.alloc([8, 128], F32)
    xsT = [A.alloc([8, 128], BF16) for _ in range(6)]
    thT = A.alloc([128], BF16)
    alT = A.alloc([128], BF16)
    sgT = A.alloc([2, 128], BF16)
    rf = A.alloc([D], F32)
    kf = A.alloc([D], F32)
    vf = A.alloc([D], F32)
    asig = [A.alloc([D], F32) for _ in range(2)]
    kkt = A.alloc([D], F32)
    t1 = A.alloc([D], F32)
    t2 = A.alloc([D], F32)
    hs16 = A.alloc([16], F32)
    ob16 = [A.alloc([D], BF16) for _ in range(4)]
    of32 = [A.alloc([D], F32) for _ in range(2)]
    nb16 = [0]
    nf32 = [0]

    def out16():
        nb16[0] += 1
        j = nb16[0] % 4
        return ob16[j], ('ob16', j)

    def outf32():
        nf32[0] += 1
        j = nf32[0] % 2
        return of32[j], ('of32', j)

    NEH = -float(np.exp(-0.5))
    for t in range(NT):
        r0 = t * 128
        rows = slice(r0, r0 + 128)
        first = t in (0, 2)
        lastt = t in (1, NT - 1)
        DMA('sp', hc, hbuf[rows, :], [], ['hc'])
        if first:
            O('pool', 'memset', [], ['hp'], ap=hp_, constant=0.0)
            DMA('sp', hp_[1:128, :], hbuf[r0:r0 + 127, :], [], ['hp'])
        else:
            DMA('sp', hp_, hbuf[r0 - 1:r0 + 127, :], [], ['hp'])
        if lastt:
            O('pool', 'memset', [], ['hn'], ap=hn_, constant=0.0)
            DMA('sp', hn_[0:127, :], hbuf[r0 + 1:r0 + 128, :], [], ['hn'])
        else:
            DMA('sp', hn_, hbuf[r0 + 1:r0 + 129, :], [], ['hn'])
        O('dve', 'tensor_tensor', ['hp', 'hn'], ['hp'], out=hp_, in0=hp_, in1=hn_, op=ALU.add)
        O('dve', 'scalar_tensor_tensor', ['hp', 'hc'], ['hp'], out=hp_, in0=hp_, scalar=0.5, in1=hc, op0=ALU.mult, op1=ALU.subtract)
        pvh = psum[:, 0:1024].rearrange("p (c t) -> p c t", t=128)
        pvx = psum[:, 1024:2048].rearrange("p (c t) -> p c t", t=128)
        for c in range(8):
            O('pe', 'transpose', ['hc', 'identf'], [('ps', c // 4)], out=pvh[:, c, :], in_=hc[:, c * 128:(c + 1) * 128], identity=identf)
        for c in range(8):
            O('pe', 'transpose', ['hp', 'identf'], [('ps', 2 + c // 4)], out=pvx[:, c, :], in_=hp_[:, c * 128:(c + 1) * 128], identity=identf)
        O('act', 'copy', [('ps', 0), ('ps', 1)], ['hcT'], out=hcT, in_=pvh)
        for sidx in range(6):
            for c in range(8):
                O('dve' if (sidx * 8 + c) % 3 else 'pool' if False else 'dve', 'scalar_tensor_tensor', [('ps', 2), ('ps', 3), 'hcT', 'muT'], [('xsT', sidx)],
                  out=xsT[sidx][:, c, :], in0=pvx[:, c, :], scalar=muT[:, sidx, c:c + 1], in1=hcT[:, c, :], op0=ALU.mult, op1=ALU.add)
        xr, xw, xk, xv, xa, xg = xsT
        xrk, xwk, xkk, xvk, xak, xgk = [('xsT', j) for j in range(6)]
        for j, (xs_, xsk, dstf, dk) in enumerate(((xr, xrk, rf, 'rf'), (xk, xkk, kf, 'kf'), (xv, xvk, vf, 'vf'))):
            for n in range(2):
                b = 4 + n
                for k in range(8):
                    O('pe', 'matmul', [xsk, 'wrkv'], [('ps', b)], out=bank(b), lhsT=xs_[:, k, :], rhs=wrkv[:, j, k, n * 512:(n + 1) * 512],
                      start=(k == 0), stop=(k == 7))
            O('act', 'copy', [('ps', 4), ('ps', 5)], [dk], out=dstf, in_=psum[:, 2048:3072])
        ro, rok = out16()
        O('pool', 'tensor_copy', ['rf'], [rok], out=ro, in_=rf)
        DMA('pool', r_s[rows, :], ro, [rok], [])
        vo, vok = out16()
        O('pool', 'tensor_copy', ['vf'], [vok], out=vo, in_=vf)
        DMA('pool', v_s[rows, :], vo, [vok], [])
        for k in range(8):
            O('pe', 'matmul', [xwk, 'w1c'], [('ps', 6)], out=bank(6, 128), lhsT=w1c[:, k, :], rhs=xw[:, k, :], start=(k == 0), stop=(k == 7))
        O('act', 'activation', [('ps', 6)], ['thT'], out=thT, in_=bank(6, 128), func=AF.Tanh)
        for k in range(8):
            O('pe', 'matmul', [xak, 'a1c'], [('ps', 6)], out=bank(6, 128, 128), lhsT=a1c[:, k, :], rhs=xa[:, k, :], start=(k == 0), stop=(k == 7))
        O('act', 'copy', [('ps', 6)], ['alT'], out=alT, in_=bank(6, 128, 128))
        for z in range(2):
            for n in range(2):
                O('pe', 'matmul', ['thT', 'w2c'], [('ps', 4 + n)], out=bank(4 + n), lhsT=thT[z * 64:(z + 1) * 64, :], rhs=w2c[z * 64:(z + 1) * 64, n * 512:(n + 1) * 512],
                  start=True, stop=True)
            O('dve', 'tensor_tensor', [('ps', 4), ('ps', 5), ('vec', z)], ['t1'], out=t1, in0=psum[:, 2048:3072], in1=w0b[z], op=ALU.add)
            O('act', 'activation', ['t1'], ['t1'], out=t1, in_=t1, func=AF.Sigmoid)
            lo, lok = outf32()
            O('pool', 'tensor_scalar', ['t1'], [lok], out=lo, in0=t1, scalar1=NEH, scalar2=None, op0=ALU.mult)
            DMA('pool', lw_s[z, rows, :], lo, [lok], [])
        for z in range(2):
            for n in range(2):
                O('pe', 'matmul', ['alT', 'a2c'], [('ps', 4 + n)], out=bank(4 + n), lhsT=alT[z * 64:(z + 1) * 64, :], rhs=a2c[z * 64:(z + 1) * 64, n * 512:(n + 1) * 512],
                  start=True, stop=True)
            O('dve', 'tensor_tensor', [('ps', 4), ('ps', 5), ('vec', 2 + z)], [('asig', z)], out=asig[z], in0=psum[:, 2048:3072], in1=a0b[z], op=ALU.add)
            O('act', 'activation', [('asig', z)], [('asig', z)], out=asig[z], in_=asig[z], func=AF.Sigmoid)
        if t >= 2:
            for k in range(8):
                O('pe', 'matmul', [xgk, 'g1'], [('ps', 7)], out=bank(7, 128), lhsT=g1[:, k, 0:128], rhs=xg[:, k, :], start=(k == 0), stop=(k == 7))
            for k in range(8):
                O('pe', 'matmul', [xgk, 'g1'], [('ps', 7)], out=bank(7, 128, 128)[0:32, :], lhsT=g1[:, k, 128:160], rhs=xg[:, k, :], start=(k == 0), stop=(k == 7))
            O('act', 'activation', [('ps', 7)], ['sgT'], out=sgT[:, 0, :], in_=bank(7, 128), func=AF.Sigmoid)
            O('act', 'activation', [('ps', 7)], ['sgT'], out=sgT[0:32, 1, :], in_=bank(7, 128, 128)[0:32, :], func=AF.Sigmoid)
            for n in range(2):
                O('pe', 'matmul', ['sgT', 'g2'], [('ps', 4 + n)], out=bank(4 + n), lhsT=sgT[:, 0, :], rhs=g2[:, 0, n * 512:(n + 1) * 512], start=True, stop=False)
                O('pe', 'matmul', ['sgT', 'g2'], [('ps', 4 + n)], out=bank(4 + n), lhsT=sgT[0:32, 1, :], rhs=g2[0:32, 1, n * 512:(n + 1) * 512], start=False, stop=True)
            go_, gok = out16()
            O('act', 'copy', [('ps', 4), ('ps', 5)], [gok], out=go_, in_=psum[:, 2048:3072])
            DMA('pool', g_s[rows, :], go_, [gok], [])
        h3 = lambda ap: ap.rearrange("p (h d) -> p h d", d=64)
        O('dve', 'tensor_tensor', ['kf', ('vec', 4)], ['kkt'], out=kkt, in0=kf, in1=kkv, op=ALU.mult)
        O('dve', 'tensor_tensor', ['kkt'], ['t1'], out=t1, in0=kkt, in1=kkt, op=ALU.mult)
        O('dve', 'tensor_reduce', ['t1'], ['hs16'], out=hs16, in_=h3(t1), axis=AX.X, op=ALU.add)
        O('dve', 'tensor_scalar', ['hs16'], ['hs16'], out=hs16, in0=hs16, scalar1=1e-24, scalar2=None, op0=ALU.max)
        O('act', 'activation', ['hs16'], ['hs16'], out=hs16, in_=hs16, func=AF.Ln)
        O('act', 'activation', ['hs16'], ['hs16'], out=hs16, in_=hs16, func=AF.Exp, scale=-0.5)
        O('dve', 'tensor_tensor', ['kkt', 'hs16'], ['kkt'], out=h3(kkt), in0=h3(kkt), in1=hs16.unsqueeze(2).broadcast_to([128, 16, 64]), op=ALU.mult)
        ao, aok = out16()
        O('pool', 'tensor_scalar', ['kkt'], [aok], out=ao, in0=kkt, scalar1=-1.0, scalar2=None, op0=ALU.mult)
        DMA('pool', a_s[rows, :], ao, [aok], [])
        for z in range(2):
            bo, bok = out16()
            O('dve', 'tensor_tensor', ['kkt', ('asig', z)], [bok], out=bo, in0=kkt, in1=asig[z], op=ALU.mult)
            DMA('pool', bb_s[z, rows, :], bo, [bok], [])
        for z in range(2):
            O('dve', 'tensor_tensor', [('asig', z), ('vec', 5)], [('asig', z)], out=asig[z], in0=asig[z], in1=kav, op=ALU.mult)
            O('dve', 'tensor_tensor', [('asig', z), 'omka'], [('asig', z)], out=asig[z], in0=asig[z], in1=omka, op=ALU.add)
            O('dve', 'tensor_tensor', [('asig', z), 'kf'], [('asig', z)], out=asig[z], in0=asig[z], in1=kf, op=ALU.mult)
            ko_, kok = out16()
            O('pool', 'tensor_copy', [('asig', z)], [kok], out=ko_, in_=asig[z])
            DMA('pool', kd_s[z, rows, :], ko_, [kok], [])
        if t >= 2:
            O('dve', 'tensor_tensor', [('asig', 0), ('asig', 1)], ['t2'], out=t2, in0=asig[0], in1=asig[1], op=ALU.add)
            O('dve', 'tensor_tensor', ['rf', ('vec', 6)], ['t1'], out=t1, in0=rf, in1=rkv, op=ALU.mult)
            O('dve', 'tensor_tensor', ['t1', 't2'], ['t1'], out=t1, in0=t1, in1=t2, op=ALU.mult)
            O('dve', 'tensor_reduce', ['t1'], ['hs16'], out=hs16, in_=h3(t1), axis=AX.X, op=ALU.add)
            bo_, bok_ = outf32()
            O('dve', 'tensor_tensor', ['vf', 'hs16'], [bok_], out=h3(bo_), in0=h3(vf), in1=hs16.unsqueeze(2).broadcast_to([128, 16, 64]), op=ALU.mult)
            DMA('pool', bon_s[rows, :], bo_, [bok_], [])
    P.barrier()
    A.reset()
    if stop_after == 'F':
        P.finish()
        return nc

    rm = A.alloc([4, 128], F32)
    DMA('sp', rm, rwmask.rearrange("a p f -> p a f"), [], ['rm'])
    onesf = A.alloc([128], F32)
    O('dve', 'memset', [], ['onesf'], ap=onesf, constant=1.0)
    pm = A.alloc([2], F32)
    DMA('sp', pm, pmask_in, [], ['pm'])
    tP = {n_: [A.alloc([8, 128], BF16) for _ in range(2)] for n_ in ('at', 'rt', 'bt')}
    lw = A.alloc([D], F32)
    ain = A.alloc([D], BF16)
    rin = A.alloc([D], BF16)
    bin_ = A.alloc([D], BF16)
    kin = A.alloc([D], BF16)
    vin = A.alloc([D], BF16)
    cumS = A.alloc([D], F32)
    E1 = A.alloc([D], F32)
    E2 = A.alloc([D], F32)
    E3 = A.alloc([D], F32)
    Ee = A.alloc([D], F32)
    gL = A.alloc([8], F32)
    sc6 = {n_: A.alloc([D], BF16) for n_ in ('at', 'rt', 'bt', 'kt', 'bh', 'kh')}
    tT = {n_: A.alloc([8, 128], BF16) for n_ in ('at', 'rt', 'bt', 'kt')}
    Db = [A.alloc([4, 128], BF16) for _ in range(2)]
    DTb = [A.alloc([4, 128], BF16) for _ in range(2)]
    NT0 = A.alloc([4, 128], BF16)
    NoffT = A.alloc([4, 128], BF16)
    Z1sb = A.alloc([4, 128], BF16)
    Wsb = A.alloc([4, 128], BF16)
    lm = A.alloc([7, 128], BF16)
    lmst = A.alloc([7, 128], F32)
    DMA('sp', lmst, lvlmask.rearrange("a p f -> p a f"), [], ['lmst'])
    O('dve', 'tensor_copy', ['lmst'], ['lm'], out=lm, in_=lmst)
    X = A.alloc([16, 128], BF16)
    AakT = A.alloc([16, 128], BF16)
    ArbT = A.alloc([16, 128], BF16)
    ArkT = A.alloc([16, 128], BF16)
    Wb = A.alloc([16, 64], BF16)
    Ub = A.alloc([16, 64], BF16)
    yt = A.alloc([D], F32)
    Hst = [A.alloc([8, 64], F32) for _ in range(2)]
    Hbf = [A.alloc([8, 64], BF16) for _ in range(2)]
    tmpH = A.alloc([8, 64], F32)
    for z in range(2):
        O('dve', 'memset', [], [('Hst', z)], ap=Hst[z], constant=0.0)
        O('dve', 'memset', [], [('Hbf', z)], ap=Hbf[z], constant=0.0)
    bsel = [0]

    def nb_():
        bsel[0] = (bsel[0] + 1) % 4
        return 4 + bsel[0]

    def hv(b):
        return bank(b).rearrange("p (h t) -> p h t", t=128)

    for step in range(GSTEPS):
        for z in range(2):
            t = orders[z][step]
            rows = slice(t * 128, (t + 1) * 128)
            tri = rm[:, z, :]
            strict = rm[:, 2 + z, :]
            strictT = rm[:, 3 - z, :]
            DMA('sp', lw, lw_s[z, rows, :], [], ['lw'])
            DMA('sp', ain, a_s[rows, :], [], ['ain'])
            DMA('sp', rin, r_s[rows, :], [], ['rin'])
            DMA('sp', bin_, bb_s[z, rows, :], [], ['bin'])
            DMA('sp', kin, kd_s[z, rows, :], [], ['kin'])
            DMA('sp', vin, v_s[rows, :], [], ['vin'])
            for n in range(2):
                O('pe', 'matmul', ['lw', 'rm'], [('ps', n)], out=bank(n), lhsT=tri, rhs=lw[:, n * 512:(n + 1) * 512], start=True, stop=True)
                O('pe', 'matmul', ['lw', 'onesf'], [('ps', 2 + n)], out=bank(2 + n), lhsT=onesf, rhs=lw[:, n * 512:(n + 1) * 512], start=True, stop=True)
            for c8 in range(8):
                O('pe', 'matmul', ['lw', 'onesf'], [('ps', 4)], out=bank(4, 4, 4 * c8), lhsT=lw[:, c8 * 128:(c8 + 1) * 128], rhs=onesf[:, 0:4], start=True, stop=True)
            O('act', 'copy', [('ps', 0), ('ps', 1)], ['cumS'], out=cumS, in_=psum[:, 0:1024])
            O('act', 'activation', [('ps', 0), ('ps', 1)], ['E1'], out=E1, in_=psum[:, 0:1024], func=AF.Exp)
            O('dve', 'tensor_tensor', [('ps', 2), ('ps', 3), 'cumS'], ['E3'], out=E3, in0=psum[:, 1024:2048], in1=cumS, op=ALU.subtract)
            O('act', 'activation', ['E3'], ['E3'], out=E3, in_=E3, func=AF.Exp)
            O('dve', 'tensor_tensor', ['cumS', 'lw'], ['Ee'], out=Ee, in0=cumS, in1=lw, op=ALU.subtract)
            O('act', 'activation', ['Ee'], ['Ee'], out=Ee, in_=Ee, func=AF.Exp)
            O('act', 'activation', ['cumS'], ['E2'], out=E2, in_=cumS, func=AF.Exp, scale=-1.0)
            O('act', 'activation', [('ps', 4)], ['gL'], out=gL, in_=bank(4, 32).rearrange("p (c x) -> p c x", x=4)[:, :, 0], func=AF.Exp)
            if GLEVEL < 2:
                continue
            O('dve', 'tensor_tensor', ['ain', 'Ee'], ['at'], out=sc6['at'], in0=ain, in1=Ee, op=ALU.mult)
            O('dve', 'tensor_tensor', ['rin', 'E1'], ['rt'], out=sc6['rt'], in0=rin, in1=E1, op=ALU.mult)
            O('dve', 'tensor_tensor', ['bin', 'E2'], ['bt'], out=sc6['bt'], in0=bin_, in1=E2, op=ALU.mult)
            O('dve', 'tensor_tensor', ['kin', 'E2'], ['kt'], out=sc6['kt'], in0=kin, in1=E2, op=ALU.mult)
            O('dve', 'tensor_tensor', ['bin', 'E3'], ['bh'], out=sc6['bh'], in0=bin_, in1=E3, op=ALU.mult)
            O('dve', 'tensor_tensor', ['kin', 'E3'], ['kh'], out=sc6['kh'], in0=kin, in1=E3, op=ALU.mult)
            for bi, n_ in enumerate(('at', 'rt', 'bt', 'kt')):
                transpose8(sc6[n_], n_, tT[n_], n_ + 'T', bi, ('ps', bi), evac=('act' if bi % 2 == 0 else 'dve'))
            atT, rtT, btT, ktT = tT['at'], tT['rt'], tT['bt'], tT['kt']
            for n_ in ('at', 'rt', 'bt'):
                for par in range(2):
                    O('dve' if par == 0 else 'pool', 'tensor_scalar', [n_ + 'T', 'pm'], [(n_ + 'P', par)], out=tP[n_][par], in0=tT[n_],
                      scalar1=pm[:, par:par + 1], scalar2=None, op0=ALU.mult)

            def hsl(T_, h):
                return T_[(h % 2) * 64:(h % 2) * 64 + 64, h // 2, :]

            if GLEVEL < 3:
                continue
            for hg in range(4):
                for (dst_, dk, L_, Lk, R_, Rk, msk) in ((AakT, 'AakT', ktT, 'ktT', 'at', 'atP', strict), (ArbT, 'ArbT', btT, 'btT', 'rt', 'rtP', tri),
                                                      (ArkT, 'ArkT', ktT, 'ktT', 'rt', 'rtP', tri)):
                    b = nb_()
                    for hh in range(4):
                        h = hg * 4 + hh
                        if GTEST == 1 and h % 2 == 1:
                            continue
                        if GTEST == 2 and h % 2 == 0:
                            continue
                        O('pe', 'matmul', [Lk, (Rk, h % 2)], [('ps', b)], out=bank(b, 128, hh * 128), lhsT=L_[:, h // 2, :], rhs=tP[R_][h % 2][:, h // 2, :],
                          start=True, stop=True)
                    if GLEVEL >= 3.1:
                        O('dve', 'tensor_tensor', [('ps', b), 'rm'], [(dk, hg)], out=dst_[:, hg * 4:hg * 4 + 4, :], in0=hv(b),
                          in1=msk.unsqueeze(1).broadcast_to([128, 4, 128]), op=ALU.mult)
                if GLEVEL < 3.2:
                    continue
                b = nb_()
                for hh in range(4):
                    h = hg * 4 + hh
                    O('pe', 'matmul', [('btP', h % 2), 'atT'], [('ps', b)], out=bank(b, 128, hh * 128), lhsT=atT[:, h // 2, :], rhs=tP['bt'][h % 2][:, h // 2, :],
                      start=True, stop=True)
                O('dve', 'tensor_tensor', [('ps', b), 'rm'], ['NT0'], out=NT0, in0=hv(b), in1=strictT.unsqueeze(1).broadcast_to([128, 4, 128]), op=ALU.mult)
                O('dve', 'tensor_copy', ['identb'], [('D', 0)], out=Db[0], in_=identb.unsqueeze(1).broadcast_to([128, 4, 128]))
                O('dve', 'tensor_copy', ['identb'], [('DT', 0)], out=DTb[0], in_=identb.unsqueeze(1).broadcast_to([128, 4, 128]))
                if GLEVEL < 3.4:
                    continue
                for lv in range(7):
                    cur, nxt = lv % 2, (lv + 1) % 2
                    O('dve', 'tensor_tensor', ['NT0', 'lm'], ['NoffT'], out=NoffT, in0=NT0, in1=lm[:, lv, :].unsqueeze(1).broadcast_to([128, 4, 128]), op=ALU.mult)
                    b1 = nb_()
                    for hh in range(4):
                        O('pe', 'matmul', ['NoffT', ('D', cur)], [('ps', b1)], out=bank(b1, 128, hh * 128), lhsT=NoffT[:, hh, :], rhs=Db[cur][:, hh, :],
                          start=True, stop=True)
                    O('act', 'copy', [('ps', b1)], ['Z1sb'], out=Z1sb, in_=hv(b1))
                    b2 = nb_()
                    for hh in range(4):
                        O('pe', 'matmul', ['Z1sb', ('DT', cur)], [('ps', b2)], out=bank(b2, 128, hh * 128), lhsT=DTb[cur][:, hh, :], rhs=Z1sb[:, hh, :],
                          start=True, stop=True)
                    if GLEVEL < 3.6:
                        continue
                    if lv < 6:
                        O('act', 'copy', [('ps', b2)], ['Wsb'], out=Wsb, in_=hv(b2))
                        O('dve', 'tensor_tensor', [('ps', b2), ('D', cur)], [('D', nxt)], out=Db[nxt], in0=hv(b2), in1=Db[cur], op=ALU.add)
                        if GLEVEL < 3.8:
                            continue
                        b3 = nb_()
                        pv3 = bank_bf(b3).rearrange("p (h t) -> p h t", t=128)[:, 0:4, :]
                        for hh in range(4):
                            O('pe', 'transpose', ['Wsb', 'identb'], [('ps', b3)], out=pv3[:, hh, :], in_=Wsb[:, hh, :], identity=identb)
                        O('dve', 'tensor_tensor', [('ps', b3), ('DT', cur)], [('DT', nxt)], out=DTb[nxt], in0=pv3, in1=DTb[cur], op=ALU.add)
                    else:
                        O('dve', 'tensor_tensor', [('ps', b2), ('D', cur)], [('X', hg)], out=X[:, hg * 4:hg * 4 + 4, :], in0=hv(b2), in1=Db[cur], op=ALU.add)
            if GLEVEL < 4:
                continue
            Hk = ('Hbf', z)
            Xk = [('X', hg) for hg in range(4)]
            for h in range(16):
                po, c8 = (h % 2) * 64, h // 2
                outp = bank(h // 8, 64, (h % 8) * 64)
                O('pe', 'matmul', [('AakT', h // 4), 'vin'], [('ps', h // 8)], out=outp, lhsT=AakT[:, h, :], rhs=vin[:, h * 64:(h + 1) * 64], start=True, stop=False)
                O('pe', 'matmul', [('atP', h % 2), Hk], [('ps', h // 8)], out=outp, lhsT=tP['at'][h % 2][:, c8, :], rhs=Hbf[z][:, c8, :], start=False, stop=True)
            O('act', 'copy', [('ps', 0), ('ps', 1)], ['Wb'], out=Wb.rearrange("p h d -> p (h d)"), in_=psum[:, 0:1024])
            for h in range(16):
                O('pe', 'matmul', Xk + ['Wb'], [('ps', 2 + h // 8)], out=bank(2 + h // 8, 64, (h % 8) * 64), lhsT=X[:, h, :], rhs=Wb[:, h, :], start=True, stop=True)
            O('dve', 'tensor_copy', [('ps', 2), ('ps', 3)], ['Ub'], out=Ub.rearrange("p h d -> p (h d)"), in_=psum[:, 1024:2048])
            if t >= 2:
                for h in range(16):
                    po, c8 = (h % 2) * 64, h // 2
                    outp = bank(4 + h // 8, 64, (h % 8) * 64)
                    O('pe', 'matmul', [('rtP', h % 2), Hk], [('ps', 4 + h // 8)], out=outp, lhsT=tP['rt'][h % 2][:, c8, :], rhs=Hbf[z][:, c8, :], start=True, stop=False)
                    O('pe', 'matmul', [('ArbT', h // 4), 'Ub'], [('ps', 4 + h // 8)], out=outp, lhsT=ArbT[:, h, :], rhs=Ub[:, h, :], start=False, stop=False)
                    O('pe', 'matmul', [('ArkT', h // 4), 'vin'], [('ps', 4 + h // 8)], out=outp, lhsT=ArkT[:, h, :], rhs=vin[:, h * 64:(h + 1) * 64], start=False, stop=True)
                O('act', 'copy', [('ps', 4), ('ps', 5)], ['yt'], out=yt, in_=psum[:, 2048:3072])
                DMA('pool', y_s[z, rows, :], yt, ['yt'], [])
            for c8 in range(8):
                outp = bank(6 + c8 // 4, 128, (c8 % 4) * 128)
                O('pe', 'matmul', ['bh', 'Ub'], [('ps', 6 + c8 // 4)], out=outp, lhsT=sc6['bh'][:, c8 * 128:(c8 + 1) * 128],
                  rhs=Ub[:, 2 * c8:2 * c8 + 2, :].rearrange("p h d -> p (h d)"), start=True, stop=False)
                O('pe', 'matmul', ['kh', 'vin'], [('ps', 6 + c8 // 4)], out=outp, lhsT=sc6['kh'][:, c8 * 128:(c8 + 1) * 128], rhs=vin[:, c8 * 128:(c8 + 1) * 128],
                  start=False, stop=True)
            pD = psum[:, 3072:4096].rearrange("p (c x) -> p c x", x=128)
            O('dve', 'tensor_tensor', [('Hst', z), 'gL'], ['tmpH'], out=tmpH, in0=Hst[z], in1=gL.unsqueeze(2).broadcast_to([128, 8, 64]), op=ALU.mult)
            O('dve', 'scalar_tensor_tensor', ['tmpH', ('ps', 6), ('ps', 7), 'pm'], ['tmpH'], out=tmpH, in0=pD[:, :, 0:64], scalar=pm[:, 0:1], in1=tmpH,
              op0=ALU.mult, op1=ALU.add)
            O('dve', 'scalar_tensor_tensor', ['tmpH', ('ps', 6), ('ps', 7), 'pm'], [('Hst', z)], out=Hst[z], in0=pD[:, :, 64:128], scalar=pm[:, 1:2], in1=tmpH,
              op0=ALU.mult, op1=ALU.add)
            O('act', 'copy', [('Hst', z)], [Hk], out=Hbf[z], in_=Hst[z])
            if 'dumpG' in debug and z == 0 and step < 3:
                for (nm, ap_, keys, dt_) in (('X', X, Xk, BF16), ('AakT', AakT, [('AakT', q) for q in range(4)], BF16),
                                             ('ArbT', ArbT, [('ArbT', q) for q in range(4)], BF16), ('Wb', Wb, ['Wb'], BF16), ('Ub', Ub, ['Ub'], BF16),
                                             ('Hst', Hst[0], [('Hst', 0)], F32), ('at', sc6['at'], ['at'], BF16), ('atT', atT, ['atT'], BF16),
                                             ('E1', E1, ['E1'], F32), ('gL', gL, ['gL'], F32), ('bh', sc6['bh'], ['bh'], BF16)):
                    shp = list(ap_.shape)
                    dd = nc.dram_tensor("dbg_%s_%d" % (nm, step), shp, dt_, kind="ExternalOutput").ap()
                    DMA('sp', dd, ap_, keys, [])
    P.barrier()
    A.reset()
    if stop_after == 'G':
        P.finish()
        return nc

    lnw = A.alloc([D], F32)
    lnb = A.alloc([D], F32)
    DMA('pool', lnw, rw_vecs[7:8, :].partition_broadcast(128), [], ['lnw'])
    DMA('pool', lnb, rw_vecs[8:9, :].partition_broadcast(128), [], ['lnb'])
    eps2 = A.alloc([1], F32)
    O('dve', 'memset', [], ['eps2'], ap=eps2, constant=64e-5)
    yf = [A.alloc([D], F32) for _ in range(2)]
    yb = [A.alloc([D], F32) for _ in range(2)]
    bon = [A.alloc([D], F32) for _ in range(2)]
    gin = [A.alloc([D], BF16) for _ in range(2)]
    yo = [A.alloc([D], BF16) for _ in range(2)]
    sq = A.alloc([D], F32)
    st = A.alloc([16], F32)
    h3 = lambda ap: ap.rearrange("p (h d) -> p h d", d=64)
    for t in range(2, NT):
        p = t % 2
        rows = slice(t * 128, (t + 1) * 128)
        DMA('sp', yf[p], y_s[0, rows, :], [], [('yf', p)])
        DMA('sp', yb[p], y_s[1, rows, :], [], [('yb', p)])
        DMA('sp', bon[p], bon_s[rows, :], [], [('bon', p)])
        DMA('sp', gin[p], g_s[rows, :], [], [('gin', p)])
        y = yf[p]
        yk = ('yf', p)
        O('dve', 'tensor_tensor', [yk, ('yb', p)], [yk], out=y, in0=y, in1=yb[p], op=ALU.add)
        O('dve', 'tensor_reduce', [yk], ['st'], out=st, in_=h3(y), axis=AX.X, op=ALU.add)
        O('dve', 'tensor_scalar', ['st'], ['st'], out=st, in0=st, scalar1=-1.0 / 64, scalar2=None, op0=ALU.mult)
        O('dve', 'tensor_tensor', [yk, 'st'], [yk], out=h3(y), in0=h3(y), in1=st.unsqueeze(2).broadcast_to([128, 16, 64]), op=ALU.add)
        O('dve', 'tensor_tensor', [yk], ['sq'], out=sq, in0=y, in1=y, op=ALU.mult)
        O('dve', 'tensor_reduce', ['sq'], ['st'], out=st, in_=h3(sq), axis=AX.X, op=ALU.add)
        O('act', 'activation', ['st', 'eps2'], ['st'], out=st, in_=st, func=AF.Ln, scale=1.0 / 64, bias=eps2)
        O('act', 'activation', ['st'], ['st'], out=st, in_=st, func=AF.Exp, scale=-0.5)
        O('dve', 'tensor_tensor', [yk, 'st'], [yk], out=h3(y), in0=h3(y), in1=st.unsqueeze(2).broadcast_to([128, 16, 64]), op=ALU.mult)
        O('dve', 'tensor_tensor', [yk, 'lnw'], [yk], out=y, in0=y, in1=lnw, op=ALU.mult)
        O('dve', 'tensor_tensor', [('bon', p), 'lnb'], [('bon', p)], out=bon[p], in0=bon[p], in1=lnb, op=ALU.add)
        O('dve', 'tensor_tensor', [yk, ('bon', p)], [yk], out=y, in0=y, in1=bon[p], op=ALU.add)
        O('dve', 'tensor_tensor', [yk, ('gin', p)], [('yo', p)], out=yo[p], in0=y, in1=gin[p], op=ALU.mult)
        DMA('pool', yc_s[rows, :], yo[p], [('yo', p)], [])
    P.barrier()
    A.reset()

    outproj_mlp(1, rw_w_o, [(yc_s, D)], list(range(2, NT)), lambda t: xs1[t * 128:(t + 1) * 128, :],
                lambda t: out_d[(t - 2) * 128:(t - 1) * 128, :])

    P.finish()
    return nc


def host_inputs(inputs, b):
    f = np.float32
    m = {}
    m["x"] = np.ascontiguousarray(inputs["x"][b])
    m["ctx"] = np.ascontiguousarray(inputs["ctx"][b])
    c2 = np.stack([inputs["c"][b], inputs["c_ctx"]], 0)
    m["c2T"] = np.ascontiguousarray(c2.reshape(2, 8, 128).transpose(2, 1, 0))
    m["ada_w"] = inputs["ada_w"]
    m["ada_b"] = inputs["ada_b"]
    m["ab_w_in"] = np.ascontiguousarray(inputs["ab_w_in"][0])
    m["ab_gate_b"] = np.ascontiguousarray(inputs["ab_gate_b"])
    m["qk_norm"] = np.stack([np.tile(inputs["na_q_norm"][0], 8), np.tile(inputs["na_k_norm"][0], 8)], 0).astype(f)
    m["ident"] = np.eye(128, dtype=f)
    pos = np.arange(T_LAT)
    rows = (pos // 64).astype(f)
    cols = (pos % 64).astype(f)
    inv = (10000.0 ** (-np.arange(32, dtype=f) / 32)).astype(f)
    ar = (rows[:, None] * inv).astype(f)
    ac = (cols[:, None] * inv).astype(f)
    C = np.concatenate([np.cos(ar), np.cos(ar), np.cos(ac), np.cos(ac)], 1).astype(f)
    S = np.concatenate([-np.sin(ar), np.sin(ar), -np.sin(ac), np.sin(ac)], 1).astype(f)
    m["ropeC"] = np.ascontiguousarray(np.tile(C, (1, 4)))
    m["ropeS"] = np.ascontiguousarray(np.tile(S, (1, 4)))
    m["na_bias"] = na_bias_table(inputs["na_rpb"][0])
    k = np.arange(128)
    triF = (k[:, None] <= k[None, :]).astype(f)
    triB = (k[:, None] >= k[None, :]).astype(f)
    m["trimask"] = np.stack([triF, triB, (1 - triF) * NEG, (1 - triB) * NEG], 0).astype(f)
    sF = (k[:, None] < k[None, :]).astype(f)
    m["rwmask"] = np.stack([triF, triB, sF, sF.T], 0).astype(f)
    lv = []
    for l in range(7):
        B = 1 << l
        lv.append(((k[:, None] // (2 * B)) == (k[None, :] // (2 * B))) & ((k[:, None] // B) != (k[None, :] // B)))
    m["lvlmask"] = np.stack(lv, 0).astype(f)
    m["pmask"] = np.stack([(k < 64), (k >= 64)], 1).astype(f)
    m["head_norm"] = np.ascontiguousarray(inputs["ml_head_norm"][0].reshape(1, 512))
    m["ab_w_out"] = np.ascontiguousarray(inputs["ab_w_out"][0])
    m["mlp_w1"] = inputs["mlp_w1"]
    m["mlp_w2"] = inputs["mlp_w2"]
    m["rw_muT"] = np.ascontiguousarray(inputs["rw_mu"][0].reshape(6, 8, 128).transpose(2, 0, 1))
    m["rw_w_rkv"] = np.ascontiguousarray(inputs["rw_w_rkv"][0])
    m["rw_w1c"] = np.ascontiguousarray(np.concatenate([inputs["rw_w1"][0, 0], inputs["rw_w1"][0, 1]], 1))
    m["rw_w2c"] = np.ascontiguousarray(np.concatenate([inputs["rw_w2"][0, 0], inputs["rw_w2"][0, 1]], 0))
    m["rw_a1c"] = np.ascontiguousarray(np.concatenate([inputs["rw_a1"][0, 0], inputs["rw_a1"][0, 1]], 1))
    m["rw_a2c"] = np.ascontiguousarray(np.concatenate([inputs["rw_a2"][0, 0], inputs["rw_a2"][0, 1]], 0))
    m["rw_g1"] = np.ascontiguousarray(inputs["rw_g1"][0])
    m["rw_g2"] = np.ascontiguousarray(inputs["rw_g2"][0])
    m["rw_vecs"] = np.stack([inputs["rw_w0"][0, 0], inputs["rw_w0"][0, 1], inputs["rw_a0"][0, 0], inputs["rw_a0"][0, 1],
                             inputs["rw_k_k"][0], inputs["rw_k_a"][0], inputs["rw_r_k"][0], inputs["rw_lnx_w"][0],
                             inputs["rw_lnx_b"][0]], 0).astype(f)
    m["rw_w_o"] = np.ascontiguousarray(inputs["rw_w_o"][0])
    return m


def na_bias_table(rpb):
    tab = np.full((8, 21, 128, 128), NEG, np.float32)
    col = np.arange(64)
    cw = np.clip(col - 8, 0, 48)
    valid = (col[:, None] >= cw[None, :]) & (col[:, None] < cw[None, :] + 16)
    cidx = np.clip(col[:, None] - col[None, :] + 15, 0, 30)

    def fill(idx, qt, kc):
        for qr in range(2):
            q_row = 2 * qt + qr
            rs = min(max(q_row - 4, 0), 56)
            for kr in range(2):
                k_row = 2 * kc + kr
                if not (rs <= k_row < rs + 8):
                    continue
                vals = rpb[:, k_row - q_row + 7][:, cidx]
                tab[:, idx, kr * 64:(kr + 1) * 64, qr * 64:(qr + 1) * 64] = np.where(valid[None], vals, NEG)

    for rel in range(-2, 3):
        fill(rel + 2, 10, 10 + rel)
    for sidx, lq in enumerate((0, 1, 30, 31)):
        base = 0 if lq < 2 else 28
        for j in range(4):
            fill(5 + 4 * sidx + j, lq, base + j)
    return np.ascontiguousarray(tab.transpose(0, 2, 1, 3).reshape(8, 128, 21 * 128))


_CACHE = {}


def kernel(**inputs):
    inputs = {k: np.asarray(v) for k, v in inputs.items()}
    if "nc" not in _CACHE:
        _CACHE["nc"] = build_program()
    nc = _CACHE["nc"]
    in_maps = [host_inputs(inputs, b) for b in range(8)]
    res = run_bass_kernel_spmd(nc, in_maps, core_ids=list(range(8)))
    return np.stack([r["out"] for r in res.results], 0).astype(np.float32)
```

```python
import numpy as np
import concourse.bass as bass
import concourse.mybir as mybir
from concourse.bass_utils import run_bass_kernel_spmd

F32 = mybir.dt.float32
BF16 = mybir.dt.bfloat16
U8 = mybir.dt.uint8
ALU = mybir.AluOpType
AF = mybir.ActivationFunctionType
AX = mybir.AxisListType

ENGS = ['pe', 'act', 'dve', 'pool', 'sp']
EPOCH = 12000
NDS = 12

D = 1024
T_LAT = 4096
T_CTX = 256
NT = 34
NEG = -30000.0
GLEVEL = 4
GTEST = 0
GSTEPS = NT


class Prog:
    def __init__(self, nc):
        self.nc = nc
        self.ops = {e: [] for e in ENGS}
        self.cnt = {}
        self.last_w = {}
        self.readers = {}
        self.waited = {e: {} for e in ENGS}
        self.sems = {}
        self.nops = 0

    def _ticket(self, eng, kind):
        k = (eng, kind)
        n = self.cnt.get(k, 0)
        self.cnt[k] = n + 1
        if kind == 'c':
            ep, v = divmod(n, EPOCH)
            return ((eng, kind, ep), v + 1)
        return ((eng, kind, n % NDS), 16 * (n // NDS + 1))

    def op(self, eng, fn, reads=(), writes=(), dma=False):
        kind = 'd' if dma else 'c'
        deps = {}
        pr = [r for r in reads if isinstance(r, tuple) and r[0] == 'ps']
        if pr:
            reads = [r for r in reads if not (isinstance(r, tuple) and r[0] == 'ps')]
            writes = list(writes) + [r for r in pr if r not in writes]

        def add(t, war=False):
            if t is None:
                return
            key, val = t
            if not dma and key[1] == 'c' and key[0] == eng:
                if eng == 'pe':
                    return
            if deps.get(key, 0) < val:
                deps[key] = val

        for r in reads:
            add(self.last_w.get(r))
        for w in writes:
            add(self.last_w.get(w))
            for t in self.readers.get(w, ()):
                add(t, war=True)
        if dma:
            n = self.cnt.get((eng, 'd'), 0)
            if n >= NDS:
                add(((eng, 'd', n % NDS), 16 * (n // NDS)))
        waits = []
        wd = self.waited[eng]
        for key, val in deps.items():
            if wd.get(key, 0) >= val:
                continue
            wd[key] = val
            waits.append((key, val))
        t = self._ticket(eng, kind)
        self.ops[eng].append((fn, waits, t))
        for w in writes:
            self.last_w[w] = t
            self.readers[w] = []
        for r in reads:
            self.readers.setdefault(r, []).append(t)
        self.nops += 1
        return t

    def barrier(self):
        finals = {}
        for e in ENGS:
            for (fn, waits, t) in self.ops[e]:
                if t is None:
                    continue
                key, val = t
                if finals.get(key, 0) < val:
                    finals[key] = val
        for e in ENGS:
            wd = self.waited[e]
            waits = []
            for key, val in finals.items():
                if key[0] == e and key[1] == 'c' and e == 'pe':
                    continue
                if wd.get(key, 0) >= val:
                    continue
                wd[key] = val
                waits.append((key, val))
            if waits:
                self.ops[e].append((None, waits, None))
        self.last_w = {}
        self.readers = {}

    def finish(self):
        self.barrier()
        nc = self.nc
        keys = set()
        for e in ENGS:
            for (fn, waits, t) in self.ops[e]:
                if t is not None:
                    keys.add(t[0])
                for (k, v) in waits:
                    keys.add(k)
        for k in sorted(keys):
            self.sems[k] = nc.alloc_semaphore("s_%s_%s_%d" % k)
        sems = self.sems
        ops = self.ops

        def emit(ename, e):
            for (fn, waits, t) in ops[ename]:
                for (k, v) in waits:
                    e.wait_ge(sems[k], v)
                if fn is None:
                    continue
                ins = fn(e)
                ins.then_inc(sems[t[0]], 16 if t[0][1] == 'd' else 1)

        with nc.Block() as block:
            @block.tensor
            def _(e):
                emit('pe', e)

            @block.scalar
            def _(e):
                emit('act', e)

            @block.vector
            def _(e):
                emit('dve', e)

            @block.gpsimd
            def _(e):
                emit('pool', e)

            @block.sync
            def _(e):
                emit('sp', e)


class Arena:
    def __init__(self, nc, nbytes):
        self.t = nc.alloc_sbuf_tensor("arena", [128, nbytes], U8)
        self.size = nbytes
        self.off = 0
        self.mark_ = 0

    def alloc(self, shape, dt):
        esz = 2 if dt == BF16 else 4
        n = 1
        for s in shape:
            n *= s
        nb = (n * esz + 63) // 64 * 64
        assert self.off + nb <= self.size, ("SBUF arena overflow", self.off, nb)
        self.hw = max(getattr(self, 'hw', 0), self.off + nb)
        ap = self.t[:, self.off:self.off + n * esz].bitcast(dt)
        self.off += nb
        if len(shape) == 2:
            ap = ap.rearrange("p (a b) -> p a b", b=shape[1])
        elif len(shape) == 3:
            ap = ap.rearrange("p (a b c) -> p a b c", b=shape[1], c=shape[2])
        return ap

    def mark(self):
        self.mark_ = self.off

    def reset(self):
        self.off = self.mark_


def build_program(debug=(), stop_after=None):
    nc = bass.Bass("TRN2", target_bir_lowering=False)
    P = Prog(nc)

    def din(name, shape, dt=F32):
        return nc.dram_tensor(name, list(shape), dt, kind="ExternalInput").ap()

    def scratch(name, shape, dt):
        kind = "ExternalOutput" if name in debug else "Internal"
        return nc.dram_tensor(name, list(shape), dt, kind=kind).ap()

    x_in = din("x", [T_LAT, D])
    ctx_in = din("ctx", [T_CTX, D])
    c2T = din("c2T", [128, 8, 2])
    ada_w = din("ada_w", [2, D, 6 * D])
    ada_b = din("ada_b", [2, 6 * D])
    ab_w_in = din("ab_w_in", [D, 3600])
    gate_b = din("ab_gate_b", [1, 16])
    qk_norm = din("qk_norm", [2, 512])
    ident_in = din("ident", [128, 128])
    ropeC = din("ropeC", [T_LAT, 512])
    ropeS = din("ropeS", [T_LAT, 512])
    na_bias = din("na_bias", [8, 128, 21 * 128])
    trimask = din("trimask", [4, 128, 128])
    rwmask = din("rwmask", [4, 128, 128])
    lvlmask = din("lvlmask", [7, 128, 128])
    pmask_in = din("pmask", [128, 2])
    head_norm = din("head_norm", [1, 512])
    ab_w_out = din("ab_w_out", [D, D])
    mlp_w1 = din("mlp_w1", [2, D, 4 * D])
    mlp_w2 = din("mlp_w2", [2, 4 * D, D])
    rw_muT = din("rw_muT", [128, 6, 8])
    rw_w_rkv = din("rw_w_rkv", [3, D, D])
    rw_w1c = din("rw_w1c", [D, 128])
    rw_w2c = din("rw_w2c", [128, D])
    rw_a1c = din("rw_a1c", [D, 128])
    rw_a2c = din("rw_a2c", [128, D])
    rw_g1 = din("rw_g1", [D, 160])
    rw_g2 = din("rw_g2", [160, D])
    rw_vecs = din("rw_vecs", [9, D])
    rw_w_o = din("rw_w_o", [D, D])
    out_d = nc.dram_tensor("out", [T_LAT, D], F32, kind="ExternalOutput").ap()

    modv = scratch("modv", [2, 2, 6 * D], F32)
    qa_s = scratch("qa_s", [NT * 128, 512], BF16)
    ka_s = scratch("ka_s", [NT * 128, 512], BF16)
    va_s = scratch("va_s", [NT * 128, 520], BF16)
    qb_s = scratch("qb_s", [NT * 128, 512], BF16)
    kb_s = scratch("kb_s", [NT * 128, 512], BF16)
    vb_s = scratch("vb_s", [NT * 128, 516], BF16)
    ob_s = scratch("ob_s", [NT * 128, 512], F32)
    gt_s = scratch("gt_s", [NT * 128, 16], F32)
    na_s = scratch("na_s", [NT * 128, 512], BF16)
    hf_s = scratch("hf_s", [NT * 128, 512], F32)
    hb_s = scratch("hb_s", [NT * 128, 512], F32)
    ml_s = scratch("ml_s", [NT * 128, 512], BF16)
    x1_s = scratch("x1_s", [NT * 128, D], F32)
    xs1 = scratch("xs1", [NT * 128, D], F32)
    hbuf = scratch("hbuf", [NT * 128, D], F32)
    r_s = scratch("r_s", [NT * 128, D], BF16)
    v_s = scratch("v_s", [NT * 128, D], BF16)
    a_s = scratch("a_s", [NT * 128, D], BF16)
    g_s = scratch("g_s", [NT * 128, D], BF16)
    bon_s = scratch("bon_s", [NT * 128, D], F32)
    kd_s = scratch("kd_s", [2, NT * 128, D], BF16)
    bb_s = scratch("bb_s", [2, NT * 128, D], BF16)
    lw_s = scratch("lw_s", [2, NT * 128, D], F32)
    y_s = scratch("y_s", [2, NT * 128, D], F32)
    yc_s = scratch("yc_s", [NT * 128, D], BF16)

    A = Arena(nc, 190 * 1024)
    psum = nc.alloc_psum_tensor("psum", [128, 4096], F32)

    def bank(i, n=512, off=0):
        return psum[:, i * 512 + off:i * 512 + off + n]

    def bank_bf(i):
        return psum[:, i * 512:(i + 1) * 512].bitcast(BF16)

    def O(eng, method, R, W, **kw):
        return P.op(eng, lambda e: getattr(e, method)(**kw), reads=R, writes=W)

    def DMA(eng, out, in_, R, W, slow=False):
        if slow:
            return P.op(eng, lambda e: e.dma_start(out=out, in_=in_, allow_slow_non_contiguous=True), reads=R, writes=W, dma=True)
        return P.op(eng, lambda e: e.dma_start(out=out, in_=in_), reads=R, writes=W, dma=True)

    def xrows(t):
        if t < 2:
            return ctx_in[t * 128:(t + 1) * 128, :]
        return x_in[(t - 2) * 128:(t - 1) * 128, :]

    identf = A.alloc([128], F32)
    identb = A.alloc([128], BF16)
    epsb = A.alloc([1], F32)
    DMA('sp', identf, ident_in, [], ['identf'])
    O('dve', 'tensor_copy', ['identf'], ['identb'], out=identb, in_=identf)
    O('dve', 'memset', [], ['epsb'], ap=epsb, constant=1e-6)
    oneb = A.alloc([1], F32)
    O('dve', 'memset', [], ['oneb'], ap=oneb, constant=1.0)
    A.mark()

    sT = A.alloc([8, 2], F32)
    mod = A.alloc([6 * D], F32)
    badd = A.alloc([6 * D], F32)
    wst = [A.alloc([8, 512], F32) for _ in range(2)]
    DMA('sp', sT, c2T, [], ['sT'])
    O('act', 'activation', ['sT'], ['sT'], out=sT, in_=sT, func=AF.Silu)
    it = 0
    for l in range(2):
        DMA('pool', badd[0:2, :], ada_b[l:l + 1, :].partition_broadcast(2), [], ['badd'])
        for n in range(12):
            wb_ = wst[it % 2]
            wk = ('wst', it % 2)
            DMA('sp', wb_, ada_w[l].rearrange("(k p) n -> p k n", p=128)[:, :, n * 512:(n + 1) * 512], [], [wk])
            pb = ('ps', it % 2)
            for k in range(8):
                O('pe', 'matmul', ['sT', wk], [pb], out=bank(it % 2)[0:2, :], lhsT=sT[:, k, :], rhs=wb_[:, k, :],
                  start=(k == 0), stop=(k == 7))
            O('dve', 'tensor_tensor', [pb, 'badd'], ['mod'], out=mod[0:2, n * 512:(n + 1) * 512],
              in0=bank(it % 2)[0:2, :], in1=badd[0:2, n * 512:(n + 1) * 512], op=ALU.add)
            it += 1
        for i in (1, 4):
            O('dve', 'tensor_scalar', ['mod'], ['mod'], out=mod[0:2, i * D:(i + 1) * D], in0=mod[0:2, i * D:(i + 1) * D],
              scalar1=1.0, scalar2=None, op0=ALU.add)
        DMA('pool', modv[l], mod[0:2, :], ['mod'], ['modv'])
    P.barrier()
    A.reset()

    def load_mod(dst, layer, seg, idx, key):
        DMA('pool', dst, modv[layer, seg:seg + 1, idx * D:(idx + 1) * D].partition_broadcast(128), ['modv'], [key])

    def rms_modulate(xt, xk, sc, sck, sh, shk, outb, outk, tmp, tmpk, ss, ssk, junk, junkk):
        O('act', 'activation', [xk], [junkk, ssk], out=junk, in_=xt, func=AF.Square, accum_out=ss)
        O('act', 'activation', [ssk, 'epsb'], [ssk], out=ss, in_=ss, func=AF.Ln, scale=1.0 / D, bias=epsb)
        O('act', 'activation', [ssk], [ssk], out=ss, in_=ss, func=AF.Exp, scale=-0.5)
        O('dve', 'scalar_tensor_tensor', [xk, ssk, sck], [tmpk], out=tmp, in0=xt, scalar=ss[:, 0:1], in1=sc,
          op0=ALU.mult, op1=ALU.mult)
        O('dve', 'tensor_tensor', [tmpk, shk], [outk], out=outb, in0=tmp, in1=sh, op=ALU.add)

    def transpose8(src, srck, dst, dstk, pbank, pkey, evac='act'):
        pv = bank_bf(pbank).rearrange("p (c t) -> p c t", t=128)
        for c in range(8):
            O('pe', 'transpose', [srck, 'identb'], [pkey], out=pv[:, c, :], in_=src[:, c * 128:(c + 1) * 128], identity=identb)
        if evac == 'act':
            O('act', 'copy', [pkey], [dstk], out=dst, in_=pv)
        else:
            O('dve', 'tensor_copy', [pkey], [dstk], out=dst, in_=pv)

    w_in = A.alloc([8, 3600], BF16)
    wst = [A.alloc([8, 400], F32) for _ in range(2)]
    for i in range(9):
        DMA('sp', wst[i % 2], ab_w_in.rearrange("(k p) n -> p k n", p=128)[:, :, i * 400:(i + 1) * 400], [], [('wst', i % 2)])
        O('pool', 'tensor_copy', [('wst', i % 2)], ['w_in'], out=w_in[:, :, i * 400:(i + 1) * 400], in_=wst[i % 2])
    scs = [A.alloc([D], F32) for _ in range(2)]
    shs = [A.alloc([D], F32) for _ in range(2)]
    for seg in range(2):
        load_mod(scs[seg], 0, seg, 1, ('sc', seg))
        load_mod(shs[seg], 0, seg, 0, ('sh', seg))
    qkn = A.alloc([2, 512], F32)
    DMA('pool', qkn[:, 0, :], qk_norm[0:1, :].partition_broadcast(128), [], ['qkn'])
    DMA('pool', qkn[:, 1, :], qk_norm[1:2, :].partition_broadcast(128), [], ['qkn'])
    gb = A.alloc([16], F32)
    DMA('pool', gb, gate_b.partition_broadcast(128), [], ['gb'])
    xts = [A.alloc([D], F32) for _ in range(2)]
    rC = [A.alloc([512], F32) for _ in range(2)]
    rS = [A.alloc([512], F32) for _ in range(2)]
    junk = A.alloc([D], F32)
    tmp = A.alloc([D], F32)
    hb = A.alloc([D], BF16)
    hT = A.alloc([8, 128], BF16)
    ss = A.alloc([1], F32)
    proj = A.alloc([3600], F32)
    sq = A.alloc([512], F32)
    hs = A.alloc([8], F32)
    qo = [A.alloc([512], BF16) for _ in range(2)]
    ko = [A.alloc([512], BF16) for _ in range(2)]
    qbo = [A.alloc([512], BF16) for _ in range(2)]
    kbo = [A.alloc([512], BF16) for _ in range(2)]
    vao = [A.alloc([8, 65], BF16) for _ in range(2)]
    vbo = [A.alloc([4, 129], BF16) for _ in range(2)]
    oo = [A.alloc([512], F32) for _ in range(2)]
    go = [A.alloc([16], F32) for _ in range(2)]
    rt1 = A.alloc([512], F32)
    rt2 = A.alloc([512], F32)
    for i in range(2):
        O('dve', 'memset', [], [('vao', i)], ap=vao[i], constant=1.0)
        O('dve', 'memset', [], [('vbo', i)], ap=vbo[i], constant=1.0)
    nchunks = [(n * 512, 512) for n in range(7)] + [(3584, 16)]
    for t in range(NT):
        p = t % 2
        seg = 1 if t < 2 else 0
        xt = xts[p]
        xk = ('xt', p)
        DMA('sp', xt, xrows(t), [], [xk])
        if seg == 0:
            DMA('sp', rC[p], ropeC[(t - 2) * 128:(t - 1) * 128, :], [], [('rC', p)])
            DMA('sp', rS[p], ropeS[(t - 2) * 128:(t - 1) * 128, :], [], [('rS', p)])
        rms_modulate(xt, xk, scs[seg], ('sc', seg), shs[seg], ('sh', seg), hb, 'hb', tmp, 'tmp', ss, 'ss', junk, 'junk')
        transpose8(hb, 'hb', hT, 'hT', 7, ('ps', 7))
        for ci, (c0, cn) in enumerate(nchunks):
            b = ci % 4
            pk = ('ps', b)
            for k in range(8):
                O('pe', 'matmul', ['hT', 'w_in'], [pk], out=bank(b, cn), lhsT=hT[:, k, :], rhs=w_in[:, k, c0:c0 + cn],
                  start=(k == 0), stop=(k == 7))
            if ci % 2 == 0:
                O('act', 'copy', [pk], [('proj', ci)], out=proj[:, c0:c0 + cn], in_=bank(b, cn))
            else:
                O('dve', 'tensor_copy', [pk], [('proj', ci)], out=proj[:, c0:c0 + cn], in_=bank(b, cn))
        rows = slice(t * 128, (t + 1) * 128)
        for which, (dst, dstk, dram) in enumerate(((qo[p], ('qo', p), qa_s), (ko[p], ('ko', p), ka_s))):
            src = proj[:, which * 512:(which + 1) * 512]
            sk = ('proj', which)
            O('dve', 'tensor_tensor', [sk], ['sq'], out=sq, in0=src, in1=src, op=ALU.mult)
            O('dve', 'tensor_reduce', ['sq'], ['hs'], out=hs, in_=sq.rearrange("p (h d) -> p h d", d=64), axis=AX.X, op=ALU.add)
            O('act', 'activation', ['hs', 'epsb'], ['hs'], out=hs, in_=hs, func=AF.Ln, scale=1.0 / 64, bias=epsb)
            O('act', 'activation', ['hs'], ['hs'], out=hs, in_=hs, func=AF.Exp, scale=-0.5)
            O('dve', 'tensor_tensor', [sk, 'hs'], ['sq'], out=sq.rearrange("p (h d) -> p h d", d=64),
              in0=src.rearrange("p (h d) -> p h d", d=64), in1=hs.unsqueeze(2).broadcast_to([128, 8, 64]), op=ALU.mult)
            O('dve', 'scalar_tensor_tensor', ['sq', 'qkn'], [dstk], out=dst, in0=sq, scalar=(0.125 if which == 0 else 1.0),
              in1=qkn[:, which, :], op0=ALU.mult, op1=ALU.mult)
            DMA('pool', dram[rows, :], dst, [dstk], [])
        O('act', 'copy', [('proj', 2)], [('vao', p)], out=vao[p][:, :, 0:64], in_=proj[:, 1024:1536].rearrange("p (h d) -> p h d", d=64))
        DMA('pool', va_s[rows, :], vao[p].rearrange("p h d -> p (h d)"), [('vao', p)], [])
        for which, (dst, dstk, dram, scl) in enumerate(((qbo[p], ('qbo', p), qb_s, 1.0), (kbo[p], ('kbo', p), kb_s, 128 ** -0.5))):
            c0 = 1536 + which * 512
            src = proj[:, c0:c0 + 512]
            sk = ('proj', 3 + which)
            if seg == 1:
                O('act', 'mul', [sk], [dstk], out=dst, in_=src, mul=scl)
            else:
                v5 = lambda ap: ap.rearrange("p (h b f d) -> p h b f d", h=4, b=2, f=2)
                O('dve', 'tensor_tensor', [sk, ('rC', p)], ['rt1'], out=rt1, in0=src, in1=rC[p], op=ALU.mult)
                for f in range(2):
                    O('dve', 'tensor_tensor', [sk, ('rS', p)], [('rt2', f)], out=v5(rt2)[:, :, :, f, :], in0=v5(src)[:, :, :, 1 - f, :],
                      in1=v5(rS[p])[:, :, :, f, :], op=ALU.mult)
                O('dve', 'tensor_tensor', ['rt1', ('rt2', 0), ('rt2', 1)], ['rt1'], out=rt1, in0=rt1, in1=rt2, op=ALU.add)
                O('act', 'mul', ['rt1'], [dstk], out=dst, in_=rt1, mul=scl)
            DMA('pool', dram[rows, :], dst, [dstk], [])
        O('act', 'copy', [('proj', 5)], [('vbo', p)], out=vbo[p][:, :, 0:128], in_=proj[:, 2560:3072].rearrange("p (h d) -> p h d", d=128))
        DMA('pool', vb_s[rows, :], vbo[p].rearrange("p h d -> p (h d)"), [('vbo', p)], [])
        O('act', 'activation', [('proj', 6)], [('oo', p)], out=oo[p], in_=proj[:, 3072:3584], func=AF.Sigmoid)
        DMA('pool', ob_s[rows, :], oo[p], [('oo', p)], [])
        g = go[p]
        gk = ('go', p)
        O('dve', 'tensor_tensor', [('proj', 7), 'gb'], [gk], out=g, in0=proj[:, 3584:3600], in1=gb, op=ALU.add)
        O('act', 'activation', [gk], [gk], out=g, in_=g, func=AF.Tanh, scale=1.0 / 15.0)
        O('dve', 'tensor_scalar', [gk], [gk], out=g, in0=g, scalar1=15.0, scalar2=None, op0=ALU.mult)
        gv = g.rearrange("p (a b h) -> p a b h", a=2, b=2)
        fv = gv[:, :, 1, :]
        O('act', 'activation', [gk], [gk], out=fv, in_=fv, func=AF.Exp, scale=-1.0)
        O('act', 'activation', [gk, 'oneb'], [gk], out=fv, in_=fv, func=AF.Ln, bias=oneb)
        O('dve', 'tensor_scalar', [gk], [gk], out=fv, in0=fv, scalar1=-1.0, scalar2=None, op0=ALU.mult)
        DMA('pool', gt_s[rows, :], g, [gk], [])
    P.barrier()
    A.reset()
    if stop_after == 'B':
        P.finish()
        return nc

    def load_T(src_dram, dstT, dstk, nchunk=4):
        tin = [A.alloc([nchunk * 128], BF16) for _ in range(2)]
        for t in range(NT):
            p = t % 2
            DMA('sp', tin[p], src_dram[t * 128:(t + 1) * 128, :], [], [('tin', p)])
            pv = bank_bf(6 + p).rearrange("p (c t) -> p c t", t=128)
            for c in range(nchunk):
                O('pe', 'transpose', [('tin', p), 'identb'], [('ps', 6 + p)], out=pv[:, c, :], in_=tin[p][:, c * 128:(c + 1) * 128],
                  identity=identb)
            if p == 0:
                O('act', 'copy', [('ps', 6 + p)], [dstk], out=dstT[:, :, t * 128:(t + 1) * 128], in_=pv[:, 0:nchunk, :])
            else:
                O('dve', 'tensor_copy', [('ps', 6 + p)], [dstk], out=dstT[:, :, t * 128:(t + 1) * 128], in_=pv[:, 0:nchunk, :])

    QT = A.alloc([4, NT * 128], BF16)
    KT = A.alloc([4, NT * 128], BF16)
    Vn = A.alloc([NT, 520], BF16)
    BT = A.alloc([8, 21, 128], BF16)
    bst = [A.alloc([21 * 128], F32) for _ in range(2)]
    for h in range(8):
        DMA('sp', bst[h % 2], na_bias[h], [], [('bst', h % 2)])
        O('pool', 'tensor_copy', [('bst', h % 2)], ['BT'], out=BT[:, h].rearrange("p a b -> p (a b)"), in_=bst[h % 2])
    DMA('sp', Vn, va_s.rearrange("(t p) f -> p t f", p=128), [], ['Vn'])
    load_T(qa_s, QT, 'QT')
    load_T(ka_s, KT, 'KT')
    PT = [A.alloc([7 * 128], BF16) for _ in range(2)]
    nao = [A.alloc([512], BF16) for _ in range(2)]
    rden = A.alloc([8], F32)
    order = list(range(2, NT)) + [0, 1]
    for qi, qt in enumerate(order):
        p = qi % 2
        if qt >= 2:
            lq = qt - 2
            if 2 <= lq <= 29:
                chunks = [(2 + lq + r, r + 2) for r in range(-2, 3)]
            else:
                sidx = {0: 0, 1: 1, 30: 2, 31: 3}[lq]
                base = 0 if lq < 2 else 28
                chunks = [(2 + base + j, 5 + 4 * sidx + j) for j in range(4)]
            chunks += [(0, None), (1, None)]
        else:
            chunks = [(0, None), (1, None)]
        ncx = len(chunks)
        for h in range(8):
            c, po = h // 2, (h % 2) * 64
            sb = h % 2
            sk = ('ps', 2 * sb)
            sk2 = ('ps', 2 * sb + 1)
            for ci, (kt, bidx) in enumerate(chunks):
                bnk = sb * 2 + ci // 4
                outp = bank(bnk, 128, (ci % 4) * 128)
                O('pe', 'matmul', ['QT', 'KT'], [sk, sk2], out=outp, lhsT=KT[po:po + 64, c, kt * 128:(kt + 1) * 128],
                  rhs=QT[po:po + 64, c, qt * 128:(qt + 1) * 128], start=True, stop=(bidx is None))
                if bidx is not None:
                    O('pe', 'matmul', ['BT', 'identb'], [sk, sk2], out=outp, lhsT=identb, rhs=BT[:, h, bidx, :], start=False, stop=True)
            n = ncx * 128
            O('act', 'activation', [sk, sk2], [('PT', sb)], out=PT[sb][:, 0:n], in_=psum[:, sb * 1024:sb * 1024 + n], func=AF.Exp)
            ob = 4 + p * 2 + h // 4
            for ci, (kt, _) in enumerate(chunks):
                O('pe', 'matmul', [('PT', sb), 'Vn'], [('ps', ob)], out=bank(ob, 65, (h % 4) * 65), lhsT=PT[sb][:, ci * 128:(ci + 1) * 128],
                  rhs=Vn[:, kt, h * 65:(h + 1) * 65], start=(ci == 0), stop=(ci == ncx - 1))
        for half in range(2):
            ob = 4 + p * 2 + half
            ov = bank(ob, 260).rearrange("p (h d) -> p h d", d=65)
            O('dve', 'reciprocal', [('ps', ob)], ['rden'], out=rden[:, half * 4:(half + 1) * 4], in_=ov[:, :, 64])
            O('dve', 'tensor_tensor', [('ps', ob), 'rden'], [('nao', p)],
              out=nao[p][:, half * 256:(half + 1) * 256].rearrange("p (h d) -> p h d", d=64), in0=ov[:, :, 0:64],
              in1=rden[:, half * 4:(half + 1) * 4].unsqueeze(2).broadcast_to([128, 4, 64]), op=ALU.mult)
        DMA('pool', na_s[qt * 128:(qt + 1) * 128, :], nao[p], [('nao', p)], [])
    P.barrier()
    A.reset()
    if stop_after == 'C':
        P.finish()
        return nc

    QbT = A.alloc([4, NT * 128], BF16)
    KbT = A.alloc([4, NT * 128], BF16)
    Kb = A.alloc([NT, 512], BF16)
    Vb = A.alloc([NT, 516], BF16)
    G = A.alloc([NT, 16], F32)
    tm = A.alloc([4, 128], F32)
    hn = A.alloc([512], F32)
    DMA('sp', Kb, kb_s.rearrange("(t p) f -> p t f", p=128), [], ['Kb'])
    DMA('sp', Vb, vb_s.rearrange("(t p) f -> p t f", p=128), [], ['Vb'])
    DMA('sp', G, gt_s.rearrange("(t p) f -> p t f", p=128), [], ['G'])
    DMA('sp', tm, trimask.rearrange("a p f -> p a f"), [], ['tm'])
    DMA('pool', hn, head_norm.partition_broadcast(128), [], ['hn'])
    load_T(qb_s, QbT, 'QbT')
    load_T(kb_s, KbT, 'KbT')
    Cst = [A.alloc([4, 129], F32) for _ in range(2)]
    Cbf = [A.alloc([4, 129], BF16) for _ in range(2)]
    for d_ in range(2):
        O('dve', 'memset', [], [('Cst', d_)], ap=Cst[d_], constant=0.0)
        O('dve', 'memset', [], [('Cbf', d_)], ap=Cbf[d_], constant=0.0)
    nb = A.alloc([4], F32)
    LFbc = A.alloc([4, 128], F32)
    Ebc = A.alloc([4, 128], F32)
    Dm = A.alloc([4, 128], F32)
    DT = A.alloc([4, 128], F32)
    PTm = A.alloc([4, 128], BF16)
    Qs = A.alloc([4, 128], BF16)
    Kt = A.alloc([4, 128], BF16)
    den = A.alloc([4], F32)
    hout = [A.alloc([512], F32) for _ in range(2)]
    hfl = [A.alloc([512], F32) for _ in range(2)]
    obl = [A.alloc([512], F32) for _ in range(2)]
    msq = A.alloc([512], F32)
    mss = A.alloc([4], F32)
    mlo = [A.alloc([512], BF16) for _ in range(2)]
    orders = [list(range(NT)), [1, 0] + list(range(NT - 1, 1, -1))]
    for step in range(NT):
        for d_ in range(2):
            t = orders[d_][step]
            p = step % 2
            tri = tm[:, d_, :]
            mask = tm[:, 2 + d_, :]
            last = 127 if d_ == 0 else 0
            lf = G[:, t, d_ * 8 + 4:d_ * 8 + 8]
            ii = G[:, t, d_ * 8:d_ * 8 + 4]
            tok = slice(t * 128, (t + 1) * 128)
            O('pe', 'matmul', ['G', 'tm'], [('ps', 0)], out=bank(0, 4), lhsT=tri, rhs=lf, start=True, stop=True)
            O('dve', 'tensor_tensor', ['G', ('ps', 0)], ['nb'], out=nb, in0=ii, in1=bank(0, 4), op=ALU.subtract)
            O('dve', 'tensor_copy', ['G'], ['LFbc'], out=LFbc, in_=lf.unsqueeze(2).broadcast_to([128, 4, 128]))
            for h in range(4):
                O('pe', 'matmul', ['LFbc', 'tm'], [('ps', 1)], out=bank(1, 128, h * 128), lhsT=LFbc[:, h, :], rhs=tri, start=True, stop=True)
            pY = bank(1).rearrange("p (h t) -> p h t", t=128)
            O('act', 'activation', [('ps', 1)], ['Ebc'], out=Ebc, in_=pY, func=AF.Exp)
            O('dve', 'tensor_tensor', [('ps', 1), 'tm'], ['Dm'], out=Dm, in0=pY, in1=mask.unsqueeze(1).broadcast_to([128, 4, 128]), op=ALU.add)
            for h in range(4):
                O('act', 'activation', ['Dm', 'nb'], [('DT', h)], out=DT[:, h, :], in_=Dm[:, h, :], func=AF.Exp, bias=nb[:, h:h + 1])
            for h in range(4):
                O('pe', 'matmul', ['QbT', 'KbT'], [('ps', 2)], out=bank(2, 128, h * 128), lhsT=KbT[:, h, tok], rhs=QbT[:, h, tok], start=True, stop=True)
            DTk = [('DT', h) for h in range(4)]
            O('dve', 'tensor_tensor', [('ps', 2)] + DTk, ['PTm'], out=PTm, in0=bank(2).rearrange("p (h t) -> p h t", t=128), in1=DT, op=ALU.mult)
            O('dve', 'tensor_tensor', ['QbT', 'Ebc'], ['Qs'], out=Qs, in0=QbT[:, :, tok], in1=Ebc, op=ALU.mult)
            O('dve', 'tensor_tensor', ['Kb'] + DTk, ['Kt'], out=Kt, in0=Kb[:, t, :].rearrange("p (h d) -> p h d", d=128),
              in1=DT[:, :, last:last + 1].broadcast_to([128, 4, 128]), op=ALU.mult)
            for h in range(4):
                wb_ = 3 + h // 2
                outp = bank(wb_, 129, (h % 2) * 129)
                O('pe', 'matmul', ['PTm', 'Vb'], [('ps', wb_)], out=outp, lhsT=PTm[:, h, :], rhs=Vb[:, t, h * 129:(h + 1) * 129], start=True, stop=False)
                O('pe', 'matmul', ['Qs', ('Cbf', d_)], [('ps', wb_)], out=outp, lhsT=Qs[:, h, :], rhs=Cbf[d_][:, h, :], start=False, stop=True)
            ho = hout[p]
            hk = ('hout', p)
            for half in range(2):
                wv = bank(3 + half, 258).rearrange("p (h d) -> p h d", d=129)
                O('act', 'activation', [('ps', 3 + half)], ['den'], out=den[:, half * 2:half * 2 + 2], in_=wv[:, :, 128], func=AF.Abs)
                O('dve', 'tensor_scalar', ['den'], ['den'], out=den[:, half * 2:half * 2 + 2], in0=den[:, half * 2:half * 2 + 2], scalar1=1.0, scalar2=None,
                  op0=ALU.max)
                O('dve', 'reciprocal', ['den'], ['den'], out=den[:, half * 2:half * 2 + 2], in_=den[:, half * 2:half * 2 + 2])
                O('dve', 'tensor_tensor', [('ps', 3 + half), 'den'], [hk], out=ho[:, half * 256:(half + 1) * 256].rearrange("p (h d) -> p h d", d=128),
                  in0=wv[:, :, 0:128], in1=den[:, half * 2:half * 2 + 2].unsqueeze(2).broadcast_to([128, 2, 128]), op=ALU.mult)
            for h in range(4):
                vb_ = 5 + h // 2
                O('pe', 'matmul', ['Kt', 'Vb'], [('ps', vb_)], out=bank(vb_, 129, (h % 2) * 129), lhsT=Kt[:, h, :], rhs=Vb[:, t, h * 129:(h + 1) * 129],
                  start=True, stop=True)
            for h in range(4):
                vb_ = 5 + h // 2
                O('dve', 'scalar_tensor_tensor', [('Cst', d_), 'Ebc', ('ps', vb_)], [('Cst', d_)], out=Cst[d_][:, h, :], in0=Cst[d_][:, h, :],
                  scalar=Ebc[:, h, last:last + 1], in1=bank(vb_, 129, (h % 2) * 129), op0=ALU.mult, op1=ALU.add)
            O('act', 'copy', [('Cst', d_)], [('Cbf', d_)], out=Cbf[d_], in_=Cst[d_])
            rows = slice(t * 128, (t + 1) * 128)
            DMA('pool', (hf_s if d_ == 0 else hb_s)[rows, :], ho, [hk], [('hfb_s', d_, t)])
    for t in range(NT):
        p = t % 2
        rows = slice(t * 128, (t + 1) * 128)
        ho = hout[p]
        hk = ('hout', p)
        DMA('sp', ho, hb_s[rows, :], [('hfb_s', 1, t)], [hk])
        DMA('sp', hfl[p], hf_s[rows, :], [('hfb_s', 0, t)], [('hfl', p)])
        DMA('sp', obl[p], ob_s[rows, :], [], [('obl', p)])
        O('dve', 'tensor_tensor', [hk, ('hfl', p)], [hk], out=ho, in0=ho, in1=hfl[p], op=ALU.add)
        for h in range(4):
            O('act', 'activation', [hk], ['msq', 'mss'], out=msq[:, h * 128:(h + 1) * 128], in_=ho[:, h * 128:(h + 1) * 128], func=AF.Square,
              accum_out=mss[:, h:h + 1])
        O('act', 'activation', ['mss', 'epsb'], ['mss'], out=mss, in_=mss, func=AF.Ln, scale=1.0 / 128, bias=epsb)
        O('act', 'activation', ['mss'], ['mss'], out=mss, in_=mss, func=AF.Exp, scale=-0.5)
        O('dve', 'tensor_tensor', [hk, 'mss'], ['msq'], out=msq.rearrange("p (h d) -> p h d", d=128), in0=ho.rearrange("p (h d) -> p h d", d=128),
          in1=mss.unsqueeze(2).broadcast_to([128, 4, 128]), op=ALU.mult)
        O('dve', 'tensor_tensor', ['msq', 'hn'], ['msq'], out=msq, in0=msq, in1=hn, op=ALU.mult)
        O('dve', 'tensor_tensor', ['msq', ('obl', p)], [('mlo', p)], out=mlo[p], in0=msq, in1=obl[p], op=ALU.mult)
        DMA('pool', ml_s[rows, :], mlo[p], [('mlo', p)], [])
    P.barrier()
    A.reset()
    if stop_after == 'D':
        P.finish()
        return nc

    def outproj_mlp(layer, wo_dram, cat_srcs, tiles, xsrc, dst):
        wo = A.alloc([8, D], BF16)
        wst = [A.alloc([8, 512], F32) for _ in range(2)]
        for n in range(2):
            DMA('sp', wst[n], wo_dram.rearrange("(k p) n -> p k n", p=128)[:, :, n * 512:(n + 1) * 512], [], [('wst', n)])
            O('pool', 'tensor_copy', [('wst', n)], ['wo'], out=wo[:, :, n * 512:(n + 1) * 512], in_=wst[n])
        gta = A.alloc([D], F32)
        cat = [A.alloc([D], BF16) for _ in range(2)]
        catT = A.alloc([8, 128], BF16)
        xts = [A.alloc([D], F32) for _ in range(2)]
        x1o = [A.alloc([D], F32) for _ in range(2)]
        tmp = A.alloc([D], F32)
        cur_seg = None
        for ti, t in enumerate(tiles):
            p = ti % 2
            seg = 1 if t < 2 else 0
            if seg != cur_seg:
                load_mod(gta, layer, seg, 2, 'gta')
                cur_seg = seg
            rows = slice(t * 128, (t + 1) * 128)
            c0 = 0
            for (src, wd) in cat_srcs:
                DMA('sp', cat[p][:, c0:c0 + wd], src[rows, :], [], [('cat', p)])
                c0 += wd
            DMA('sp', xts[p], xsrc(t), [], [('xt', p)])
            transpose8(cat[p], ('cat', p), catT, 'catT', 7, ('ps', 7))
            for n in range(2):
                for k in range(8):
                    O('pe', 'matmul', ['catT', 'wo'], [('ps', 2 * p + n)], out=bank(2 * p + n), lhsT=catT[:, k, :], rhs=wo[:, k, n * 512:(n + 1) * 512],
                      start=(k == 0), stop=(k == 7))
            O('dve', 'tensor_tensor', [('ps', 2 * p), ('ps', 2 * p + 1), 'gta'], ['tmp'], out=tmp, in0=psum[:, p * 1024:(p + 1) * 1024], in1=gta, op=ALU.mult)
            O('dve', 'tensor_tensor', ['tmp', ('xt', p)], [('x1o', p)], out=x1o[p], in0=tmp, in1=xts[p], op=ALU.add)
            DMA('pool', x1_s[rows, :], x1o[p], [('x1o', p)], [])
        P.barrier()
        A.reset()
        w1 = A.alloc([8, 4 * D], BF16)
        w2 = A.alloc([32, D], BF16)
        off0 = A.off
        wst = [A.alloc([8, 512], F32) for _ in range(2)]
        i = 0
        for n in range(8):
            DMA('sp', wst[i % 2], mlp_w1[layer].rearrange("(k p) n -> p k n", p=128)[:, :, n * 512:(n + 1) * 512], [], [('wst', i % 2)])
            O('pool', 'tensor_copy', [('wst', i % 2)], ['w1'], out=w1[:, :, n * 512:(n + 1) * 512], in_=wst[i % 2])
            i += 1
        for n in range(8):
            wv_ = wst[i % 2].rearrange("p a b -> p (a b)").rearrange("p (x c) -> p x c", c=1024)
            DMA('sp', wv_, mlp_w2[layer].rearrange("(k p) n -> p k n", p=128)[:, n * 4:(n + 1) * 4, :], [], [('wst', i % 2)])
            O('pool', 'tensor_copy', [('wst', i % 2)], ['w2'], out=w2[:, n * 4:(n + 1) * 4, :], in_=wv_)
            i += 1
        P.barrier()
        A.off = off0
        mods = [A.alloc([D], F32) for _ in range(3)]
        xts = [A.alloc([D], F32) for _ in range(2)]
        tmp = A.alloc([D], F32)
        hb = A.alloc([D], BF16)
        xmT = A.alloc([8, 128], BF16)
        r1 = [A.alloc([128], F32) for _ in range(2)]
        h1T = A.alloc([32, 128], BF16)
        ss = A.alloc([1], F32)
        cur_seg = None
        for ti, t in enumerate(tiles):
            p = ti % 2
            seg = 1 if t < 2 else 0
            if seg != cur_seg:
                for mi, idx in enumerate((3, 4, 5)):
                    load_mod(mods[mi], layer, seg, idx, ('mod', mi))
                cur_seg = seg
            rows = slice(t * 128, (t + 1) * 128)
            x1t = xts[p]
            x1k = ('xt', p)
            DMA('sp', x1t, x1_s[rows, :], [], [x1k])
            rms_modulate(x1t, x1k, mods[1], ('mod', 1), mods[0], ('mod', 0), hb, 'hb', tmp, 'tmp', ss, 'ss', tmp, 'tmp')
            transpose8(hb, 'hb', xmT, 'xmT', 7, ('ps', 7), evac='dve')
            for f in range(32):
                b = 2 + f % 4
                for k in range(8):
                    O('pe', 'matmul', ['xmT', 'w1'], [('ps', b)], out=bank(b, 128), lhsT=w1[:, k, f * 128:(f + 1) * 128], rhs=xmT[:, k, :],
                      start=(k == 0), stop=(k == 7))
                O('act', 'activation', [('ps', b)], [('r1', f % 2)], out=r1[f % 2], in_=bank(b, 128), func=AF.Relu)
                O('dve', 'tensor_tensor', [('r1', f % 2)], [('h1T', f)], out=h1T[:, f, :], in0=r1[f % 2], in1=r1[f % 2], op=ALU.mult)
            h1k = [('h1T', f) for f in range(32)]
            for n in range(2):
                for k in range(32):
                    O('pe', 'matmul', h1k + ['w2'], [('ps', n)], out=bank(n), lhsT=h1T[:, k, :], rhs=w2[:, k, n * 512:(n + 1) * 512],
                      start=(k == 0), stop=(k == 31))
            O('dve', 'tensor_tensor', [('ps', 0), ('ps', 1), ('mod', 2)], ['tmp'], out=tmp, in0=psum[:, 0:1024], in1=mods[2], op=ALU.mult)
            O('dve', 'tensor_tensor', ['tmp', x1k], [x1k], out=x1t, in0=tmp, in1=x1t, op=ALU.add)
            DMA('pool', dst(t), x1t, [x1k], [])
        P.barrier()
        A.reset()

    outproj_mlp(0, ab_w_out, [(na_s, 512), (ml_s, 512)], list(range(NT)), xrows, lambda t: xs1[t * 128:(t + 1) * 128, :])
    if stop_after == 'E':
        P.finish()
        return nc

    scs = [A.alloc([D], F32) for _ in range(2)]
    shs = [A.alloc([D], F32) for _ in range(2)]
    for seg in range(2):
        load_mod(scs[seg], 1, seg, 1, ('sc', seg))
        load_mod(shs[seg], 1, seg, 0, ('sh', seg))
    xts = [A.alloc([D], F32) for _ in range(2)]
    hos = [A.alloc([D], F32) for _ in range(2)]
    tmp = A.alloc([D], F32)
    ss = A.alloc([1], F32)
    for t in range(NT):
        p = t % 2
        seg = 1 if t < 2 else 0
        rows = slice(t * 128, (t + 1) * 128)
        DMA('sp', xts[p], xs1[rows, :], [], [('xt', p)])
        rms_modulate(xts[p], ('xt', p), scs[seg], ('sc', seg), shs[seg], ('sh', seg), hos[p], ('ho', p), tmp, 'tmp', ss, 'ss', tmp, 'tmp')
        DMA('pool', hbuf[rows, :], hos[p], [('ho', p)], [])
    P.barrier()
    A.reset()

    wrkv = A.alloc([3, 8, D], BF16)
    w1c = A.alloc([8, 128], BF16)
    a1c = A.alloc([8, 128], BF16)
    g1 = A.alloc([8, 160], BF16)
    w2c = A.alloc([D], BF16)
    a2c = A.alloc([D], BF16)
    g2 = A.alloc([2, D], BF16)
    vecs = [A.alloc([D], F32) for _ in range(7)]
    muT = A.alloc([6, 8], F32)
    omka = A.alloc([D], F32)
    off0 = A.off
    wst = [A.alloc([8, 512], F32) for _ in range(2)]
    i = 0
    for j in range(3):
        for n in range(2):
            DMA('sp', wst[i % 2], rw_w_rkv[j].rearrange("(k p) n -> p k n", p=128)[:, :, n * 512:(n + 1) * 512], [], [('wst', i % 2)])
            O('pool', 'tensor_copy', [('wst', i % 2)], ['wrkv'], out=wrkv[:, j, :, n * 512:(n + 1) * 512], in_=wst[i % 2])
            i += 1
    for (src, dst_, wd, key) in ((rw_w1c, w1c, 128, 'w1c'), (rw_a1c, a1c, 128, 'a1c'), (rw_g1, g1, 160, 'g1')):
        DMA('sp', wst[i % 2][:, :, 0:wd], src.rearrange("(k p) n -> p k n", p=128), [], [('wst', i % 2)])
        O('pool', 'tensor_copy', [('wst', i % 2)], [key], out=dst_, in_=wst[i % 2][:, :, 0:wd])
        i += 1
    for (src, dst_, key) in ((rw_w2c, w2c, 'w2c'), (rw_a2c, a2c, 'a2c')):
        wv_ = wst[i % 2].rearrange("p a b -> p (a b)")[:, 0:D]
        DMA('sp', wv_, src, [], [('wst', i % 2)])
        O('pool', 'tensor_copy', [('wst', i % 2)], [key], out=dst_, in_=wv_)
        i += 1
    wv_ = wst[i % 2].rearrange("p a b -> p (a b)")[:, 0:2 * D].rearrange("p (a b) -> p a b", b=D)
    DMA('sp', wv_[:, 0, :], rw_g2[0:128, :], [], [('wst', i % 2)])
    DMA('sp', wv_[0:32, 1, :], rw_g2[128:160, :], [], [('wst', i % 2)])
    O('pool', 'tensor_copy', [('wst', i % 2)], ['g2'], out=g2[:, 0, :], in_=wv_[:, 0, :])
    O('pool', 'tensor_copy', [('wst', i % 2)], ['g2'], out=g2[0:32, 1, :], in_=wv_[0:32, 1, :])
    for j in range(7):
        DMA('pool', vecs[j], rw_vecs[j:j + 1, :].partition_broadcast(128), [], [('vec', j)])
    DMA('sp', muT, rw_muT, [], ['muT'])
    O('dve', 'tensor_scalar', [('vec', 5)], ['omka'], out=omka, in0=vecs[5], scalar1=-1.0, scalar2=1.0, op0=ALU.mult, op1=ALU.add)
    P.barrier()
    A.off = off0
    w0b, a0b, kkv, kav, rkv = vecs[0:2], vecs[2:4], vecs[4], vecs[5], vecs[6]
    hc = A.alloc([D], F32)
    hp_ = A.alloc([D], F32)
    hn_ = A.alloc([D], F32)
    hcT = A.alloc([8, 128], F32)
    xsT = [A.alloc([8, 128], BF16) for _ in range(6)]
    thT = A.alloc([128], BF16)
    alT = A.alloc([128], BF16)
    sgT = A.alloc([2, 128], BF16)
    rf = A.alloc([D], F32)
    kf = A.alloc([D], F32)
    vf = A.alloc([D], F32)
    asig = [A.alloc([D], F32) for _ in range(2)]
    kkt = A.alloc([D], F32)
    t1 = A.alloc([D], F32)
    t2 = A.alloc([D], F32)
    hs16 = A.alloc([16], F32)
    ob16 = [A.alloc([D], BF16) for _ in range(4)]
    of32 = [A.alloc([D], F32) for _ in range(2)]
    nb16 = [0]
    nf32 = [0]

    def out16():
        nb16[0] += 1
        j = nb16[0] % 4
        return ob16[j], ('ob16', j)

    def outf32():
        nf32[0] += 1
        j = nf32[0] % 2
        return of32[j], ('of32', j)

    NEH = -float(np.exp(-0.5))
    for t in range(NT):
        r0 = t * 128
        rows = slice(r0, r0 + 128)
        first = t in (0, 2)
        lastt = t in (1, NT - 1)
        DMA('sp', hc, hbuf[rows, :], [], ['hc'])
        if first:
            O('pool', 'memset', [], ['hp'], ap=hp_, constant=0.0)
            DMA('sp', hp_[1:128, :], hbuf[r0:r0 + 127, :], [], ['hp'])
        else:
            DMA('sp', hp_, hbuf[r0 - 1:r0 + 127, :], [], ['hp'])
        if lastt:
            O('pool', 'memset', [], ['hn'], ap=hn_, constant=0.0)
            DMA('sp', hn_[0:127, :], hbuf[r0 + 1:r0 + 128, :], [], ['hn'])
        else:
            DMA('sp', hn_, hbuf[r0 + 1:r0 + 129, :], [], ['hn'])
        O('dve', 'tensor_tensor', ['hp', 'hn'], ['hp'], out=hp_, in0=hp_, in1=hn_, op=ALU.add)
        O('dve', 'scalar_tensor_tensor', ['hp', 'hc'], ['hp'], out=hp_, in0=hp_, scalar=0.5, in1=hc, op0=ALU.mult, op1=ALU.subtract)
        pvh = psum[:, 0:1024].rearrange("p (c t) -> p c t", t=128)
        pvx = psum[:, 1024:2048].rearrange("p (c t) -> p c t", t=128)
        for c in range(8):
            O('pe', 'transpose', ['hc', 'identf'], [('ps', c // 4)], out=pvh[:, c, :], in_=hc[:, c * 128:(c + 1) * 128], identity=identf)
        for c in range(8):
            O('pe', 'transpose', ['hp', 'identf'], [('ps', 2 + c // 4)], out=pvx[:, c, :], in_=hp_[:, c * 128:(c + 1) * 128], identity=identf)
        O('act', 'copy', [('ps', 0), ('ps', 1)], ['hcT'], out=hcT, in_=pvh)
        for sidx in range(6):
            for c in range(8):
                O('dve' if (sidx * 8 + c) % 3 else 'pool' if False else 'dve', 'scalar_tensor_tensor', [('ps', 2), ('ps', 3), 'hcT', 'muT'], [('xsT', sidx)],
                  out=xsT[sidx][:, c, :], in0=pvx[:, c, :], scalar=muT[:, sidx, c:c + 1], in1=hcT[:, c, :], op0=ALU.mult, op1=ALU.add)
        xr, xw, xk, xv, xa, xg = xsT
        xrk, xwk, xkk, xvk, xak, xgk = [('xsT', j) for j in range(6)]
        for j, (xs_, xsk, dstf, dk) in enumerate(((xr, xrk, rf, 'rf'), (xk, xkk, kf, 'kf'), (xv, xvk, vf, 'vf'))):
            for n in range(2):
                b = 4 + n
                for k in range(8):
                    O('pe', 'matmul', [xsk, 'wrkv'], [('ps', b)], out=bank(b), lhsT=xs_[:, k, :], rhs=wrkv[:, j, k, n * 512:(n + 1) * 512],
                      start=(k == 0), stop=(k == 7))
            O('act', 'copy', [('ps', 4), ('ps', 5)], [dk], out=dstf, in_=psum[:, 2048:3072])
        ro, rok = out16()
        O('pool', 'tensor_copy', ['rf'], [rok], out=ro, in_=rf)
        DMA('pool', r_s[rows, :], ro, [rok], [])
        vo, vok = out16()
        O('pool', 'tensor_copy', ['vf'], [vok], out=vo, in_=vf)
        DMA('pool', v_s[rows, :], vo, [vok], [])
        for k in range(8):
            O('pe', 'matmul', [xwk, 'w1c'], [('ps', 6)], out=bank(6, 128), lhsT=w1c[:, k, :], rhs=xw[:, k, :], start=(k == 0), stop=(k == 7))
        O('act', 'activation', [('ps', 6)], ['thT'], out=thT, in_=bank(6, 128), func=AF.Tanh)
        for k in range(8):
            O('pe', 'matmul', [xak, 'a1c'], [('ps', 6)], out=bank(6, 128, 128), lhsT=a1c[:, k, :], rhs=xa[:, k, :], start=(k == 0), stop=(k == 7))
        O('act', 'copy', [('ps', 6)], ['alT'], out=alT, in_=bank(6, 128, 128))
        for z in range(2):
            for n in range(2):
                O('pe', 'matmul', ['thT', 'w2c'], [('ps', 4 + n)], out=bank(4 + n), lhsT=thT[z * 64:(z + 1) * 64, :], rhs=w2c[z * 64:(z + 1) * 64, n * 512:(n + 1) * 512],
                  start=True, stop=True)
            O('dve', 'tensor_tensor', [('ps', 4), ('ps', 5), ('vec', z)], ['t1'], out=t1, in0=psum[:, 2048:3072], in1=w0b[z], op=ALU.add)
            O('act', 'activation', ['t1'], ['t1'], out=t1, in_=t1, func=AF.Sigmoid)
            lo, lok = outf32()
            O('pool', 'tensor_scalar', ['t1'], [lok], out=lo, in0=t1, scalar1=NEH, scalar2=None, op0=ALU.mult)
            DMA('pool', lw_s[z, rows, :], lo, [lok], [])
        for z in range(2):
            for n in range(2):
                O('pe', 'matmul', ['alT', 'a2c'], [('ps', 4 + n)], out=bank(4 + n), lhsT=alT[z * 64:(z + 1) * 64, :], rhs=a2c[z * 64:(z + 1) * 64, n * 512:(n + 1) * 512],
                  start=True, stop=True)
            O('dve', 'tensor_tensor', [('ps', 4), ('ps', 5), ('vec', 2 + z)], [('asig', z)], out=asig[z], in0=psum[:, 2048:3072], in1=a0b[z], op=ALU.add)
            O('act', 'activation', [('asig', z)], [('asig', z)], out=asig[z], in_=asig[z], func=AF.Sigmoid)
        if t >= 2:
            for k in range(8):
                O('pe', 'matmul', [xgk, 'g1'], [('ps', 7)], out=bank(7, 128), lhsT=g1[:, k, 0:128], rhs=xg[:, k, :], start=(k == 0), stop=(k == 7))
            for k in range(8):
                O('pe', 'matmul', [xgk, 'g1'], [('ps', 7)], out=bank(7, 128, 128)[0:32, :], lhsT=g1[:, k, 128:160], rhs=xg[:, k, :], start=(k == 0), stop=(k == 7))
            O('act', 'activation', [('ps', 7)], ['sgT'], out=sgT[:, 0, :], in_=bank(7, 128), func=AF.Sigmoid)
            O('act', 'activation', [('ps', 7)], ['sgT'], out=sgT[0:32, 1, :], in_=bank(7, 128, 128)[0:32, :], func=AF.Sigmoid)
            for n in range(2):
                O('pe', 'matmul', ['sgT', 'g2'], [('ps', 4 + n)], out=bank(4 + n), lhsT=sgT[:, 0, :], rhs=g2[:, 0, n * 512:(n + 1) * 512], start=True, stop=False)
                O('pe', 'matmul', ['sgT', 'g2'], [('ps', 4 + n)], out=bank(4 + n), lhsT=sgT[0:32, 1, :], rhs=g2[0:32, 1, n * 512:(n + 1) * 512], start=False, stop=True)
            go_, gok = out16()
            O('act', 'copy', [('ps', 4), ('ps', 5)], [gok], out=go_, in_=psum[:, 2048:3072])
            DMA('pool', g_s[rows, :], go_, [gok], [])
        h3 = lambda ap: ap.rearrange("p (h d) -> p h d", d=64)
        O('dve', 'tensor_tensor', ['kf', ('vec', 4)], ['kkt'], out=kkt, in0=kf, in1=kkv, op=ALU.mult)
        O('dve', 'tensor_tensor', ['kkt'], ['t1'], out=t1, in0=kkt, in1=kkt, op=ALU.mult)
        O('dve', 'tensor_reduce', ['t1'], ['hs16'], out=hs16, in_=h3(t1), axis=AX.X, op=ALU.add)
        O('dve', 'tensor_scalar', ['hs16'], ['hs16'], out=hs16, in0=hs16, scalar1=1e-24, scalar2=None, op0=ALU.max)
        O('act', 'activation', ['hs16'], ['hs16'], out=hs16, in_=hs16, func=AF.Ln)
        O('act', 'activation', ['hs16'], ['hs16'], out=hs16, in_=hs16, func=AF.Exp, scale=-0.5)
        O('dve', 'tensor_tensor', ['kkt', 'hs16'], ['kkt'], out=h3(kkt), in0=h3(kkt), in1=hs16.unsqueeze(2).broadcast_to([128, 16, 64]), op=ALU.mult)
        ao, aok = out16()
        O('pool', 'tensor_scalar', ['kkt'], [aok], out=ao, in0=kkt, scalar1=-1.0, scalar2=None, op0=ALU.mult)
        DMA('pool', a_s[rows, :], ao, [aok], [])
        for z in range(2):
            bo, bok = out16()
            O('dve', 'tensor_tensor', ['kkt', ('asig', z)], [bok], out=bo, in0=kkt, in1=asig[z], op=ALU.mult)
            DMA('pool', bb_s[z, rows, :], bo, [bok], [])
        for z in range(2):
            O('dve', 'tensor_tensor', [('asig', z), ('vec', 5)], [('asig', z)], out=asig[z], in0=asig[z], in1=kav, op=ALU.mult)
            O('dve', 'tensor_tensor', [('asig', z), 'omka'], [('asig', z)], out=asig[z], in0=asig[z], in1=omka, op=ALU.add)
            O('dve', 'tensor_tensor', [('asig', z), 'kf'], [('asig', z)], out=asig[z], in0=asig[z], in1=kf, op=ALU.mult)
            ko_, kok = out16()
            O('pool', 'tensor_copy', [('asig', z)], [kok], out=ko_, in_=asig[z])
            DMA('pool', kd_s[z, rows, :], ko_, [kok], [])
        if t >= 2:
            O('dve', 'tensor_tensor', [('asig', 0), ('asig', 1)], ['t2'], out=t2, in0=asig[0], in1=asig[1], op=ALU.add)
            O('dve', 'tensor_tensor', ['rf', ('vec', 6)], ['t1'], out=t1, in0=rf, in1=rkv, op=ALU.mult)
            O('dve', 'tensor_tensor', ['t1', 't2'], ['t1'], out=t1, in0=t1, in1=t2, op=ALU.mult)
            O('dve', 'tensor_reduce', ['t1'], ['hs16'], out=hs16, in_=h3(t1), axis=AX.X, op=ALU.add)
            bo_, bok_ = outf32()
            O('dve', 'tensor_tensor', ['vf', 'hs16'], [bok_], out=h3(bo_), in0=h3(vf), in1=hs16.unsqueeze(2).broadcast_to([128, 16, 64]), op=ALU.mult)
            DMA('pool', bon_s[rows, :], bo_, [bok_], [])
    P.barrier()
    A.reset()
    if stop_after == 'F':
        P.finish()
        return nc

    rm = A.alloc([4, 128], F32)
    DMA('sp', rm, rwmask.rearrange("a p f -> p a f"), [], ['rm'])
    onesf = A.alloc([128], F32)
    O('dve', 'memset', [], ['onesf'], ap=onesf, constant=1.0)
    pm = A.alloc([2], F32)
    DMA('sp', pm, pmask_in, [], ['pm'])
    tP = {n_: [A.alloc([8, 128], BF16) for _ in range(2)] for n_ in ('at', 'rt', 'bt')}
    lw = A.alloc([D], F32)
    ain = A.alloc([D], BF16)
    rin = A.alloc([D], BF16)
    bin_ = A.alloc([D], BF16)
    kin = A.alloc([D], BF16)
    vin = A.alloc([D], BF16)
    cumS = A.alloc([D], F32)
    E1 = A.alloc([D], F32)
    E2 = A.alloc([D], F32)
    E3 = A.alloc([D], F32)
    Ee = A.alloc([D], F32)
    gL = A.alloc([8], F32)
    sc6 = {n_: A.alloc([D], BF16) for n_ in ('at', 'rt', 'bt', 'kt', 'bh', 'kh')}
    tT = {n_: A.alloc([8, 128], BF16) for n_ in ('at', 'rt', 'bt', 'kt')}
    Db = [[A.alloc([4, 128], BF16) for _ in range(2)] for _ in range(4)]
    DTb = [[A.alloc([4, 128], BF16) for _ in range(2)] for _ in range(4)]
    NT0 = [A.alloc([4, 128], BF16) for _ in range(4)]
    NoffT = [A.alloc([4, 128], BF16) for _ in range(4)]
    Z1sb = [A.alloc([4, 128], BF16) for _ in range(4)]
    Wsb = [A.alloc([4, 128], BF16) for _ in range(4)]
    Iden4 = A.alloc([4, 128], BF16)
    O('dve', 'tensor_copy', ['identb'], ['Iden4'], out=Iden4, in_=identb.unsqueeze(1).broadcast_to([128, 4, 128]))
    lm = A.alloc([7, 128], BF16)
    lmst = A.alloc([7, 128], F32)
    DMA('sp', lmst, lvlmask.rearrange("a p f -> p a f"), [], ['lmst'])
    O('dve', 'tensor_copy', ['lmst'], ['lm'], out=lm, in_=lmst)
    X = A.alloc([16, 128], BF16)
    AakT = A.alloc([16, 128], BF16)
    ArbT = A.alloc([16, 128], BF16)
    ArkT = A.alloc([16, 128], BF16)
    Wb = A.alloc([16, 64], BF16)
    Ub = A.alloc([16, 64], BF16)
    yt = A.alloc([D], F32)
    Hst = [A.alloc([8, 64], F32) for _ in range(2)]
    Hbf = [A.alloc([8, 64], BF16) for _ in range(2)]
    tmpH = A.alloc([8, 64], F32)
    for z in range(2):
        O('dve', 'memset', [], [('Hst', z)], ap=Hst[z], constant=0.0)
        O('dve', 'memset', [], [('Hbf', z)], ap=Hbf[z], constant=0.0)
    bsel = [0]

    def nb_():
        bsel[0] = (bsel[0] + 1) % 4
        return 4 + bsel[0]

    def hv(b):
        return bank(b).rearrange("p (h t) -> p h t", t=128)

    bsel8 = [0]

    def nb8_():
        bsel8[0] = (bsel8[0] + 1) % 8
        return bsel8[0]

    for step in range(GSTEPS):
        for z in range(2):
            t = orders[z][step]
            rows = slice(t * 128, (t + 1) * 128)
            tri = rm[:, z, :]
            strict = rm[:, 2 + z, :]
            strictT = rm[:, 3 - z, :]
            DMA('sp', lw, lw_s[z, rows, :], [], ['lw'])
            DMA('sp', ain, a_s[rows, :], [], ['ain'])
            DMA('sp', rin, r_s[rows, :], [], ['rin'])
            DMA('sp', bin_, bb_s[z, rows, :], [], ['bin'])
            DMA('sp', kin, kd_s[z, rows, :], [], ['kin'])
            DMA('sp', vin, v_s[rows, :], [], ['vin'])
            for n in range(2):
                O('pe', 'matmul', ['lw', 'rm'], [('ps', n)], out=bank(n), lhsT=tri, rhs=lw[:, n * 512:(n + 1) * 512], start=True, stop=True)
                O('pe', 'matmul', ['lw', 'onesf'], [('ps', 2 + n)], out=bank(2 + n), lhsT=onesf, rhs=lw[:, n * 512:(n + 1) * 512], start=True, stop=True)
            for c8 in range(8):
                O('pe', 'matmul', ['lw', 'onesf'], [('ps', 4)], out=bank(4, 4, 4 * c8), lhsT=lw[:, c8 * 128:(c8 + 1) * 128], rhs=onesf[:, 0:4], start=True, stop=True)
            O('act', 'copy', [('ps', 0), ('ps', 1)], ['cumS'], out=cumS, in_=psum[:, 0:1024])
            O('act', 'activation', [('ps', 0), ('ps', 1)], ['E1'], out=E1, in_=psum[:, 0:1024], func=AF.Exp)
            O('dve', 'tensor_tensor', [('ps', 2), ('ps', 3), 'cumS'], ['E3'], out=E3, in0=psum[:, 1024:2048], in1=cumS, op=ALU.subtract)
            O('act', 'activation', ['E3'], ['E3'], out=E3, in_=E3, func=AF.Exp)
            O('dve', 'tensor_tensor', ['cumS', 'lw'], ['Ee'], out=Ee, in0=cumS, in1=lw, op=ALU.subtract)
            O('act', 'activation', ['Ee'], ['Ee'], out=Ee, in_=Ee, func=AF.Exp)
            O('act', 'activation', ['cumS'], ['E2'], out=E2, in_=cumS, func=AF.Exp, scale=-1.0)
            O('act', 'activation', [('ps', 4)], ['gL'], out=gL, in_=bank(4, 32).rearrange("p (c x) -> p c x", x=4)[:, :, 0], func=AF.Exp)
            if GLEVEL < 2:
                continue
            O('dve', 'tensor_tensor', ['ain', 'Ee'], ['at'], out=sc6['at'], in0=ain, in1=Ee, op=ALU.mult)
            O('dve', 'tensor_tensor', ['rin', 'E1'], ['rt'], out=sc6['rt'], in0=rin, in1=E1, op=ALU.mult)
            O('dve', 'tensor_tensor', ['bin', 'E2'], ['bt'], out=sc6['bt'], in0=bin_, in1=E2, op=ALU.mult)
            O('dve', 'tensor_tensor', ['kin', 'E2'], ['kt'], out=sc6['kt'], in0=kin, in1=E2, op=ALU.mult)
            O('dve', 'tensor_tensor', ['bin', 'E3'], ['bh'], out=sc6['bh'], in0=bin_, in1=E3, op=ALU.mult)
            O('dve', 'tensor_tensor', ['kin', 'E3'], ['kh'], out=sc6['kh'], in0=kin, in1=E3, op=ALU.mult)
            for bi, n_ in enumerate(('at', 'rt', 'bt', 'kt')):
                transpose8(sc6[n_], n_, tT[n_], n_ + 'T', bi, ('ps', bi), evac=('act' if bi % 2 == 0 else 'dve'))
            atT, rtT, btT, ktT = tT['at'], tT['rt'], tT['bt'], tT['kt']
            for n_ in ('at', 'rt', 'bt'):
                for par in range(2):
                    O('dve' if par == 0 else 'pool', 'tensor_scalar', [n_ + 'T', 'pm'], [(n_ + 'P', par)], out=tP[n_][par], in0=tT[n_],
                      scalar1=pm[:, par:par + 1], scalar2=None, op0=ALU.mult)

            def hsl(T_, h):
                return T_[(h % 2) * 64:(h % 2) * 64 + 64, h // 2, :]

            if GLEVEL < 3:
                continue
            for hg in range(4):
                for (dst_, dk, L_, Lk, R_, Rk, msk) in ((AakT, 'AakT', ktT, 'ktT', 'at', 'atP', strict), (ArbT, 'ArbT', btT, 'btT', 'rt', 'rtP', tri),
                                                      (ArkT, 'ArkT', ktT, 'ktT', 'rt', 'rtP', tri)):
                    b = nb_()
                    for hh in range(4):
                        h = hg * 4 + hh
                        if GTEST == 1 and h % 2 == 1:
                            continue
                        if GTEST == 2 and h % 2 == 0:
                            continue
                        O('pe', 'matmul', [Lk, (Rk, h % 2)], [('ps', b)], out=bank(b, 128, hh * 128), lhsT=L_[:, h // 2, :], rhs=tP[R_][h % 2][:, h // 2, :],
                          start=True, stop=True)
                    if GLEVEL >= 3.1:
                        O('dve', 'tensor_tensor', [('ps', b), 'rm'], [(dk, hg)], out=dst_[:, hg * 4:hg * 4 + 4, :], in0=hv(b),
                          in1=msk.unsqueeze(1).broadcast_to([128, 4, 128]), op=ALU.mult)
                b = nb_()
                for hh in range(4):
                    h = hg * 4 + hh
                    O('pe', 'matmul', [('btP', h % 2), 'atT'], [('ps', b)], out=bank(b, 128, hh * 128), lhsT=atT[:, h // 2, :], rhs=tP['bt'][h % 2][:, h // 2, :],
                      start=True, stop=True)
                O('dve', 'tensor_tensor', [('ps', b), 'rm'], [('NT0', hg)], out=NT0[hg], in0=hv(b), in1=strictT.unsqueeze(1).broadcast_to([128, 4, 128]), op=ALU.mult)
            for lv in range(7):
                cur, nxt = lv % 2, (lv + 1) % 2
                for hg in range(4):
                    Dc, Dck = (Iden4, 'Iden4') if lv == 0 else (Db[hg][cur], ('D', hg, cur))
                    DTc, DTck = (Iden4, 'Iden4') if lv == 0 else (DTb[hg][cur], ('DT', hg, cur))
                    O('dve', 'tensor_tensor', [('NT0', hg), 'lm'], [('NoffT', hg)], out=NoffT[hg], in0=NT0[hg], in1=lm[:, lv, :].unsqueeze(1).broadcast_to([128, 4, 128]),
                      op=ALU.mult)
                    b1 = nb8_()
                    for hh in range(4):
                        O('pe', 'matmul', [('NoffT', hg), Dck], [('ps', b1)], out=bank(b1, 128, hh * 128), lhsT=NoffT[hg][:, hh, :], rhs=Dc[:, hh, :],
                          start=True, stop=True)
                    O('act', 'copy', [('ps', b1)], [('Z1sb', hg)], out=Z1sb[hg], in_=hv(b1))
                    b2 = nb8_()
                    for hh in range(4):
                        O('pe', 'matmul', [('Z1sb', hg), DTck], [('ps', b2)], out=bank(b2, 128, hh * 128), lhsT=DTc[:, hh, :], rhs=Z1sb[hg][:, hh, :],
                          start=True, stop=True)
                    if lv < 6:
                        O('act', 'copy', [('ps', b2)], [('Wsb', hg)], out=Wsb[hg], in_=hv(b2))
                        O('dve', 'tensor_tensor', [('ps', b2), Dck], [('D', hg, nxt)], out=Db[hg][nxt], in0=hv(b2), in1=Dc, op=ALU.add)
                        b3 = nb8_()
                        pv3 = bank_bf(b3).rearrange("p (h t) -> p h t", t=128)[:, 0:4, :]
                        for hh in range(4):
                            O('pe', 'transpose', [('Wsb', hg), 'identb'], [('ps', b3)], out=pv3[:, hh, :], in_=Wsb[hg][:, hh, :], identity=identb)
                        O('dve', 'tensor_tensor', [('ps', b3), DTck], [('DT', hg, nxt)], out=DTb[hg][nxt], in0=pv3, in1=DTc, op=ALU.add)
                    else:
                        O('dve', 'tensor_tensor', [('ps', b2), Dck], [('X', hg)], out=X[:, hg * 4:hg * 4 + 4, :], in0=hv(b2), in1=Dc, op=ALU.add)
            if GLEVEL < 4:
                continue
            Hk = ('Hbf', z)
            Xk = [('X', hg) for hg in range(4)]
            for h in range(16):
                po, c8 = (h % 2) * 64, h // 2
                outp = bank(h // 8, 64, (h % 8) * 64)
                O('pe', 'matmul', [('AakT', h // 4), 'vin'], [('ps', h // 8)], out=outp, lhsT=AakT[:, h, :], rhs=vin[:, h * 64:(h + 1) * 64], start=True, stop=False)
                O('pe', 'matmul', [('atP', h % 2), Hk], [('ps', h // 8)], out=outp, lhsT=tP['at'][h % 2][:, c8, :], rhs=Hbf[z][:, c8, :], start=False, stop=True)
            O('act', 'copy', [('ps', 0), ('ps', 1)], ['Wb'], out=Wb.rearrange("p h d -> p (h d)"), in_=psum[:, 0:1024])
            for h in range(16):
                O('pe', 'matmul', Xk + ['Wb'], [('ps', 2 + h // 8)], out=bank(2 + h // 8, 64, (h % 8) * 64), lhsT=X[:, h, :], rhs=Wb[:, h, :], start=True, stop=True)
            O('dve', 'tensor_copy', [('ps', 2), ('ps', 3)], ['Ub'], out=Ub.rearrange("p h d -> p (h d)"), in_=psum[:, 1024:2048])
            if t >= 2:
                for h in range(16):
                    po, c8 = (h % 2) * 64, h // 2
                    outp = bank(4 + h // 8, 64, (h % 8) * 64)
                    O('pe', 'matmul', [('rtP', h % 2), Hk], [('ps', 4 + h // 8)], out=outp, lhsT=tP['rt'][h % 2][:, c8, :], rhs=Hbf[z][:, c8, :], start=True, stop=False)
                    O('pe', 'matmul', [('ArbT', h // 4), 'Ub'], [('ps', 4 + h // 8)], out=outp, lhsT=ArbT[:, h, :], rhs=Ub[:, h, :], start=False, stop=False)
                    O('pe', 'matmul', [('ArkT', h // 4), 'vin'], [('ps', 4 + h // 8)], out=outp, lhsT=ArkT[:, h, :], rhs=vin[:, h * 64:(h + 1) * 64], start=False, stop=True)
                O('act', 'copy', [('ps', 4), ('ps', 5)], ['yt'], out=yt, in_=psum[:, 2048:3072])
                DMA('pool', y_s[z, rows, :], yt, ['yt'], [])
            for c8 in range(8):
                outp = bank(6 + c8 // 4, 128, (c8 % 4) * 128)
                O('pe', 'matmul', ['bh', 'Ub'], [('ps', 6 + c8 // 4)], out=outp, lhsT=sc6['bh'][:, c8 * 128:(c8 + 1) * 128],
                  rhs=Ub[:, 2 * c8:2 * c8 + 2, :].rearrange("p h d -> p (h d)"), start=True, stop=False)
                O('pe', 'matmul', ['kh', 'vin'], [('ps', 6 + c8 // 4)], out=outp, lhsT=sc6['kh'][:, c8 * 128:(c8 + 1) * 128], rhs=vin[:, c8 * 128:(c8 + 1) * 128],
                  start=False, stop=True)
            pD = psum[:, 3072:4096].rearrange("p (c x) -> p c x", x=128)
            O('dve', 'tensor_tensor', [('Hst', z), 'gL'], ['tmpH'], out=tmpH, in0=Hst[z], in1=gL.unsqueeze(2).broadcast_to([128, 8, 64]), op=ALU.mult)
            O('dve', 'scalar_tensor_tensor', ['tmpH', ('ps', 6), ('ps', 7), 'pm'], ['tmpH'], out=tmpH, in0=pD[:, :, 0:64], scalar=pm[:, 0:1], in1=tmpH,
              op0=ALU.mult, op1=ALU.add)
            O('dve', 'scalar_tensor_tensor', ['tmpH', ('ps', 6), ('ps', 7), 'pm'], [('Hst', z)], out=Hst[z], in0=pD[:, :, 64:128], scalar=pm[:, 1:2], in1=tmpH,
              op0=ALU.mult, op1=ALU.add)
            O('act', 'copy', [('Hst', z)], [Hk], out=Hbf[z], in_=Hst[z])
            if 'dumpG' in debug and z == 0 and step < 3:
                for (nm, ap_, keys, dt_) in (('X', X, Xk, BF16), ('AakT', AakT, [('AakT', q) for q in range(4)], BF16),
                                             ('ArbT', ArbT, [('ArbT', q) for q in range(4)], BF16), ('Wb', Wb, ['Wb'], BF16), ('Ub', Ub, ['Ub'], BF16),
                                             ('Hst', Hst[0], [('Hst', 0)], F32), ('at', sc6['at'], ['at'], BF16), ('atT', atT, ['atT'], BF16),
                                             ('E1', E1, ['E1'], F32), ('gL', gL, ['gL'], F32), ('bh', sc6['bh'], ['bh'], BF16)):
                    shp = list(ap_.shape)
                    dd = nc.dram_tensor("dbg_%s_%d" % (nm, step), shp, dt_, kind="ExternalOutput").ap()
                    DMA('sp', dd, ap_, keys, [])
    P.barrier()
    A.reset()
    if stop_after == 'G':
        P.finish()
        return nc

    lnw = A.alloc([D], F32)
    lnb = A.alloc([D], F32)
    DMA('pool', lnw, rw_vecs[7:8, :].partition_broadcast(128), [], ['lnw'])
    DMA('pool', lnb, rw_vecs[8:9, :].partition_broadcast(128), [], ['lnb'])
    eps2 = A.alloc([1], F32)
    O('dve', 'memset', [], ['eps2'], ap=eps2, constant=64e-5)
    yf = [A.alloc([D], F32) for _ in range(2)]
    yb = [A.alloc([D], F32) for _ in range(2)]
    bon = [A.alloc([D], F32) for _ in range(2)]
    gin = [A.alloc([D], BF16) for _ in range(2)]
    yo = [A.alloc([D], BF16) for _ in range(2)]
    sq = A.alloc([D], F32)
    st = A.alloc([16], F32)
    h3 = lambda ap: ap.rearrange("p (h d) -> p h d", d=64)
    for t in range(2, NT):
        p = t % 2
        rows = slice(t * 128, (t + 1) * 128)
        DMA('sp', yf[p], y_s[0, rows, :], [], [('yf', p)])
        DMA('sp', yb[p], y_s[1, rows, :], [], [('yb', p)])
        DMA('sp', bon[p], bon_s[rows, :], [], [('bon', p)])
        DMA('sp', gin[p], g_s[rows, :], [], [('gin', p)])
        y = yf[p]
        yk = ('yf', p)
        O('dve', 'tensor_tensor', [yk, ('yb', p)], [yk], out=y, in0=y, in1=yb[p], op=ALU.add)
        O('dve', 'tensor_reduce', [yk], ['st'], out=st, in_=h3(y), axis=AX.X, op=ALU.add)
        O('dve', 'tensor_scalar', ['st'], ['st'], out=st, in0=st, scalar1=-1.0 / 64, scalar2=None, op0=ALU.mult)
        O('dve', 'tensor_tensor', [yk, 'st'], [yk], out=h3(y), in0=h3(y), in1=st.unsqueeze(2).broadcast_to([128, 16, 64]), op=ALU.add)
        O('dve', 'tensor_tensor', [yk], ['sq'], out=sq, in0=y, in1=y, op=ALU.mult)
        O('dve', 'tensor_reduce', ['sq'], ['st'], out=st, in_=h3(sq), axis=AX.X, op=ALU.add)
        O('act', 'activation', ['st', 'eps2'], ['st'], out=st, in_=st, func=AF.Ln, scale=1.0 / 64, bias=eps2)
        O('act', 'activation', ['st'], ['st'], out=st, in_=st, func=AF.Exp, scale=-0.5)
        O('dve', 'tensor_tensor', [yk, 'st'], [yk], out=h3(y), in0=h3(y), in1=st.unsqueeze(2).broadcast_to([128, 16, 64]), op=ALU.mult)
        O('dve', 'tensor_tensor', [yk, 'lnw'], [yk], out=y, in0=y, in1=lnw, op=ALU.mult)
        O('dve', 'tensor_tensor', [('bon', p), 'lnb'], [('bon', p)], out=bon[p], in0=bon[p], in1=lnb, op=ALU.add)
        O('dve', 'tensor_tensor', [yk, ('bon', p)], [yk], out=y, in0=y, in1=bon[p], op=ALU.add)
        O('dve', 'tensor_tensor', [yk, ('gin', p)], [('yo', p)], out=yo[p], in0=y, in1=gin[p], op=ALU.mult)
        DMA('pool', yc_s[rows, :], yo[p], [('yo', p)], [])
    P.barrier()
    A.reset()

    outproj_mlp(1, rw_w_o, [(yc_s, D)], list(range(2, NT)), lambda t: xs1[t * 128:(t + 1) * 128, :],
                lambda t: out_d[(t - 2) * 128:(t - 1) * 128, :])

    P.finish()
    return nc


def host_inputs(inputs, b):
    f = np.float32
    m = {}
    m["x"] = np.ascontiguousarray(inputs["x"][b])
    m["ctx"] = np.ascontiguousarray(inputs["ctx"][b])
    c2 = np.stack([inputs["c"][b], inputs["c_ctx"]], 0)
    m["c2T"] = np.ascontiguousarray(c2.reshape(2, 8, 128).transpose(2, 1, 0))
    m["ada_w"] = inputs["ada_w"]
    m["ada_b"] = inputs["ada_b"]
    m["ab_w_in"] = np.ascontiguousarray(inputs["ab_w_in"][0])
    m["ab_gate_b"] = np.ascontiguousarray(inputs["ab_gate_b"])
    m["qk_norm"] = np.stack([np.tile(inputs["na_q_norm"][0], 8), np.tile(inputs["na_k_norm"][0], 8)], 0).astype(f)
    m["ident"] = np.eye(128, dtype=f)
    pos = np.arange(T_LAT)
    rows = (pos // 64).astype(f)
    cols = (pos % 64).astype(f)
    inv = (10000.0 ** (-np.arange(32, dtype=f) / 32)).astype(f)
    ar = (rows[:, None] * inv).astype(f)
    ac = (cols[:, None] * inv).astype(f)
    C = np.concatenate([np.cos(ar), np.cos(ar), np.cos(ac), np.cos(ac)], 1).astype(f)
    S = np.concatenate([-np.sin(ar), np.sin(ar), -np.sin(ac), np.sin(ac)], 1).astype(f)
    m["ropeC"] = np.ascontiguousarray(np.tile(C, (1, 4)))
    m["ropeS"] = np.ascontiguousarray(np.tile(S, (1, 4)))
    m["na_bias"] = na_bias_table(inputs["na_rpb"][0])
    k = np.arange(128)
    triF = (k[:, None] <= k[None, :]).astype(f)
    triB = (k[:, None] >= k[None, :]).astype(f)
    m["trimask"] = np.stack([triF, triB, (1 - triF) * NEG, (1 - triB) * NEG], 0).astype(f)
    sF = (k[:, None] < k[None, :]).astype(f)
    m["rwmask"] = np.stack([triF, triB, sF, sF.T], 0).astype(f)
    lv = []
    for l in range(7):
        B = 1 << l
        lv.append(((k[:, None] // (2 * B)) == (k[None, :] // (2 * B))) & ((k[:, None] // B) != (k[None, :] // B)))
    m["lvlmask"] = np.stack(lv, 0).astype(f)
    m["pmask"] = np.stack([(k < 64), (k >= 64)], 1).astype(f)
    m["head_norm"] = np.ascontiguousarray(inputs["ml_head_norm"][0].reshape(1, 512))
    m["ab_w_out"] = np.ascontiguousarray(inputs["ab_w_out"][0])
    m["mlp_w1"] = inputs["mlp_w1"]
    m["mlp_w2"] = inputs["mlp_w2"]
    m["rw_muT"] = np.ascontiguousarray(inputs["rw_mu"][0].reshape(6, 8, 128).transpose(2, 0, 1))
    m["rw_w_rkv"] = np.ascontiguousarray(inputs["rw_w_rkv"][0])
    m["rw_w1c"] = np.ascontiguousarray(np.concatenate([inputs["rw_w1"][0, 0], inputs["rw_w1"][0, 1]], 1))
    m["rw_w2c"] = np.ascontiguousarray(np.concatenate([inputs["rw_w2"][0, 0], inputs["rw_w2"][0, 1]], 0))
    m["rw_a1c"] = np.ascontiguousarray(np.concatenate([inputs["rw_a1"][0, 0], inputs["rw_a1"][0, 1]], 1))
    m["rw_a2c"] = np.ascontiguousarray(np.concatenate([inputs["rw_a2"][0, 0], inputs["rw_a2"][0, 1]], 0))
    m["rw_g1"] = np.ascontiguousarray(inputs["rw_g1"][0])
    m["rw_g2"] = np.ascontiguousarray(inputs["rw_g2"][0])
    m["rw_vecs"] = np.stack([inputs["rw_w0"][0, 0], inputs["rw_w0"][0, 1], inputs["rw_a0"][0, 0], inputs["rw_a0"][0, 1],
                             inputs["rw_k_k"][0], inputs["rw_k_a"][0], inputs["rw_r_k"][0], inputs["rw_lnx_w"][0],
                             inputs["rw_lnx_b"][0]], 0).astype(f)
    m["rw_w_o"] = np.ascontiguousarray(inputs["rw_w_o"][0])
    return m


def na_bias_table(rpb):
    tab = np.full((8, 21, 128, 128), NEG, np.float32)
    col = np.arange(64)
    cw = np.clip(col - 8, 0, 48)
    valid = (col[:, None] >= cw[None, :]) & (col[:, None] < cw[None, :] + 16)
    cidx = np.clip(col[:, None] - col[None, :] + 15, 0, 30)

    def fill(idx, qt, kc):
        for qr in range(2):
            q_row = 2 * qt + qr
            rs = min(max(q_row - 4, 0), 56)
            for kr in range(2):
                k_row = 2 * kc + kr
                if not (rs <= k_row < rs + 8):
                    continue
                vals = rpb[:, k_row - q_row + 7][:, cidx]
                tab[:, idx, kr * 64:(kr + 1) * 64, qr * 64:(qr + 1) * 64] = np.where(valid[None], vals, NEG)

    for rel in range(-2, 3):
        fill(rel + 2, 10, 10 + rel)
    for sidx, lq in enumerate((0, 1, 30, 31)):
        base = 0 if lq < 2 else 28
        for j in range(4):
            fill(5 + 4 * sidx + j, lq, base + j)
    return np.ascontiguousarray(tab.transpose(0, 2, 1, 3).reshape(8, 128, 21 * 128))


_CACHE = {}


def kernel(**inputs):
    inputs = {k: np.asarray(v) for k, v in inputs.items()}
    if "nc" not in _CACHE:
        _CACHE["nc"] = build_program()
    nc = _CACHE["nc"]
    in_maps = [host_inputs(inputs, b) for b in range(8)]
    res = run_bass_kernel_spmd(nc, in_maps, core_ids=list(range(8)))
    return np.stack([r["out"] for r in res.results], 0).astype(np.float32)
```

```python
import numpy as np
import concourse.bass as bass
import concourse.mybir as mybir
from concourse.bass_utils import run_bass_kernel_spmd

F32 = mybir.dt.float32
BF16 = mybir.dt.bfloat16
U8 = mybir.dt.uint8
ALU = mybir.AluOpType
AF = mybir.ActivationFunctionType
AX = mybir.AxisListType

ENGS = ['pe', 'act', 'dve', 'pool', 'sp']
EPOCH = 12000
NDS = 12

D = 1024
T_LAT = 4096
T_CTX = 256
NT = 34
NEG = -30000.0
GLEVEL = 4
GTEST = 0
GSTEPS = NT


class Prog:
    def __init__(self, nc):
        self.nc = nc
        self.ops = {e: [] for e in ENGS}
        self.cnt = {}
        self.last_w = {}
        self.readers = {}
        self.waited = {e: {} for e in ENGS}
        self.sems = {}
        self.nops = 0

    def _ticket(self, eng, kind):
        k = (eng, kind)
        n = self.cnt.get(k, 0)
        self.cnt[k] = n + 1
        if kind == 'c':
            ep, v = divmod(n, EPOCH)
            return ((eng, kind, ep), v + 1)
        return ((eng, kind, n % NDS), 16 * (n // NDS + 1))

    def op(self, eng, fn, reads=(), writes=(), dma=False):
        kind = 'd' if dma else 'c'
        deps = {}
        pr = [r for r in reads if isinstance(r, tuple) and r[0] == 'ps']
        if pr:
            reads = [r for r in reads if not (isinstance(r, tuple) and r[0] == 'ps')]
            writes = list(writes) + [r for r in pr if r not in writes]

        def add(t, war=False):
            if t is None:
                return
            key, val = t
            if not dma and key[1] == 'c' and key[0] == eng:
                if eng == 'pe':
                    return
            if deps.get(key, 0) < val:
                deps[key] = val

        for r in reads:
            add(self.last_w.get(r))
        for w in writes:
            add(self.last_w.get(w))
            for t in self.readers.get(w, ()):
                add(t, war=True)
        if dma:
            n = self.cnt.get((eng, 'd'), 0)
            if n >= NDS:
                add(((eng, 'd', n % NDS), 16 * (n // NDS)))
        waits = []
        wd = self.waited[eng]
        for key, val in deps.items():
            if wd.get(key, 0) >= val:
                continue
            wd[key] = val
            waits.append((key, val))
        t = self._ticket(eng, kind)
        self.ops[eng].append((fn, waits, t))
        for w in writes:
            self.last_w[w] = t
            self.readers[w] = []
        for r in reads:
            self.readers.setdefault(r, []).append(t)
        self.nops += 1
        return t

    def barrier(self):
        finals = {}
        for e in ENGS:
            for (fn, waits, t) in self.ops[e]:
                if t is None:
                    continue
                key, val = t
                if finals.get(key, 0) < val:
                    finals[key] = val
        for e in ENGS:
            wd = self.waited[e]
            waits = []
            for key, val in finals.items():
                if key[0] == e and key[1] == 'c' and e == 'pe':
                    continue
                if wd.get(key, 0) >= val:
                    continue
                wd[key] = val
                waits.append((key, val))
            if waits:
                self.ops[e].append((None, waits, None))
        self.last_w = {}
        self.readers = {}

    def finish(self):
        self.barrier()
        nc = self.nc
        keys = set()
        for e in ENGS:
            for (fn, waits, t) in self.ops[e]:
                if t is not None:
                    keys.add(t[0])
                for (k, v) in waits:
                    keys.add(k)
        for k in sorted(keys):
            self.sems[k] = nc.alloc_semaphore("s_%s_%s_%d" % k)
        sems = self.sems
        ops = self.ops

        def emit(ename, e):
            for (fn, waits, t) in ops[ename]:
                for (k, v) in waits:
                    e.wait_ge(sems[k], v)
                if fn is None:
                    continue
                ins = fn(e)
                ins.then_inc(sems[t[0]], 16 if t[0][1] == 'd' else 1)

        with nc.Block() as block:
            @block.tensor
            def _(e):
                emit('pe', e)

            @block.scalar
            def _(e):
                emit('act', e)

            @block.vector
            def _(e):
                emit('dve', e)

            @block.gpsimd
            def _(e):
                emit('pool', e)

            @block.sync
            def _(e):
                emit('sp', e)


class Arena:
    def __init__(self, nc, nbytes):
        self.t = nc.alloc_sbuf_tensor("arena", [128, nbytes], U8)
        self.size = nbytes
        self.off = 0
        self.mark_ = 0

    def alloc(self, shape, dt):
        esz = 2 if dt == BF16 else 4
        n = 1
        for s in shape:
            n *= s
        nb = (n * esz + 63) // 64 * 64
        assert self.off + nb <= self.size, ("SBUF arena overflow", self.off, nb)
        self.hw = max(getattr(self, 'hw', 0), self.off + nb)
        ap = self.t[:, self.off:self.off + n * esz].bitcast(dt)
        self.off += nb
        if len(shape) == 2:
            ap = ap.rearrange("p (a b) -> p a b", b=shape[1])
        elif len(shape) == 3:
            ap = ap.rearrange("p (a b c) -> p a b c", b=shape[1], c=shape[2])
        return ap

    def mark(self):
        self.mark_ = self.off

    def reset(self):
        self.off = self.mark_


def build_program(debug=(), stop_after=None):
    nc = bass.Bass("TRN2", target_bir_lowering=False)
    P = Prog(nc)

    def din(name, shape, dt=F32):
        return nc.dram_tensor(name, list(shape), dt, kind="ExternalInput").ap()

    def scratch(name, shape, dt):
        kind = "ExternalOutput" if name in debug else "Internal"
        return nc.dram_tensor(name, list(shape), dt, kind=kind).ap()

    x_in = din("x", [T_LAT, D])
    ctx_in = din("ctx", [T_CTX, D])
    c2T = din("c2T", [128, 8, 2])
    ada_w = din("ada_w", [2, D, 6 * D])
    ada_b = din("ada_b", [2, 6 * D])
    ab_w_in = din("ab_w_in", [D, 3600])
    gate_b = din("ab_gate_b", [1, 16])
    qk_norm = din("qk_norm", [2, 512])
    ident_in = din("ident", [128, 128])
    ropeC = din("ropeC", [T_LAT, 512])
    ropeS = din("ropeS", [T_LAT, 512])
    na_bias = din("na_bias", [8, 128, 21 * 128])
    trimask = din("trimask", [4, 128, 128])
    rwmask = din("rwmask", [4, 128, 128])
    lvlmask = din("lvlmask", [7, 128, 128])
    pmask_in = din("pmask", [128, 2])
    head_norm = din("head_norm", [1, 512])
    ab_w_out = din("ab_w_out", [D, D])
    mlp_w1 = din("mlp_w1", [2, D, 4 * D])
    mlp_w2 = din("mlp_w2", [2, 4 * D, D])
    rw_muT = din("rw_muT", [128, 6, 8])
    rw_w_rkv = din("rw_w_rkv", [3, D, D])
    rw_w1c = din("rw_w1c", [D, 128])
    rw_w2c = din("rw_w2c", [128, D])
    rw_a1c = din("rw_a1c", [D, 128])
    rw_a2c = din("rw_a2c", [128, D])
    rw_g1 = din("rw_g1", [D, 160])
    rw_g2 = din("rw_g2", [160, D])
    rw_vecs = din("rw_vecs", [9, D])
    rw_w_o = din("rw_w_o", [D, D])
    out_d = nc.dram_tensor("out", [T_LAT, D], F32, kind="ExternalOutput").ap()

    modv = scratch("modv", [2, 2, 6 * D], F32)
    qa_s = scratch("qa_s", [NT * 128, 512], BF16)
    ka_s = scratch("ka_s", [NT * 128, 512], BF16)
    va_s = scratch("va_s", [NT * 128, 520], BF16)
    qb_s = scratch("qb_s", [NT * 128, 512], BF16)
    kb_s = scratch("kb_s", [NT * 128, 512], BF16)
    vb_s = scratch("vb_s", [NT * 128, 516], BF16)
    ob_s = scratch("ob_s", [NT * 128, 512], F32)
    gt_s = scratch("gt_s", [NT * 128, 16], F32)
    na_s = scratch("na_s", [NT * 128, 512], BF16)
    hf_s = scratch("hf_s", [NT * 128, 512], F32)
    hb_s = scratch("hb_s", [NT * 128, 512], F32)
    ml_s = scratch("ml_s", [NT * 128, 512], BF16)
    x1_s = scratch("x1_s", [NT * 128, D], F32)
    xs1 = scratch("xs1", [NT * 128, D], F32)
    hbuf = scratch("hbuf", [NT * 128, D], F32)
    r_s = scratch("r_s", [NT * 128, D], BF16)
    v_s = scratch("v_s", [NT * 128, D], BF16)
    a_s = scratch("a_s", [NT * 128, D], BF16)
    g_s = scratch("g_s", [NT * 128, D], BF16)
    bon_s = scratch("bon_s", [NT * 128, D], F32)
    kd_s = scratch("kd_s", [2, NT * 128, D], BF16)
    bb_s = scratch("bb_s", [2, NT * 128, D], BF16)
    lw_s = scratch("lw_s", [2, NT * 128, D], F32)
    y_s = scratch("y_s", [2, NT * 128, D], F32)
    yc_s = scratch("yc_s", [NT * 128, D], BF16)

    A = Arena(nc, 190 * 1024)
    psum = nc.alloc_psum_tensor("psum", [128, 4096], F32)

    def bank(i, n=512, off=0):
        return psum[:, i * 512 + off:i * 512 + off + n]

    def bank_bf(i):
        return psum[:, i * 512:(i + 1) * 512].bitcast(BF16)

    def O(eng, method, R, W, **kw):
        return P.op(eng, lambda e: getattr(e, method)(**kw), reads=R, writes=W)

    def DMA(eng, out, in_, R, W, slow=False):
        if slow:
            return P.op(eng, lambda e: e.dma_start(out=out, in_=in_, allow_slow_non_contiguous=True), reads=R, writes=W, dma=True)
        return P.op(eng, lambda e: e.dma_start(out=out, in_=in_), reads=R, writes=W, dma=True)

    def xrows(t):
        if t < 2:
            return ctx_in[t * 128:(t + 1) * 128, :]
        return x_in[(t - 2) * 128:(t - 1) * 128, :]

    identf = A.alloc([128], F32)
    identb = A.alloc([128], BF16)
    epsb = A.alloc([1], F32)
    DMA('sp', identf, ident_in, [], ['identf'])
    O('dve', 'tensor_copy', ['identf'], ['identb'], out=identb, in_=identf)
    O('dve', 'memset', [], ['epsb'], ap=epsb, constant=1e-6)
    oneb = A.alloc([1], F32)
    O('dve', 'memset', [], ['oneb'], ap=oneb, constant=1.0)
    A.mark()

    sT = A.alloc([8, 2], F32)
    mod = A.alloc([6 * D], F32)
    badd = A.alloc([6 * D], F32)
    wst = [A.alloc([8, 512], F32) for _ in range(2)]
    DMA('sp', sT, c2T, [], ['sT'])
    O('act', 'activation', ['sT'], ['sT'], out=sT, in_=sT, func=AF.Silu)
    it = 0
    for l in range(2):
        DMA('pool', badd[0:2, :], ada_b[l:l + 1, :].partition_broadcast(2), [], ['badd'])
        for n in range(12):
            wb_ = wst[it % 2]
            wk = ('wst', it % 2)
            DMA('sp', wb_, ada_w[l].rearrange("(k p) n -> p k n", p=128)[:, :, n * 512:(n + 1) * 512], [], [wk])
            pb = ('ps', it % 2)
            for k in range(8):
                O('pe', 'matmul', ['sT', wk], [pb], out=bank(it % 2)[0:2, :], lhsT=sT[:, k, :], rhs=wb_[:, k, :],
                  start=(k == 0), stop=(k == 7))
            O('dve', 'tensor_tensor', [pb, 'badd'], ['mod'], out=mod[0:2, n * 512:(n + 1) * 512],
              in0=bank(it % 2)[0:2, :], in1=badd[0:2, n * 512:(n + 1) * 512], op=ALU.add)
            it += 1
        for i in (1, 4):
            O('dve', 'tensor_scalar', ['mod'], ['mod'], out=mod[0:2, i * D:(i + 1) * D], in0=mod[0:2, i * D:(i + 1) * D],
              scalar1=1.0, scalar2=None, op0=ALU.add)
        DMA('pool', modv[l], mod[0:2, :], ['mod'], ['modv'])
    P.barrier()
    A.reset()

    def load_mod(dst, layer, seg, idx, key):
        DMA('pool', dst, modv[layer, seg:seg + 1, idx * D:(idx + 1) * D].partition_broadcast(128), ['modv'], [key])

    def rms_modulate(xt, xk, sc, sck, sh, shk, outb, outk, tmp, tmpk, ss, ssk, junk, junkk):
        O('act', 'activation', [xk], [junkk, ssk], out=junk, in_=xt, func=AF.Square, accum_out=ss)
        O('act', 'activation', [ssk, 'epsb'], [ssk], out=ss, in_=ss, func=AF.Ln, scale=1.0 / D, bias=epsb)
        O('act', 'activation', [ssk], [ssk], out=ss, in_=ss, func=AF.Exp, scale=-0.5)
        O('dve', 'scalar_tensor_tensor', [xk, ssk, sck], [tmpk], out=tmp, in0=xt, scalar=ss[:, 0:1], in1=sc,
          op0=ALU.mult, op1=ALU.mult)
        O('dve', 'tensor_tensor', [tmpk, shk], [outk], out=outb, in0=tmp, in1=sh, op=ALU.add)

    def transpose8(src, srck, dst, dstk, pbank, pkey, evac='act'):
        pv = bank_bf(pbank).rearrange("p (c t) -> p c t", t=128)
        for c in range(8):
            O('pe', 'transpose', [srck, 'identb'], [pkey], out=pv[:, c, :], in_=src[:, c * 128:(c + 1) * 128], identity=identb)
        if evac == 'act':
            O('act', 'copy', [pkey], [dstk], out=dst, in_=pv)
        else:
            O('dve', 'tensor_copy', [pkey], [dstk], out=dst, in_=pv)

    w_in = A.alloc([8, 3600], BF16)
    wst = [A.alloc([8, 400], F32) for _ in range(2)]
    for i in range(9):
        DMA('sp', wst[i % 2], ab_w_in.rearrange("(k p) n -> p k n", p=128)[:, :, i * 400:(i + 1) * 400], [], [('wst', i % 2)])
        O('pool', 'tensor_copy', [('wst', i % 2)], ['w_in'], out=w_in[:, :, i * 400:(i + 1) * 400], in_=wst[i % 2])
    scs = [A.alloc([D], F32) for _ in range(2)]
    shs = [A.alloc([D], F32) for _ in range(2)]
    for seg in range(2):
        load_mod(scs[seg], 0, seg, 1, ('sc', seg))
        load_mod(shs[seg], 0, seg, 0, ('sh', seg))
    qkn = A.alloc([2, 512], F32)
    DMA('pool', qkn[:, 0, :], qk_norm[0:1, :].partition_broadcast(128), [], ['qkn'])
    DMA('pool', qkn[:, 1, :], qk_norm[1:2, :].partition_broadcast(128), [], ['qkn'])
    gb = A.alloc([16], F32)
    DMA('pool', gb, gate_b.partition_broadcast(128), [], ['gb'])
    xts = [A.alloc([D], F32) for _ in range(2)]
    rC = [A.alloc([512], F32) for _ in range(2)]
    rS = [A.alloc([512], F32) for _ in range(2)]
    junk = A.alloc([D], F32)
    tmp = A.alloc([D], F32)
    hb = A.alloc([D], BF16)
    hT = A.alloc([8, 128], BF16)
    ss = A.alloc([1], F32)
    proj = A.alloc([3600], F32)
    sq = A.alloc([512], F32)
    hs = A.alloc([8], F32)
    qo = [A.alloc([512], BF16) for _ in range(2)]
    ko = [A.alloc([512], BF16) for _ in range(2)]
    qbo = [A.alloc([512], BF16) for _ in range(2)]
    kbo = [A.alloc([512], BF16) for _ in range(2)]
    vao = [A.alloc([8, 65], BF16) for _ in range(2)]
    vbo = [A.alloc([4, 129], BF16) for _ in range(2)]
    oo = [A.alloc([512], F32) for _ in range(2)]
    go = [A.alloc([16], F32) for _ in range(2)]
    rt1 = A.alloc([512], F32)
    rt2 = A.alloc([512], F32)
    for i in range(2):
        O('dve', 'memset', [], [('vao', i)], ap=vao[i], constant=1.0)
        O('dve', 'memset', [], [('vbo', i)], ap=vbo[i], constant=1.0)
    nchunks = [(n * 512, 512) for n in range(7)] + [(3584, 16)]
    for t in range(NT):
        p = t % 2
        seg = 1 if t < 2 else 0
        xt = xts[p]
        xk = ('xt', p)
        DMA('sp', xt, xrows(t), [], [xk])
        if seg == 0:
            DMA('sp', rC[p], ropeC[(t - 2) * 128:(t - 1) * 128, :], [], [('rC', p)])
            DMA('sp', rS[p], ropeS[(t - 2) * 128:(t - 1) * 128, :], [], [('rS', p)])
        rms_modulate(xt, xk, scs[seg], ('sc', seg), shs[seg], ('sh', seg), hb, 'hb', tmp, 'tmp', ss, 'ss', junk, 'junk')
        transpose8(hb, 'hb', hT, 'hT', 7, ('ps', 7))
        for ci, (c0, cn) in enumerate(nchunks):
            b = ci % 4
            pk = ('ps', b)
            for k in range(8):
                O('pe', 'matmul', ['hT', 'w_in'], [pk], out=bank(b, cn), lhsT=hT[:, k, :], rhs=w_in[:, k, c0:c0 + cn],
                  start=(k == 0), stop=(k == 7))
            if ci % 2 == 0:
                O('act', 'copy', [pk], [('proj', ci)], out=proj[:, c0:c0 + cn], in_=bank(b, cn))
            else:
                O('dve', 'tensor_copy', [pk], [('proj', ci)], out=proj[:, c0:c0 + cn], in_=bank(b, cn))
        rows = slice(t * 128, (t + 1) * 128)
        for which, (dst, dstk, dram) in enumerate(((qo[p], ('qo', p), qa_s), (ko[p], ('ko', p), ka_s))):
            src = proj[:, which * 512:(which + 1) * 512]
            sk = ('proj', which)
            O('dve', 'tensor_tensor', [sk], ['sq'], out=sq, in0=src, in1=src, op=ALU.mult)
            O('dve', 'tensor_reduce', ['sq'], ['hs'], out=hs, in_=sq.rearrange("p (h d) -> p h d", d=64), axis=AX.X, op=ALU.add)
            O('act', 'activation', ['hs', 'epsb'], ['hs'], out=hs, in_=hs, func=AF.Ln, scale=1.0 / 64, bias=epsb)
            O('act', 'activation', ['hs'], ['hs'], out=hs, in_=hs, func=AF.Exp, scale=-0.5)
            O('dve', 'tensor_tensor', [sk, 'hs'], ['sq'], out=sq.rearrange("p (h d) -> p h d", d=64),
              in0=src.rearrange("p (h d) -> p h d", d=64), in1=hs.unsqueeze(2).broadcast_to([128, 8, 64]), op=ALU.mult)
            O('dve', 'scalar_tensor_tensor', ['sq', 'qkn'], [dstk], out=dst, in0=sq, scalar=(0.125 if which == 0 else 1.0),
              in1=qkn[:, which, :], op0=ALU.mult, op1=ALU.mult)
            DMA('pool', dram[rows, :], dst, [dstk], [])
        O('act', 'copy', [('proj', 2)], [('vao', p)], out=vao[p][:, :, 0:64], in_=proj[:, 1024:1536].rearrange("p (h d) -> p h d", d=64))
        DMA('pool', va_s[rows, :], vao[p].rearrange("p h d -> p (h d)"), [('vao', p)], [])
        for which, (dst, dstk, dram, scl) in enumerate(((qbo[p], ('qbo', p), qb_s, 1.0), (kbo[p], ('kbo', p), kb_s, 128 ** -0.5))):
            c0 = 1536 + which * 512
            src = proj[:, c0:c0 + 512]
            sk = ('proj', 3 + which)
            if seg == 1:
                O('act', 'mul', [sk], [dstk], out=dst, in_=src, mul=scl)
            else:
                v5 = lambda ap: ap.rearrange("p (h b f d) -> p h b f d", h=4, b=2, f=2)
                O('dve', 'tensor_tensor', [sk, ('rC', p)], ['rt1'], out=rt1, in0=src, in1=rC[p], op=ALU.mult)
                for f in range(2):
                    O('dve', 'tensor_tensor', [sk, ('rS', p)], [('rt2', f)], out=v5(rt2)[:, :, :, f, :], in0=v5(src)[:, :, :, 1 - f, :],
                      in1=v5(rS[p])[:, :, :, f, :], op=ALU.mult)
                O('dve', 'tensor_tensor', ['rt1', ('rt2', 0), ('rt2', 1)], ['rt1'], out=rt1, in0=rt1, in1=rt2, op=ALU.add)
                O('act', 'mul', ['rt1'], [dstk], out=dst, in_=rt1, mul=scl)
            DMA('pool', dram[rows, :], dst, [dstk], [])
        O('act', 'copy', [('proj', 5)], [('vbo', p)], out=vbo[p][:, :, 0:128], in_=proj[:, 2560:3072].rearrange("p (h d) -> p h d", d=128))
        DMA('pool', vb_s[rows, :], vbo[p].rearrange("p h d -> p (h d)"), [('vbo', p)], [])
        O('act', 'activation', [('proj', 6)], [('oo', p)], out=oo[p], in_=proj[:, 3072:3584], func=AF.Sigmoid)
        DMA('pool', ob_s[rows, :], oo[p], [('oo', p)], [])
        g = go[p]
        gk = ('go', p)
        O('dve', 'tensor_tensor', [('proj', 7), 'gb'], [gk], out=g, in0=proj[:, 3584:3600], in1=gb, op=ALU.add)
        O('act', 'activation', [gk], [gk], out=g, in_=g, func=AF.Tanh, scale=1.0 / 15.0)
        O('dve', 'tensor_scalar', [gk], [gk], out=g, in0=g, scalar1=15.0, scalar2=None, op0=ALU.mult)
        gv = g.rearrange("p (a b h) -> p a b h", a=2, b=2)
        fv = gv[:, :, 1, :]
        O('act', 'activation', [gk], [gk], out=fv, in_=fv, func=AF.Exp, scale=-1.0)
        O('act', 'activation', [gk, 'oneb'], [gk], out=fv, in_=fv, func=AF.Ln, bias=oneb)
        O('dve', 'tensor_scalar', [gk], [gk], out=fv, in0=fv, scalar1=-1.0, scalar2=None, op0=ALU.mult)
        DMA('pool', gt_s[rows, :], g, [gk], [])
    P.barrier()
    A.reset()
    if stop_after == 'B':
        P.finish()
        return nc

    def load_T(src_dram, dstT, dstk, nchunk=4):
        tin = [A.alloc([nchunk * 128], BF16) for _ in range(2)]
        for t in range(NT):
            p = t % 2
            DMA('sp', tin[p], src_dram[t * 128:(t + 1) * 128, :], [], [('tin', p)])
            pv = bank_bf(6 + p).rearrange("p (c t) -> p c t", t=128)
            for c in range(nchunk):
                O('pe', 'transpose', [('tin', p), 'identb'], [('ps', 6 + p)], out=pv[:, c, :], in_=tin[p][:, c * 128:(c + 1) * 128],
                  identity=identb)
            if p == 0:
                O('act', 'copy', [('ps', 6 + p)], [dstk], out=dstT[:, :, t * 128:(t + 1) * 128], in_=pv[:, 0:nchunk, :])
            else:
                O('dve', 'tensor_copy', [('ps', 6 + p)], [dstk], out=dstT[:, :, t * 128:(t + 1) * 128], in_=pv[:, 0:nchunk, :])

    QT = A.alloc([4, NT * 128], BF16)
    KT = A.alloc([4, NT * 128], BF16)
    Vn = A.alloc([NT, 520], BF16)
    BT = A.alloc([8, 21, 128], BF16)
    bst = [A.alloc([21 * 128], F32) for _ in range(2)]
    for h in range(8):
        DMA('sp', bst[h % 2], na_bias[h], [], [('bst', h % 2)])
        O('pool', 'tensor_copy', [('bst', h % 2)], ['BT'], out=BT[:, h].rearrange("p a b -> p (a b)"), in_=bst[h % 2])
    DMA('sp', Vn, va_s.rearrange("(t p) f -> p t f", p=128), [], ['Vn'])
    load_T(qa_s, QT, 'QT')
    load_T(ka_s, KT, 'KT')
    PT = [A.alloc([7 * 128], BF16) for _ in range(2)]
    nao = [A.alloc([512], BF16) for _ in range(2)]
    rden = A.alloc([8], F32)
    order = list(range(2, NT)) + [0, 1]
    for qi, qt in enumerate(order):
        p = qi % 2
        if qt >= 2:
            lq = qt - 2
            if 2 <= lq <= 29:
                chunks = [(2 + lq + r, r + 2) for r in range(-2, 3)]
            else:
                sidx = {0: 0, 1: 1, 30: 2, 31: 3}[lq]
                base = 0 if lq < 2 else 28
                chunks = [(2 + base + j, 5 + 4 * sidx + j) for j in range(4)]
            chunks += [(0, None), (1, None)]
        else:
            chunks = [(0, None), (1, None)]
        ncx = len(chunks)
        for h in range(8):
            c, po = h // 2, (h % 2) * 64
            sb = h % 2
            sk = ('ps', 2 * sb)
            sk2 = ('ps', 2 * sb + 1)
            for ci, (kt, bidx) in enumerate(chunks):
                bnk = sb * 2 + ci // 4
                outp = bank(bnk, 128, (ci % 4) * 128)
                O('pe', 'matmul', ['QT', 'KT'], [sk, sk2], out=outp, lhsT=KT[po:po + 64, c, kt * 128:(kt + 1) * 128],
                  rhs=QT[po:po + 64, c, qt * 128:(qt + 1) * 128], start=True, stop=(bidx is None))
                if bidx is not None:
                    O('pe', 'matmul', ['BT', 'identb'], [sk, sk2], out=outp, lhsT=identb, rhs=BT[:, h, bidx, :], start=False, stop=True)
            n = ncx * 128
            O('act', 'activation', [sk, sk2], [('PT', sb)], out=PT[sb][:, 0:n], in_=psum[:, sb * 1024:sb * 1024 + n], func=AF.Exp)
            ob = 4 + p * 2 + h // 4
            for ci, (kt, _) in enumerate(chunks):
                O('pe', 'matmul', [('PT', sb), 'Vn'], [('ps', ob)], out=bank(ob, 65, (h % 4) * 65), lhsT=PT[sb][:, ci * 128:(ci + 1) * 128],
                  rhs=Vn[:, kt, h * 65:(h + 1) * 65], start=(ci == 0), stop=(ci == ncx - 1))
        for half in range(2):
            ob = 4 + p * 2 + half
            ov = bank(ob, 260).rearrange("p (h d) -> p h d", d=65)
            O('dve', 'reciprocal', [('ps', ob)], ['rden'], out=rden[:, half * 4:(half + 1) * 4], in_=ov[:, :, 64])
            O('dve', 'tensor_tensor', [('ps', ob), 'rden'], [('nao', p)],
              out=nao[p][:, half * 256:(half + 1) * 256].rearrange("p (h d) -> p h d", d=64), in0=ov[:, :, 0:64],
              in1=rden[:, half * 4:(half + 1) * 4].unsqueeze(2).broadcast_to([128, 4, 64]), op=ALU.mult)
        DMA('pool', na_s[qt * 128:(qt + 1) * 128, :], nao[p], [('nao', p)], [])
    P.barrier()
    A.reset()
    if stop_after == 'C':
        P.finish()
        return nc

    QbT = A.alloc([4, NT * 128], BF16)
    KbT = A.alloc([4, NT * 128], BF16)
    Kb = A.alloc([NT, 512], BF16)
    Vb = A.alloc([NT, 516], BF16)
    G = A.alloc([NT, 16], F32)
    tm = A.alloc([4, 128], F32)
    hn = A.alloc([512], F32)
    DMA('sp', Kb, kb_s.rearrange("(t p) f -> p t f", p=128), [], ['Kb'])
    DMA('sp', Vb, vb_s.rearrange("(t p) f -> p t f", p=128), [], ['Vb'])
    DMA('sp', G, gt_s.rearrange("(t p) f -> p t f", p=128), [], ['G'])
    DMA('sp', tm, trimask.rearrange("a p f -> p a f"), [], ['tm'])
    DMA('pool', hn, head_norm.partition_broadcast(128), [], ['hn'])
    load_T(qb_s, QbT, 'QbT')
    load_T(kb_s, KbT, 'KbT')
    Cst = [A.alloc([4, 129], F32) for _ in range(2)]
    Cbf = [A.alloc([4, 129], BF16) for _ in range(2)]
    for d_ in range(2):
        O('dve', 'memset', [], [('Cst', d_)], ap=Cst[d_], constant=0.0)
        O('dve', 'memset', [], [('Cbf', d_)], ap=Cbf[d_], constant=0.0)
    nb = A.alloc([4], F32)
    LFbc = A.alloc([4, 128], F32)
    Ebc = A.alloc([4, 128], F32)
    Dm = A.alloc([4, 128], F32)
    DT = A.alloc([4, 128], F32)
    PTm = A.alloc([4, 128], BF16)
    Qs = A.alloc([4, 128], BF16)
    Kt = A.alloc([4, 128], BF16)
    den = A.alloc([4], F32)
    hout = [A.alloc([512], F32) for _ in range(2)]
    hfl = [A.alloc([512], F32) for _ in range(2)]
    obl = [A.alloc([512], F32) for _ in range(2)]
    msq = A.alloc([512], F32)
    mss = A.alloc([4], F32)
    mlo = [A.alloc([512], BF16) for _ in range(2)]
    orders = [list(range(NT)), [1, 0] + list(range(NT - 1, 1, -1))]
    for step in range(NT):
        for d_ in range(2):
            t = orders[d_][step]
            p = step % 2
            tri = tm[:, d_, :]
            mask = tm[:, 2 + d_, :]
            last = 127 if d_ == 0 else 0
            lf = G[:, t, d_ * 8 + 4:d_ * 8 + 8]
            ii = G[:, t, d_ * 8:d_ * 8 + 4]
            tok = slice(t * 128, (t + 1) * 128)
            O('pe', 'matmul', ['G', 'tm'], [('ps', 0)], out=bank(0, 4), lhsT=tri, rhs=lf, start=True, stop=True)
            O('dve', 'tensor_tensor', ['G', ('ps', 0)], ['nb'], out=nb, in0=ii, in1=bank(0, 4), op=ALU.subtract)
            O('dve', 'tensor_copy', ['G'], ['LFbc'], out=LFbc, in_=lf.unsqueeze(2).broadcast_to([128, 4, 128]))
            for h in range(4):
                O('pe', 'matmul', ['LFbc', 'tm'], [('ps', 1)], out=bank(1, 128, h * 128), lhsT=LFbc[:, h, :], rhs=tri, start=True, stop=True)
            pY = bank(1).rearrange("p (h t) -> p h t", t=128)
            O('act', 'activation', [('ps', 1)], ['Ebc'], out=Ebc, in_=pY, func=AF.Exp)
            O('dve', 'tensor_tensor', [('ps', 1), 'tm'], ['Dm'], out=Dm, in0=pY, in1=mask.unsqueeze(1).broadcast_to([128, 4, 128]), op=ALU.add)
            for h in range(4):
                O('act', 'activation', ['Dm', 'nb'], [('DT', h)], out=DT[:, h, :], in_=Dm[:, h, :], func=AF.Exp, bias=nb[:, h:h + 1])
            for h in range(4):
                O('pe', 'matmul', ['QbT', 'KbT'], [('ps', 2)], out=bank(2, 128, h * 128), lhsT=KbT[:, h, tok], rhs=QbT[:, h, tok], start=True, stop=True)
            DTk = [('DT', h) for h in range(4)]
            O('dve', 'tensor_tensor', [('ps', 2)] + DTk, ['PTm'], out=PTm, in0=bank(2).rearrange("p (h t) -> p h t", t=128), in1=DT, op=ALU.mult)
            O('dve', 'tensor_tensor', ['QbT', 'Ebc'], ['Qs'], out=Qs, in0=QbT[:, :, tok], in1=Ebc, op=ALU.mult)
            O('dve', 'tensor_tensor', ['Kb'] + DTk, ['Kt'], out=Kt, in0=Kb[:, t, :].rearrange("p (h d) -> p h d", d=128),
              in1=DT[:, :, last:last + 1].broadcast_to([128, 4, 128]), op=ALU.mult)
            for h in range(4):
                wb_ = 3 + h // 2
                outp = bank(wb_, 129, (h % 2) * 129)
                O('pe', 'matmul', ['PTm', 'Vb'], [('ps', wb_)], out=outp, lhsT=PTm[:, h, :], rhs=Vb[:, t, h * 129:(h + 1) * 129], start=True, stop=False)
                O('pe', 'matmul', ['Qs', ('Cbf', d_)], [('ps', wb_)], out=outp, lhsT=Qs[:, h, :], rhs=Cbf[d_][:, h, :], start=False, stop=True)
            ho = hout[p]
            hk = ('hout', p)
            for half in range(2):
                wv = bank(3 + half, 258).rearrange("p (h d) -> p h d", d=129)
                O('act', 'activation', [('ps', 3 + half)], ['den'], out=den[:, half * 2:half * 2 + 2], in_=wv[:, :, 128], func=AF.Abs)
                O('dve', 'tensor_scalar', ['den'], ['den'], out=den[:, half * 2:half * 2 + 2], in0=den[:, half * 2:half * 2 + 2], scalar1=1.0, scalar2=None,
                  op0=ALU.max)
                O('dve', 'reciprocal', ['den'], ['den'], out=den[:, half * 2:half * 2 + 2], in_=den[:, half * 2:half * 2 + 2])
                O('dve', 'tensor_tensor', [('ps', 3 + half), 'den'], [hk], out=ho[:, half * 256:(half + 1) * 256].rearrange("p (h d) -> p h d", d=128),
                  in0=wv[:, :, 0:128], in1=den[:, half * 2:half * 2 + 2].unsqueeze(2).broadcast_to([128, 2, 128]), op=ALU.mult)
            for h in range(4):
                vb_ = 5 + h // 2
                O('pe', 'matmul', ['Kt', 'Vb'], [('ps', vb_)], out=bank(vb_, 129, (h % 2) * 129), lhsT=Kt[:, h, :], rhs=Vb[:, t, h * 129:(h + 1) * 129],
                  start=True, stop=True)
            for h in range(4):
                vb_ = 5 + h // 2
                O('dve', 'scalar_tensor_tensor', [('Cst', d_), 'Ebc', ('ps', vb_)], [('Cst', d_)], out=Cst[d_][:, h, :], in0=Cst[d_][:, h, :],
                  scalar=Ebc[:, h, last:last + 1], in1=bank(vb_, 129, (h % 2) * 129), op0=ALU.mult, op1=ALU.add)
            O('act', 'copy', [('Cst', d_)], [('Cbf', d_)], out=Cbf[d_], in_=Cst[d_])
            rows = slice(t * 128, (t + 1) * 128)
            DMA('pool', (hf_s if d_ == 0 else hb_s)[rows, :], ho, [hk], [('hfb_s', d_, t)])
    for t in range(NT):
        p = t % 2
        rows = slice(t * 128, (t + 1) * 128)
        ho = hout[p]
        hk = ('hout', p)
        DMA('sp', ho, hb_s[rows, :], [('hfb_s', 1, t)], [hk])
        DMA('sp', hfl[p], hf_s[rows, :], [('hfb_s', 0, t)], [('hfl', p)])
        DMA('sp', obl[p], ob_s[rows, :], [], [('obl', p)])
        O('dve', 'tensor_tensor', [hk, ('hfl', p)], [hk], out=ho, in0=ho, in1=hfl[p], op=ALU.add)
        for h in range(4):
            O('act', 'activation', [hk], ['msq', 'mss'], out=msq[:, h * 128:(h + 1) * 128], in_=ho[:, h * 128:(h + 1) * 128], func=AF.Square,
              accum_out=mss[:, h:h + 1])
        O('act', 'activation', ['mss', 'epsb'], ['mss'], out=mss, in_=mss, func=AF.Ln, scale=1.0 / 128, bias=epsb)
        O('act', 'activation', ['mss'], ['mss'], out=mss, in_=mss, func=AF.Exp, scale=-0.5)
        O('dve', 'tensor_tensor', [hk, 'mss'], ['msq'], out=msq.rearrange("p (h d) -> p h d", d=128), in0=ho.rearrange("p (h d) -> p h d", d=128),
          in1=mss.unsqueeze(2).broadcast_to([128, 4, 128]), op=ALU.mult)
        O('dve', 'tensor_tensor', ['msq', 'hn'], ['msq'], out=msq, in0=msq, in1=hn, op=ALU.mult)
        O('dve', 'tensor_tensor', ['msq', ('obl', p)], [('mlo', p)], out=mlo[p], in0=msq, in1=obl[p], op=ALU.mult)
        DMA('pool', ml_s[rows, :], mlo[p], [('mlo', p)], [])
    P.barrier()
    A.reset()
    if stop_after == 'D':
        P.finish()
        return nc

    def outproj_mlp(layer, wo_dram, cat_srcs, tiles, xsrc, dst):
        wo = A.alloc([8, D], BF16)
        wst = [A.alloc([8, 512], F32) for _ in range(2)]
        for n in range(2):
            DMA('sp', wst[n], wo_dram.rearrange("(k p) n -> p k n", p=128)[:, :, n * 512:(n + 1) * 512], [], [('wst', n)])
            O('pool', 'tensor_copy', [('wst', n)], ['wo'], out=wo[:, :, n * 512:(n + 1) * 512], in_=wst[n])
        gta = A.alloc([D], F32)
        cat = [A.alloc([D], BF16) for _ in range(2)]
        catT = A.alloc([8, 128], BF16)
        xts = [A.alloc([D], F32) for _ in range(2)]
        x1o = [A.alloc([D], F32) for _ in range(2)]
        tmp = A.alloc([D], F32)
        cur_seg = None
        for ti, t in enumerate(tiles):
            p = ti % 2
            seg = 1 if t < 2 else 0
            if seg != cur_seg:
                load_mod(gta, layer, seg, 2, 'gta')
                cur_seg = seg
            rows = slice(t * 128, (t + 1) * 128)
            c0 = 0
            for (src, wd) in cat_srcs:
                DMA('sp', cat[p][:, c0:c0 + wd], src[rows, :], [], [('cat', p)])
                c0 += wd
            DMA('sp', xts[p], xsrc(t), [], [('xt', p)])
            transpose8(cat[p], ('cat', p), catT, 'catT', 7, ('ps', 7))
            for n in range(2):
                for k in range(8):
                    O('pe', 'matmul', ['catT', 'wo'], [('ps', 2 * p + n)], out=bank(2 * p + n), lhsT=catT[:, k, :], rhs=wo[:, k, n * 512:(n + 1) * 512],
                      start=(k == 0), stop=(k == 7))
            O('dve', 'tensor_tensor', [('ps', 2 * p), ('ps', 2 * p + 1), 'gta'], ['tmp'], out=tmp, in0=psum[:, p * 1024:(p + 1) * 1024], in1=gta, op=ALU.mult)
            O('dve', 'tensor_tensor', ['tmp', ('xt', p)], [('x1o', p)], out=x1o[p], in0=tmp, in1=xts[p], op=ALU.add)
            DMA('pool', x1_s[rows, :], x1o[p], [('x1o', p)], [])
        P.barrier()
        A.reset()
        w1 = A.alloc([8, 4 * D], BF16)
        w2 = A.alloc([32, D], BF16)
        off0 = A.off
        wst = [A.alloc([8, 512], F32) for _ in range(2)]
        i = 0
        for n in range(8):
            DMA('sp', wst[i % 2], mlp_w1[layer].rearrange("(k p) n -> p k n", p=128)[:, :, n * 512:(n + 1) * 512], [], [('wst', i % 2)])
            O('pool', 'tensor_copy', [('wst', i % 2)], ['w1'], out=w1[:, :, n * 512:(n + 1) * 512], in_=wst[i % 2])
            i += 1
        for n in range(8):
            wv_ = wst[i % 2].rearrange("p a b -> p (a b)").rearrange("p (x c) -> p x c", c=1024)
            DMA('sp', wv_, mlp_w2[layer].rearrange("(k p) n -> p k n", p=128)[:, n * 4:(n + 1) * 4, :], [], [('wst', i % 2)])
            O('pool', 'tensor_copy', [('wst', i % 2)], ['w2'], out=w2[:, n * 4:(n + 1) * 4, :], in_=wv_)
            i += 1
        P.barrier()
        A.off = off0
        mods = [A.alloc([D], F32) for _ in range(3)]
        xts = [A.alloc([D], F32) for _ in range(2)]
        tmp = A.alloc([D], F32)
        hb = A.alloc([D], BF16)
        xmT = A.alloc([8, 128], BF16)
        r1 = [A.alloc([128], F32) for _ in range(2)]
        h1T = A.alloc([32, 128], BF16)
        ss = A.alloc([1], F32)
        cur_seg = None
        for ti, t in enumerate(tiles):
            p = ti % 2
            seg = 1 if t < 2 else 0
            if seg != cur_seg:
                for mi, idx in enumerate((3, 4, 5)):
                    load_mod(mods[mi], layer, seg, idx, ('mod', mi))
                cur_seg = seg
            rows = slice(t * 128, (t + 1) * 128)
            x1t = xts[p]
            x1k = ('xt', p)
            DMA('sp', x1t, x1_s[rows, :], [], [x1k])
            rms_modulate(x1t, x1k, mods[1], ('mod', 1), mods[0], ('mod', 0), hb, 'hb', tmp, 'tmp', ss, 'ss', tmp, 'tmp')
            transpose8(hb, 'hb', xmT, 'xmT', 7, ('ps', 7), evac='dve')
            for f in range(32):
                b = 2 + f % 4
                for k in range(8):
                    O('pe', 'matmul', ['xmT', 'w1'], [('ps', b)], out=bank(b, 128), lhsT=w1[:, k, f * 128:(f + 1) * 128], rhs=xmT[:, k, :],
                      start=(k == 0), stop=(k == 7))
                O('act', 'activation', [('ps', b)], [('r1', f % 2)], out=r1[f % 2], in_=bank(b, 128), func=AF.Relu)
                O('dve', 'tensor_tensor', [('r1', f % 2)], [('h1T', f)], out=h1T[:, f, :], in0=r1[f % 2], in1=r1[f % 2], op=ALU.mult)
            h1k = [('h1T', f) for f in range(32)]
            for n in range(2):
                for k in range(32):
                    O('pe', 'matmul', h1k + ['w2'], [('ps', n)], out=bank(n), lhsT=h1T[:, k, :], rhs=w2[:, k, n * 512:(n + 1) * 512],
                      start=(k == 0), stop=(k == 31))
            O('dve', 'tensor_tensor', [('ps', 0), ('ps', 1), ('mod', 2)], ['tmp'], out=tmp, in0=psum[:, 0:1024], in1=mods[2], op=ALU.mult)
            O('dve', 'tensor_tensor', ['tmp', x1k], [x1k], out=x1t, in0=tmp, in1=x1t, op=ALU.add)
            DMA('pool', dst(t), x1t, [x1k], [])
        P.barrier()
        A.reset()

    outproj_mlp(0, ab_w_out, [(na_s, 512), (ml_s, 512)], list(range(NT)), xrows, lambda t: xs1[t * 128:(t + 1) * 128, :])
    if stop_after == 'E':
        P.finish()
        return nc

    scs = [A.alloc([D], F32) for _ in range(2)]
    shs = [A.alloc([D], F32) for _ in range(2)]
    for seg in range(2):
        load_mod(scs[seg], 1, seg, 1, ('sc', seg))
        load_mod(shs[seg], 1, seg, 0, ('sh', seg))
    xts = [A.alloc([D], F32) for _ in range(2)]
    hos = [A.alloc([D], F32) for _ in range(2)]
    tmp = A.alloc([D], F32)
    ss = A.alloc([1], F32)
    for t in range(NT):
        p = t % 2
        seg = 1 if t < 2 else 0
        rows = slice(t * 128, (t + 1) * 128)
        DMA('sp', xts[p], xs1[rows, :], [], [('xt', p)])
        rms_modulate(xts[p], ('xt', p), scs[seg], ('sc', seg), shs[seg], ('sh', seg), hos[p], ('ho', p), tmp, 'tmp', ss, 'ss', tmp, 'tmp')
        DMA('pool', hbuf[rows, :], hos[p], [('ho', p)], [])
    P.barrier()
    A.reset()

    wrkv = A.alloc([3, 8, D], BF16)
    w1c = A.alloc([8, 128], BF16)
    a1c = A.alloc([8, 128], BF16)
    g1 = A.alloc([8, 160], BF16)
    w2c = A.alloc([D], BF16)
    a2c = A.alloc([D], BF16)
    g2 = A.alloc([2, D], BF16)
    vecs = [A.alloc([D], F32) for _ in range(7)]
    muT = A.alloc([6, 8], F32)
    omka = A.alloc([D], F32)
    off0 = A.off
    wst = [A.alloc([8, 512], F32) for _ in range(2)]
    i = 0
    for j in range(3):
        for n in range(2):
            DMA('sp', wst[i % 2], rw_w_rkv[j].rearrange("(k p) n -> p k n", p=128)[:, :, n * 512:(n + 1) * 512], [], [('wst', i % 2)])
            O('pool', 'tensor_copy', [('wst', i % 2)], ['wrkv'], out=wrkv[:, j, :, n * 512:(n + 1) * 512], in_=wst[i % 2])
            i += 1
    for (src, dst_, wd, key) in ((rw_w1c, w1c, 128, 'w1c'), (rw_a1c, a1c, 128, 'a1c'), (rw_g1, g1, 160, 'g1')):
        DMA('sp', wst[i % 2][:, :, 0:wd], src.rearrange("(k p) n -> p k n", p=128), [], [('wst', i % 2)])
        O('pool', 'tensor_copy', [('wst', i % 2)], [key], out=dst_, in_=wst[i % 2][:, :, 0:wd])
        i += 1
    for (src, dst_, key) in ((rw_w2c, w2c, 'w2c'), (rw_a2c, a2c, 'a2c')):
        wv_ = wst[i % 2].rearrange("p a b -> p (a b)")[:, 0:D]
        DMA('sp', wv_, src, [], [('wst', i % 2)])
        O('pool', 'tensor_copy', [('wst', i % 2)], [key], out=dst_, in_=wv_)
        i += 1
    wv_ = wst[i % 2].rearrange("p a b -> p (a b)")[:, 0:2 * D].rearrange("p (a b) -> p a b", b=D)
    DMA('sp', wv_[:, 0, :], rw_g2[0:128, :], [], [('wst', i % 2)])
    DMA('sp', wv_[0:32, 1, :], rw_g2[128:160, :], [], [('wst', i % 2)])
    O('pool', 'tensor_copy', [('wst', i % 2)], ['g2'], out=g2[:, 0, :], in_=wv_[:, 0, :])
    O('pool', 'tensor_copy', [('wst', i % 2)], ['g2'], out=g2[0:32, 1, :], in_=wv_[0:32, 1, :])
    for j in range(7):
        DMA('pool', vecs[j], rw_vecs[j:j + 1, :].partition_broadcast(128), [], [('vec', j)])
    DMA('sp', muT, rw_muT, [], ['muT'])
    O('dve', 'tensor_scalar', [('vec', 5)], ['omka'], out=omka, in0=vecs[5], scalar1=-1.0, scalar2=1.0, op0=ALU.mult, op1=ALU.add)
    P.barrier()
    A.off = off0
    w0b, a0b, kkv, kav, rkv = vecs[0:2], vecs[2:4], vecs[4], vecs[5], vecs[6]
    hc = A.alloc([D], F32)
    hp_ = A.alloc([D], F32)
    hn_ = A.alloc([D], F32)
    hcT = A.alloc([8, 128], F32)
    xsT = [A.alloc([8, 128], BF16) for _ in range(6)]
    thT = A.alloc([128], BF16)
    alT = A.alloc([128], BF16)
    sgT = A.alloc([2, 128], BF16)
    rf = A.alloc([D], F32)
    kf = A.alloc([D], F32)
    vf = A.alloc([D], F32)
    asig = [A.alloc([D], F32) for _ in range(2)]
    kkt = A.alloc([D], F32)
    t1 = A.alloc([D], F32)
    t2 = A.alloc([D], F32)
    hs16 = A.alloc([16], F32)
    ob16 = [A.alloc([D], BF16) for _ in range(4)]
    of32 = [A.alloc([D], F32) for _ in range(2)]
    nb16 = [0]
    nf32 = [0]

    def out16():
        nb16[0] += 1
        j = nb16[0] % 4
        return ob16[j], ('ob16', j)

    def outf32():
        nf32[0] += 1
        j = nf32[0] % 2
        return of32[j], ('of32', j)

    NEH = -float(np.exp(-0.5))
    for t in range(NT):
        r0 = t * 128
        rows = slice(r0, r0 + 128)
        first = t in (0, 2)
        lastt = t in (1, NT - 1)
        DMA('sp', hc, hbuf[rows, :], [], ['hc'])
        if first:
            O('pool', 'memset', [], ['hp'], ap=hp_, constant=0.0)
            DMA('sp', hp_[1:128, :], hbuf[r0:r0 + 127, :], [], ['hp'])
        else:
            DMA('sp', hp_, hbuf[r0 - 1:r0 + 127, :], [], ['hp'])
        if lastt:
            O('pool', 'memset', [], ['hn'], ap=hn_, constant=0.0)
            DMA('sp', hn_[0:127, :], hbuf[r0 + 1:r0 + 128, :], [], ['hn'])
        else:
            DMA('sp', hn_, hbuf[r0 + 1:r0 + 129, :], [], ['hn'])
        O('dve', 'tensor_tensor', ['hp', 'hn'], ['hp'], out=hp_, in0=hp_, in1=hn_, op=ALU.add)
        O('dve', 'scalar_tensor_tensor', ['hp', 'hc'], ['hp'], out=hp_, in0=hp_, scalar=0.5, in1=hc, op0=ALU.mult, op1=ALU.subtract)
        pvh = psum[:, 0:1024].rearrange("p (c t) -> p c t", t=128)
        pvx = psum[:, 1024:2048].rearrange("p (c t) -> p c t", t=128)
        for c in range(8):
            O('pe', 'transpose', ['hc', 'identf'], [('ps', c // 4)], out=pvh[:, c, :], in_=hc[:, c * 128:(c + 1) * 128], identity=identf)
        for c in range(8):
            O('pe', 'transpose', ['hp', 'identf'], [('ps', 2 + c // 4)], out=pvx[:, c, :], in_=hp_[:, c * 128:(c + 1) * 128], identity=identf)
        O('act', 'copy', [('ps', 0), ('ps', 1)], ['hcT'], out=hcT, in_=pvh)
        for sidx in range(6):
            for c in range(8):
                O('dve' if (sidx * 8 + c) % 3 else 'pool' if False else 'dve', 'scalar_tensor_tensor', [('ps', 2), ('ps', 3), 'hcT', 'muT'], [('xsT', sidx)],
                  out=xsT[sidx][:, c, :], in0=pvx[:, c, :], scalar=muT[:, sidx, c:c + 1], in1=hcT[:, c, :], op0=ALU.mult, op1=ALU.add)
        xr, xw, xk, xv, xa, xg = xsT
        xrk, xwk, xkk, xvk, xak, xgk = [('xsT', j) for j in range(6)]
        for j, (xs_, xsk, dstf, dk) in enumerate(((xr, xrk, rf, 'rf'), (xk, xkk, kf, 'kf'), (xv, xvk, vf, 'vf'))):
            for n in range(2):
                b = 4 + n
                for k in range(8):
                    O('pe', 'matmul', [xsk, 'wrkv'], [('ps', b)], out=bank(b), lhsT=xs_[:, k, :], rhs=wrkv[:, j, k, n * 512:(n + 1) * 512],
                      start=(k == 0), stop=(k == 7))
            O('act', 'copy', [('ps', 4), ('ps', 5)], [dk], out=dstf, in_=psum[:, 2048:3072])
        ro, rok = out16()
        O('pool', 'tensor_copy', ['rf'], [rok], out=ro, in_=rf)
        DMA('pool', r_s[rows, :], ro, [rok], [])
        vo, vok = out16()
        O('pool', 'tensor_copy', ['vf'], [vok], out=vo, in_=vf)
        DMA('pool', v_s[rows, :], vo, [vok], [])
        for k in range(8):
            O('pe', 'matmul', [xwk, 'w1c'], [('ps', 6)], out=bank(6, 128), lhsT=w1c[:, k, :], rhs=xw[:, k, :], start=(k == 0), stop=(k == 7))
        O('act', 'activation', [('ps', 6)], ['thT'], out=thT, in_=bank(6, 128), func=AF.Tanh)
        for k in range(8):
            O('pe', 'matmul', [xak, 'a1c'], [('ps', 6)], out=bank(6, 128, 128), lhsT=a1c[:, k, :], rhs=xa[:, k, :], start=(k == 0), stop=(k == 7))
        O('act', 'copy', [('ps', 6)], ['alT'], out=alT, in_=bank(6, 128, 128))
        for z in range(2):
            for n in range(2):
                O('pe', 'matmul', ['thT', 'w2c'], [('ps', 4 + n)], out=bank(4 + n), lhsT=thT[z * 64:(z + 1) * 64, :], rhs=w2c[z * 64:(z + 1) * 64, n * 512:(n + 1) * 512],
                  start=True, stop=True)
            O('dve', 'tensor_tensor', [('ps', 4), ('ps', 5), ('vec', z)], ['t1'], out=t1, in0=psum[:, 2048:3072], in1=w0b[z], op=ALU.add)
            O('act', 'activation', ['t1'], ['t1'], out=t1, in_=t1, func=AF.Sigmoid)
            lo, lok = outf32()
            O('pool', 'tensor_scalar', ['t1'], [lok], out=lo, in0=t1, scalar1=NEH, scalar2=None, op0=ALU.mult)
            DMA('pool', lw_s[z, rows, :], lo, [lok], [])
        for z in range(2):
            for n in range(2):
                O('pe', 'matmul', ['alT', 'a2c'], [('ps', 4 + n)], out=bank(4 + n), lhsT=alT[z * 64:(z + 1) * 64, :], rhs=a2c[z * 64:(z + 1) * 64, n * 512:(n + 1) * 512],
                  start=True, stop=True)
            O('dve', 'tensor_tensor', [('ps', 4), ('ps', 5), ('vec', 2 + z)], [('asig', z)], out=asig[z], in0=psum[:, 2048:3072], in1=a0b[z], op=ALU.add)
            O('act', 'activation', [('asig', z)], [('asig', z)], out=asig[z], in_=asig[z], func=AF.Sigmoid)
        if t >= 2:
            for k in range(8):
                O('pe', 'matmul', [xgk, 'g1'], [('ps', 7)], out=bank(7, 128), lhsT=g1[:, k, 0:128], rhs=xg[:, k, :], start=(k == 0), stop=(k == 7))
            for k in range(8):
                O('pe', 'matmul', [xgk, 'g1'], [('ps', 7)], out=bank(7, 128, 128)[0:32, :], lhsT=g1[:, k, 128:160], rhs=xg[:, k, :], start=(k == 0), stop=(k == 7))
            O('act', 'activation', [('ps', 7)], ['sgT'], out=sgT[:, 0, :], in_=bank(7, 128), func=AF.Sigmoid)
            O('act', 'activation', [('ps', 7)], ['sgT'], out=sgT[0:32, 1, :], in_=bank(7, 128, 128)[0:32, :], func=AF.Sigmoid)
            for n in range(2):
                O('pe', 'matmul', ['sgT', 'g2'], [('ps', 4 + n)], out=bank(4 + n), lhsT=sgT[:, 0, :], rhs=g2[:, 0, n * 512:(n + 1) * 512], start=True, stop=False)
                O('pe', 'matmul', ['sgT', 'g2'], [('ps', 4 + n)], out=bank(4 + n), lhsT=sgT[0:32, 1, :], rhs=g2[0:32, 1, n * 512:(n + 1) * 512], start=False, stop=True)
            go_, gok = out16()
            O('act', 'copy', [('ps', 4), ('ps', 5)], [gok], out=go_, in_=psum[:, 2048:3072])
            DMA('pool', g_s[rows, :], go_, [gok], [])
        h3 = lambda ap: ap.rearrange("p (h d) -> p h d", d=64)
        O('dve', 'tensor_tensor', ['kf', ('vec', 4)], ['kkt'], out=kkt, in0=kf, in1=kkv, op=ALU.mult)
        O('dve', 'tensor_tensor', ['kkt'], ['t1'], out=t1, in0=kkt, in1=kkt, op=ALU.mult)
        O('dve', 'tensor_reduce', ['t1'], ['hs16'], out=hs16, in_=h3(t1), axis=AX.X, op=ALU.add)
        O('dve', 'tensor_scalar', ['hs16'], ['hs16'], out=hs16, in0=hs16, scalar1=1e-24, scalar2=None, op0=ALU.max)
        O('act', 'activation', ['hs16'], ['hs16'], out=hs16, in_=hs16, func=AF.Ln)
        O('act', 'activation', ['hs16'], ['hs16'], out=hs16, in_=hs16, func=AF.Exp, scale=-0.5)
        O('dve', 'tensor_tensor', ['kkt', 'hs16'], ['kkt'], out=h3(kkt), in0=h3(kkt), in1=hs16.unsqueeze(2).broadcast_to([128, 16, 64]), op=ALU.mult)
        ao, aok = out16()
        O('pool', 'tensor_scalar', ['kkt'], [aok], out=ao, in0=kkt, scalar1=-1.0, scalar2=None, op0=ALU.mult)
        DMA('pool', a_s[rows, :], ao, [aok], [])
        for z in range(2):
            bo, bok = out16()
            O('dve', 'tensor_tensor', ['kkt', ('asig', z)], [bok], out=bo, in0=kkt, in1=asig[z], op=ALU.mult)
            DMA('pool', bb_s[z, rows, :], bo, [bok], [])
        for z in range(2):
            O('dve', 'tensor_tensor', [('asig', z), ('vec', 5)], [('asig', z)], out=asig[z], in0=asig[z], in1=kav, op=ALU.mult)
            O('dve', 'tensor_tensor', [('asig', z), 'omka'], [('asig', z)], out=asig[z], in0=asig[z], in1=omka, op=ALU.add)
            O('dve', 'tensor_tensor', [('asig', z), 'kf'], [('asig', z)], out=asig[z], in0=asig[z], in1=kf, op=ALU.mult)
            ko_, kok = out16()
            O('pool', 'tensor_copy', [('asig', z)], [kok], out=ko_, in_=asig[z])
            DMA('pool', kd_s[z, rows, :], ko_, [kok], [])
        if t >= 2:
            O('dve', 'tensor_tensor', [('asig', 0), ('asig', 1)], ['t2'], out=t2, in0=asig[0], in1=asig[1], op=ALU.add)
            O('dve', 'tensor_tensor', ['rf', ('vec', 6)], ['t1'], out=t1, in0=rf, in1=rkv, op=ALU.mult)
            O('dve', 'tensor_tensor', ['t1', 't2'], ['t1'], out=t1, in0=t1, in1=t2, op=ALU.mult)
            O('dve', 'tensor_reduce', ['t1'], ['hs16'], out=hs16, in_=h3(t1), axis=AX.X, op=ALU.add)
            bo_, bok_ = outf32()
            O('dve', 'tensor_tensor', ['vf', 'hs16'], [bok_], out=h3(bo_), in0=h3(vf), in1=hs16.unsqueeze(2).broadcast_to([128, 16, 64]), op=ALU.mult)
            DMA('pool', bon_s[rows, :], bo_, [bok_], [])
    P.barrier()
    A.reset()
    if stop_after == 'F':
        P.finish()
        return nc

    rm = A.alloc([4, 128], F32)
    DMA('sp', rm, rwmask.rearrange("a p f -> p a f"), [], ['rm'])
    onesf = A.alloc([128], F32)
    O('dve', 'memset', [], ['onesf'], ap=onesf, constant=1.0)
    pm = A.alloc([2], F32)
    DMA('sp', pm, pmask_in, [], ['pm'])
    tP = {n_: [A.alloc([8, 128], BF16) for _ in range(2)] for n_ in ('at', 'rt', 'bt')}
    lw = A.alloc([D], F32)
    ain = A.alloc([D], BF16)
    rin = A.alloc([D], BF16)
    bin_ = A.alloc([D], BF16)
    kin = A.alloc([D], BF16)
    vin = A.alloc([D], BF16)
    cumS = A.alloc([D], F32)
    E1 = A.alloc([D], F32)
    E2 = A.alloc([D], F32)
    E3 = A.alloc([D], F32)
    Ee = A.alloc([D], F32)
    gL = A.alloc([8], F32)
    sc6 = {n_: A.alloc([D], BF16) for n_ in ('at', 'rt', 'bt', 'kt', 'bh', 'kh')}
    tT = {n_: A.alloc([8, 128], BF16) for n_ in ('at', 'rt', 'bt', 'kt')}
    Db = [[A.alloc([4, 128], BF16) for _ in range(2)] for _ in range(4)]
    DTb = [[A.alloc([4, 128], BF16) for _ in range(2)] for _ in range(4)]
    NT0 = [A.alloc([4, 128], BF16) for _ in range(4)]
    NoffT = [A.alloc([4, 128], BF16) for _ in range(4)]
    Z1sb = [A.alloc([4, 128], BF16) for _ in range(4)]
    Wsb = [A.alloc([4, 128], BF16) for _ in range(4)]
    Iden4 = A.alloc([4, 128], BF16)
    O('dve', 'tensor_copy', ['identb'], ['Iden4'], out=Iden4, in_=identb.unsqueeze(1).broadcast_to([128, 4, 128]))
    lm = A.alloc([7, 128], BF16)
    lmst = A.alloc([7, 128], F32)
    DMA('sp', lmst, lvlmask.rearrange("a p f -> p a f"), [], ['lmst'])
    O('dve', 'tensor_copy', ['lmst'], ['lm'], out=lm, in_=lmst)
    lm4 = A.alloc([7, 4, 128], BF16)
    for lv_ in range(7):
        O('dve', 'tensor_copy', ['lm'], ['lm4'], out=lm4[:, lv_], in_=lm[:, lv_, :].unsqueeze(1).broadcast_to([128, 4, 128]))
    X = A.alloc([16, 128], BF16)
    AakT = A.alloc([16, 128], BF16)
    ArbT = A.alloc([16, 128], BF16)
    ArkT = A.alloc([16, 128], BF16)
    Wb = A.alloc([16, 64], BF16)
    Ub = A.alloc([16, 64], BF16)
    yt = A.alloc([D], F32)
    Hst = [A.alloc([8, 64], F32) for _ in range(2)]
    Hbf = [A.alloc([8, 64], BF16) for _ in range(2)]
    tmpH = A.alloc([8, 64], F32)
    for z in range(2):
        O('dve', 'memset', [], [('Hst', z)], ap=Hst[z], constant=0.0)
        O('dve', 'memset', [], [('Hbf', z)], ap=Hbf[z], constant=0.0)
    bsel = [0]

    def nb_():
        bsel[0] = (bsel[0] + 1) % 4
        return 4 + bsel[0]

    def hv(b):
        return bank(b).rearrange("p (h t) -> p h t", t=128)

    bsel8 = [0]

    def nb8_():
        bsel8[0] = (bsel8[0] + 1) % 8
        return bsel8[0]

    for step in range(GSTEPS):
        for z in range(2):
            t = orders[z][step]
            rows = slice(t * 128, (t + 1) * 128)
            tri = rm[:, z, :]
            strict = rm[:, 2 + z, :]
            strictT = rm[:, 3 - z, :]
            DMA('sp', lw, lw_s[z, rows, :], [], ['lw'])
            DMA('sp', ain, a_s[rows, :], [], ['ain'])
            DMA('sp', rin, r_s[rows, :], [], ['rin'])
            DMA('sp', bin_, bb_s[z, rows, :], [], ['bin'])
            DMA('sp', kin, kd_s[z, rows, :], [], ['kin'])
            DMA('sp', vin, v_s[rows, :], [], ['vin'])
            for n in range(2):
                O('pe', 'matmul', ['lw', 'rm'], [('ps', n)], out=bank(n), lhsT=tri, rhs=lw[:, n * 512:(n + 1) * 512], start=True, stop=True)
                O('pe', 'matmul', ['lw', 'onesf'], [('ps', 2 + n)], out=bank(2 + n), lhsT=onesf, rhs=lw[:, n * 512:(n + 1) * 512], start=True, stop=True)
            for c8 in range(8):
                O('pe', 'matmul', ['lw', 'onesf'], [('ps', 4)], out=bank(4, 4, 4 * c8), lhsT=lw[:, c8 * 128:(c8 + 1) * 128], rhs=onesf[:, 0:4], start=True, stop=True)
            O('act', 'copy', [('ps', 0), ('ps', 1)], ['cumS'], out=cumS, in_=psum[:, 0:1024])
            O('act', 'activation', [('ps', 0), ('ps', 1)], ['E1'], out=E1, in_=psum[:, 0:1024], func=AF.Exp)
            O('dve', 'tensor_tensor', [('ps', 2), ('ps', 3), 'cumS'], ['E3'], out=E3, in0=psum[:, 1024:2048], in1=cumS, op=ALU.subtract)
            O('act', 'activation', ['E3'], ['E3'], out=E3, in_=E3, func=AF.Exp)
            O('dve', 'tensor_tensor', ['cumS', 'lw'], ['Ee'], out=Ee, in0=cumS, in1=lw, op=ALU.subtract)
            O('act', 'activation', ['Ee'], ['Ee'], out=Ee, in_=Ee, func=AF.Exp)
            O('act', 'activation', ['cumS'], ['E2'], out=E2, in_=cumS, func=AF.Exp, scale=-1.0)
            O('act', 'activation', [('ps', 4)], ['gL'], out=gL, in_=bank(4, 32).rearrange("p (c x) -> p c x", x=4)[:, :, 0], func=AF.Exp)
            if GLEVEL < 2:
                continue
            O('dve', 'tensor_tensor', ['ain', 'Ee'], ['at'], out=sc6['at'], in0=ain, in1=Ee, op=ALU.mult)
            O('dve', 'tensor_tensor', ['rin', 'E1'], ['rt'], out=sc6['rt'], in0=rin, in1=E1, op=ALU.mult)
            O('dve', 'tensor_tensor', ['bin', 'E2'], ['bt'], out=sc6['bt'], in0=bin_, in1=E2, op=ALU.mult)
            O('dve', 'tensor_tensor', ['kin', 'E2'], ['kt'], out=sc6['kt'], in0=kin, in1=E2, op=ALU.mult)
            O('dve', 'tensor_tensor', ['bin', 'E3'], ['bh'], out=sc6['bh'], in0=bin_, in1=E3, op=ALU.mult)
            O('dve', 'tensor_tensor', ['kin', 'E3'], ['kh'], out=sc6['kh'], in0=kin, in1=E3, op=ALU.mult)
            for bi, n_ in enumerate(('at', 'rt', 'bt', 'kt')):
                transpose8(sc6[n_], n_, tT[n_], n_ + 'T', bi, ('ps', bi), evac=('act' if bi % 2 == 0 else 'dve'))
            atT, rtT, btT, ktT = tT['at'], tT['rt'], tT['bt'], tT['kt']
            for n_ in ('at', 'rt', 'bt'):
                for par in range(2):
                    O('dve' if par == 0 else 'pool', 'tensor_scalar', [n_ + 'T', 'pm'], [(n_ + 'P', par)], out=tP[n_][par], in0=tT[n_],
                      scalar1=pm[:, par:par + 1], scalar2=None, op0=ALU.mult)

            def hsl(T_, h):
                return T_[(h % 2) * 64:(h % 2) * 64 + 64, h // 2, :]

            if GLEVEL < 3:
                continue
            for hg in range(4):
                for (dst_, dk, L_, Lk, R_, Rk, msk) in ((AakT, 'AakT', ktT, 'ktT', 'at', 'atP', strict), (ArbT, 'ArbT', btT, 'btT', 'rt', 'rtP', tri),
                                                      (ArkT, 'ArkT', ktT, 'ktT', 'rt', 'rtP', tri)):
                    b = nb_()
                    for hh in range(4):
                        h = hg * 4 + hh
                        if GTEST == 1 and h % 2 == 1:
                            continue
                        if GTEST == 2 and h % 2 == 0:
                            continue
                        O('pe', 'matmul', [Lk, (Rk, h % 2)], [('ps', b)], out=bank(b, 128, hh * 128), lhsT=L_[:, h // 2, :], rhs=tP[R_][h % 2][:, h // 2, :],
                          start=True, stop=True)
                    if GLEVEL >= 3.1:
                        O('dve', 'tensor_tensor', [('ps', b), 'rm'], [(dk, hg)], out=dst_[:, hg * 4:hg * 4 + 4, :], in0=hv(b),
                          in1=msk.unsqueeze(1).broadcast_to([128, 4, 128]), op=ALU.mult)
                b = nb_()
                for hh in range(4):
                    h = hg * 4 + hh
                    O('pe', 'matmul', [('btP', h % 2), 'atT'], [('ps', b)], out=bank(b, 128, hh * 128), lhsT=atT[:, h // 2, :], rhs=tP['bt'][h % 2][:, h // 2, :],
                      start=True, stop=True)
                O('dve', 'tensor_tensor', [('ps', b), 'rm'], [('NT0', hg)], out=NT0[hg], in0=hv(b), in1=strictT.unsqueeze(1).broadcast_to([128, 4, 128]), op=ALU.mult)
            for lv in range(7):
                cur, nxt = lv % 2, (lv + 1) % 2
                for hg in range(4):
                    Dc, Dck = (Iden4, 'Iden4') if lv == 0 else (Db[hg][cur], ('D', hg, cur))
                    DTc, DTck = (Iden4, 'Iden4') if lv == 0 else (DTb[hg][cur], ('DT', hg, cur))
                    O('pool', 'tensor_tensor', [('NT0', hg), 'lm4'], [('NoffT', hg)], out=NoffT[hg], in0=NT0[hg], in1=lm4[:, lv], op=ALU.mult)
                    b1 = nb8_()
                    for hh in range(4):
                        O('pe', 'matmul', [('NoffT', hg), Dck], [('ps', b1)], out=bank(b1, 128, hh * 128), lhsT=NoffT[hg][:, hh, :], rhs=Dc[:, hh, :],
                          start=True, stop=True)
                    O('act', 'copy', [('ps', b1)], [('Z1sb', hg)], out=Z1sb[hg], in_=hv(b1))
                    b2 = nb8_()
                    for hh in range(4):
                        O('pe', 'matmul', [('Z1sb', hg), DTck], [('ps', b2)], out=bank(b2, 128, hh * 128), lhsT=DTc[:, hh, :], rhs=Z1sb[hg][:, hh, :],
                          start=True, stop=True)
                    if lv < 6:
                        O('act', 'copy', [('ps', b2)], [('Wsb', hg)], out=Wsb[hg], in_=hv(b2))
                        O('dve', 'tensor_tensor', [('ps', b2), Dck], [('D', hg, nxt)], out=Db[hg][nxt], in0=hv(b2), in1=Dc, op=ALU.add)
                        b3 = nb8_()
                        pv3 = bank_bf(b3).rearrange("p (h t) -> p h t", t=128)[:, 0:4, :]
                        for hh in range(4):
                            O('pe', 'transpose', [('Wsb', hg), 'identb'], [('ps', b3)], out=pv3[:, hh, :], in_=Wsb[hg][:, hh, :], identity=identb)
                        O('dve', 'tensor_tensor', [('ps', b3), DTck], [('DT', hg, nxt)], out=DTb[hg][nxt], in0=pv3, in1=DTc, op=ALU.add)
                    else:
                        O('dve', 'tensor_tensor', [('ps', b2), Dck], [('X', hg)], out=X[:, hg * 4:hg * 4 + 4, :], in0=hv(b2), in1=Dc, op=ALU.add)
            if GLEVEL < 4:
                continue
            Hk = ('Hbf', z)
            Xk = [('X', hg) for hg in range(4)]
            for h in range(16):
                po, c8 = (h % 2) * 64, h // 2
                outp = bank(h // 8, 64, (h % 8) * 64)
                O('pe', 'matmul', [('AakT', h // 4), 'vin'], [('ps', h // 8)], out=outp, lhsT=AakT[:, h, :], rhs=vin[:, h * 64:(h + 1) * 64], start=True, stop=False)
                O('pe', 'matmul', [('atP', h % 2), Hk], [('ps', h // 8)], out=outp, lhsT=tP['at'][h % 2][:, c8, :], rhs=Hbf[z][:, c8, :], start=False, stop=True)
            O('act', 'copy', [('ps', 0), ('ps', 1)], ['Wb'], out=Wb.rearrange("p h d -> p (h d)"), in_=psum[:, 0:1024])
            for h in range(16):
                O('pe', 'matmul', Xk + ['Wb'], [('ps', 2 + h // 8)], out=bank(2 + h // 8, 64, (h % 8) * 64), lhsT=X[:, h, :], rhs=Wb[:, h, :], start=True, stop=True)
            O('dve', 'tensor_copy', [('ps', 2), ('ps', 3)], ['Ub'], out=Ub.rearrange("p h d -> p (h d)"), in_=psum[:, 1024:2048])
            if t >= 2:
                for h in range(16):
                    po, c8 = (h % 2) * 64, h // 2
                    outp = bank(4 + h // 8, 64, (h % 8) * 64)
                    O('pe', 'matmul', [('rtP', h % 2), Hk], [('ps', 4 + h // 8)], out=outp, lhsT=tP['rt'][h % 2][:, c8, :], rhs=Hbf[z][:, c8, :], start=True, stop=False)
                    O('pe', 'matmul', [('ArbT', h // 4), 'Ub'], [('ps', 4 + h // 8)], out=outp, lhsT=ArbT[:, h, :], rhs=Ub[:, h, :], start=False, stop=False)
                    O('pe', 'matmul', [('ArkT', h // 4), 'vin'], [('ps', 4 + h // 8)], out=outp, lhsT=ArkT[:, h, :], rhs=vin[:, h * 64:(h + 1) * 64], start=False, stop=True)
                O('act', 'copy', [('ps', 4), ('ps', 5)], ['yt'], out=yt, in_=psum[:, 2048:3072])
                DMA('pool', y_s[z, rows, :], yt, ['yt'], [])
            for c8 in range(8):
                outp = bank(6 + c8 // 4, 128, (c8 % 4) * 128)
                O('pe', 'matmul', ['bh', 'Ub'], [('ps', 6 + c8 // 4)], out=outp, lhsT=sc6['bh'][:, c8 * 128:(c8 + 1) * 128],
                  rhs=Ub[:, 2 * c8:2 * c8 + 2, :].rearrange("p h d -> p (h d)"), start=True, stop=False)
                O('pe', 'matmul', ['kh', 'vin'], [('ps', 6 + c8 // 4)], out=outp, lhsT=sc6['kh'][:, c8 * 128:(c8 + 1) * 128], rhs=vin[:, c8 * 128:(c8 + 1) * 128],
                  start=False, stop=True)
            pD = psum[:, 3072:4096].rearrange("p (c x) -> p c x", x=128)
            O('dve', 'tensor_tensor', [('Hst', z), 'gL'], ['tmpH'], out=tmpH, in0=Hst[z], in1=gL.unsqueeze(2).broadcast_to([128, 8, 64]), op=ALU.mult)
            O('dve', 'scalar_tensor_tensor', ['tmpH', ('ps', 6), ('ps', 7), 'pm'], ['tmpH'], out=tmpH, in0=pD[:, :, 0:64], scalar=pm[:, 0:1], in1=tmpH,
              op0=ALU.mult, op1=ALU.add)
            O('dve', 'scalar_tensor_tensor', ['tmpH', ('ps', 6), ('ps', 7), 'pm'], [('Hst', z)], out=Hst[z], in0=pD[:, :, 64:128], scalar=pm[:, 1:2], in1=tmpH,
              op0=ALU.mult, op1=ALU.add)
            O('act', 'copy', [('Hst', z)], [Hk], out=Hbf[z], in_=Hst[z])
            if 'dumpG' in debug and z == 0 and step < 3:
                for (nm, ap_, keys, dt_) in (('X', X, Xk, BF16), ('AakT', AakT, [('AakT', q) for q in range(4)], BF16),
                                             ('ArbT', ArbT, [('ArbT', q) for q in range(4)], BF16), ('Wb', Wb, ['Wb'], BF16), ('Ub', Ub, ['Ub'], BF16),
                                             ('Hst', Hst[0], [('Hst', 0)], F32), ('at', sc6['at'], ['at'], BF16), ('atT', atT, ['atT'], BF16),
                                             ('E1', E1, ['E1'], F32), ('gL', gL, ['gL'], F32), ('bh', sc6['bh'], ['bh'], BF16)):
                    shp = list(ap_.shape)
                    dd = nc.dram_tensor("dbg_%s_%d" % (nm, step), shp, dt_, kind="ExternalOutput").ap()
                    DMA('sp', dd, ap_, keys, [])
    P.barrier()
    A.reset()
    if stop_after == 'G':
        P.finish()
        return nc

    lnw = A.alloc([D], F32)
    lnb = A.alloc([D], F32)
    DMA('pool', lnw, rw_vecs[7:8, :].partition_broadcast(128), [], ['lnw'])
    DMA('pool', lnb, rw_vecs[8:9, :].partition_broadcast(128), [], ['lnb'])
    eps2 = A.alloc([1], F32)
    O('dve', 'memset', [], ['eps2'], ap=eps2, constant=64e-5)
    yf = [A.alloc([D], F32) for _ in range(2)]
    yb = [A.alloc([D], F32) for _ in range(2)]
    bon = [A.alloc([D], F32) for _ in range(2)]
    gin = [A.alloc([D], BF16) for _ in range(2)]
    yo = [A.alloc([D], BF16) for _ in range(2)]
    sq = A.alloc([D], F32)
    st = A.alloc([16], F32)
    h3 = lambda ap: ap.rearrange("p (h d) -> p h d", d=64)
    for t in range(2, NT):
        p = t % 2
        rows = slice(t * 128, (t + 1) * 128)
        DMA('sp', yf[p], y_s[0, rows, :], [], [('yf', p)])
        DMA('sp', yb[p], y_s[1, rows, :], [], [('yb', p)])
        DMA('sp', bon[p], bon_s[rows, :], [], [('bon', p)])
        DMA('sp', gin[p], g_s[rows, :], [], [('gin', p)])
        y = yf[p]
        yk = ('yf', p)
        O('dve', 'tensor_tensor', [yk, ('yb', p)], [yk], out=y, in0=y, in1=yb[p], op=ALU.add)
        O('dve', 'tensor_reduce', [yk], ['st'], out=st, in_=h3(y), axis=AX.X, op=ALU.add)
        O('dve', 'tensor_scalar', ['st'], ['st'], out=st, in0=st, scalar1=-1.0 / 64, scalar2=None, op0=ALU.mult)
        O('dve', 'tensor_tensor', [yk, 'st'], [yk], out=h3(y), in0=h3(y), in1=st.unsqueeze(2).broadcast_to([128, 16, 64]), op=ALU.add)
        O('dve', 'tensor_tensor', [yk], ['sq'], out=sq, in0=y, in1=y, op=ALU.mult)
        O('dve', 'tensor_reduce', ['sq'], ['st'], out=st, in_=h3(sq), axis=AX.X, op=ALU.add)
        O('act', 'activation', ['st', 'eps2'], ['st'], out=st, in_=st, func=AF.Ln, scale=1.0 / 64, bias=eps2)
        O('act', 'activation', ['st'], ['st'], out=st, in_=st, func=AF.Exp, scale=-0.5)
        O('dve', 'tensor_tensor', [yk, 'st'], [yk], out=h3(y), in0=h3(y), in1=st.unsqueeze(2).broadcast_to([128, 16, 64]), op=ALU.mult)
        O('dve', 'tensor_tensor', [yk, 'lnw'], [yk], out=y, in0=y, in1=lnw, op=ALU.mult)
        O('dve', 'tensor_tensor', [('bon', p), 'lnb'], [('bon', p)], out=bon[p], in0=bon[p], in1=lnb, op=ALU.add)
        O('dve', 'tensor_tensor', [yk, ('bon', p)], [yk], out=y, in0=y, in1=bon[p], op=ALU.add)
        O('dve', 'tensor_tensor', [yk, ('gin', p)], [('yo', p)], out=yo[p], in0=y, in1=gin[p], op=ALU.mult)
        DMA('pool', yc_s[rows, :], yo[p], [('yo', p)], [])
    P.barrier()
    A.reset()

    outproj_mlp(1, rw_w_o, [(yc_s, D)], list(range(2, NT)), lambda t: xs1[t * 128:(t + 1) * 128, :],
                lambda t: out_d[(t - 2) * 128:(t - 1) * 128, :])

    P.finish()
    return nc


def host_inputs(inputs, b):
    f = np.float32
    m = {}
    m["x"] = np.ascontiguousarray(inputs["x"][b])
    m["ctx"] = np.ascontiguousarray(inputs["ctx"][b])
    c2 = np.stack([inputs["c"][b], inputs["c_ctx"]], 0)
    m["c2T"] = np.ascontiguousarray(c2.reshape(2, 8, 128).transpose(2, 1, 0))
    m["ada_w"] = inputs["ada_w"]
    m["ada_b"] = inputs["ada_b"]
    m["ab_w_in"] = np.ascontiguousarray(inputs["ab_w_in"][0])
    m["ab_gate_b"] = np.ascontiguousarray(inputs["ab_gate_b"])
    m["qk_norm"] = np.stack([np.tile(inputs["na_q_norm"][0], 8), np.tile(inputs["na_k_norm"][0], 8)], 0).astype(f)
    m["ident"] = np.eye(128, dtype=f)
    pos = np.arange(T_LAT)
    rows = (pos // 64).astype(f)
    cols = (pos % 64).astype(f)
    inv = (10000.0 ** (-np.arange(32, dtype=f) / 32)).astype(f)
    ar = (rows[:, None] * inv).astype(f)
    ac = (cols[:, None] * inv).astype(f)
    C = np.concatenate([np.cos(ar), np.cos(ar), np.cos(ac), np.cos(ac)], 1).astype(f)
    S = np.concatenate([-np.sin(ar), np.sin(ar), -np.sin(ac), np.sin(ac)], 1).astype(f)
    m["ropeC"] = np.ascontiguousarray(np.tile(C, (1, 4)))
    m["ropeS"] = np.ascontiguousarray(np.tile(S, (1, 4)))
    m["na_bias"] = na_bias_table(inputs["na_rpb"][0])
    k = np.arange(128)
    triF = (k[:, None] <= k[None, :]).astype(f)
    triB = (k[:, None] >= k[None, :]).astype(f)
    m["trimask"] = np.stack([triF, triB, (1 - triF) * NEG, (1 - triB) * NEG], 0).astype(f)
    sF = (k[:, None] < k[None, :]).astype(f)
    m["rwmask"] = np.stack([triF, triB, sF, sF.T], 0).astype(f)
    lv = []
    for l in range(7):
        B = 1 << l
        lv.append(((k[:, None] // (2 * B)) == (k[None, :] // (2 * B))) & ((k[:, None] // B) != (k[None, :] // B)))
    m["lvlmask"] = np.stack(lv, 0).astype(f)
    m["pmask"] = np.stack([(k < 64), (k >= 64)], 1).astype(f)
    m["head_norm"] = np.ascontiguousarray(inputs["ml_head_norm"][0].reshape(1, 512))
    m["ab_w_out"] = np.ascontiguousarray(inputs["ab_w_out"][0])
    m["mlp_w1"] = inputs["mlp_w1"]
    m["mlp_w2"] = inputs["mlp_w2"]
    m["rw_muT"] = np.ascontiguousarray(inputs["rw_mu"][0].reshape(6, 8, 128).transpose(2, 0, 1))
    m["rw_w_rkv"] = np.ascontiguousarray(inputs["rw_w_rkv"][0])
    m["rw_w1c"] = np.ascontiguousarray(np.concatenate([inputs["rw_w1"][0, 0], inputs["rw_w1"][0, 1]], 1))
    m["rw_w2c"] = np.ascontiguousarray(np.concatenate([inputs["rw_w2"][0, 0], inputs["rw_w2"][0, 1]], 0))
    m["rw_a1c"] = np.ascontiguousarray(np.concatenate([inputs["rw_a1"][0, 0], inputs["rw_a1"][0, 1]], 1))
    m["rw_a2c"] = np.ascontiguousarray(np.concatenate([inputs["rw_a2"][0, 0], inputs["rw_a2"][0, 1]], 0))
    m["rw_g1"] = np.ascontiguousarray(inputs["rw_g1"][0])
    m["rw_g2"] = np.ascontiguousarray(inputs["rw_g2"][0])
    m["rw_vecs"] = np.stack([inputs["rw_w0"][0, 0], inputs["rw_w0"][0, 1], inputs["rw_a0"][0, 0], inputs["rw_a0"][0, 1],
                             inputs["rw_k_k"][0], inputs["rw_k_a"][0], inputs["rw_r_k"][0], inputs["rw_lnx_w"][0],
                             inputs["rw_lnx_b"][0]], 0).astype(f)
    m["rw_w_o"] = np.ascontiguousarray(inputs["rw_w_o"][0])
    return m


def na_bias_table(rpb):
    tab = np.full((8, 21, 128, 128), NEG, np.float32)
    col = np.arange(64)
    cw = np.clip(col - 8, 0, 48)
    valid = (col[:, None] >= cw[None, :]) & (col[:, None] < cw[None, :] + 16)
    cidx = np.clip(col[:, None] - col[None, :] + 15, 0, 30)

    def fill(idx, qt, kc):
        for qr in range(2):
            q_row = 2 * qt + qr
            rs = min(max(q_row - 4, 0), 56)
            for kr in range(2):
                k_row = 2 * kc + kr
                if not (rs <= k_row < rs + 8):
                    continue
                vals = rpb[:, k_row - q_row + 7][:, cidx]
                tab[:, idx, kr * 64:(kr + 1) * 64, qr * 64:(qr + 1) * 64] = np.where(valid[None], vals, NEG)

    for rel in range(-2, 3):
        fill(rel + 2, 10, 10 + rel)
    for sidx, lq in enumerate((0, 1, 30, 31)):
        base = 0 if lq < 2 else 28
        for j in range(4):
            fill(5 + 4 * sidx + j, lq, base + j)
    return np.ascontiguousarray(tab.transpose(0, 2, 1, 3).reshape(8, 128, 21 * 128))


_CACHE = {}


def kernel(**inputs):
    inputs = {k: np.asarray(v) for k, v in inputs.items()}
    if "nc" not in _CACHE:
        _CACHE["nc"] = build_program()
    nc = _CACHE["nc"]
    in_maps = [host_inputs(inputs, b) for b in range(8)]
    res = run_bass_kernel_spmd(nc, in_maps, core_ids=list(range(8)))
    return np.stack([r["out"] for r in res.results], 0).astype(np.float32)
```
